# Optimizing a Trainium2 kernel written in Bass

```python
import math
import jax
import jax.numpy as jnp
from jax import lax
import numpy as np

D_MODEL = 1024
BATCH = 8
SEQ = 4096
DEPTH = 4

Q_BLOCK = 128
LN_EPS = 1e-5
RMS_EPS = 1e-5
N_BUCKETS = 32
MAX_DISTANCE = 2048
N_BIAS_COLS = 16
A_HEADS = 4
A_QK = 64
A_V = 128
B_PAIRS = ((128, 1), (512, 4), (2048, 16))
B_HEADS = 4
B_DIM = 64
C_HEADS = 8
C_KV_GROUPS = 2
C_DIM = 64
CMP_LEN = 32
CMP_STRIDE = 16
CMP_HIDDEN = 256
SLC_BLOCK = 64
SLC_TOP_N = 16
SLC_Q_CHUNK = 32
WIN_SIZE = 512
D_HEADS = 8
D_Q_LORA = 384
D_KV_LORA = 256
D_NOPE = 64
D_ROPE = 32
D_V = 64
ROPE_THETA = 10000.0
D_FF = 2816
CONV_W = 3

EVEN_SPLITS = (2 * A_HEADS * A_QK, 2 * A_HEADS * A_QK, A_HEADS * A_V, len(B_PAIRS) * 3 * B_HEADS * B_DIM)
EVEN_IN = sum(EVEN_SPLITS)
EVEN_OUT = A_HEADS * A_V + B_HEADS * B_DIM
ODD_SPLITS = (C_HEADS * C_DIM, 3 * 2 * C_KV_GROUPS * C_DIM, 3 * C_HEADS, D_Q_LORA, D_KV_LORA, D_ROPE)
ODD_IN = sum(ODD_SPLITS)
ODD_OUT = C_HEADS * C_DIM + D_HEADS * D_V

kernel_name = 'hybrid_diff_dilated_nsa_mla_convffn'


def split_cols(h, sizes):
    outs, off = [], 0
    for s in sizes:
        outs.append(h[..., off:off + s])
        off += s
    return outs


def layer_norm(x, g, b):
    xf = x.astype(jnp.float32)
    mu = jnp.mean(xf, -1, keepdims=True)
    var = jnp.mean(jnp.square(xf - mu), -1, keepdims=True)
    return ((xf - mu) * lax.rsqrt(var + LN_EPS) * g + b).astype(x.dtype)


def rms_norm(x, g):
    xf = x.astype(jnp.float32)
    return (xf * lax.rsqrt(jnp.mean(xf * xf, -1, keepdims=True) + RMS_EPS) * g).astype(x.dtype)


def t5_bucket(dist):
    max_exact = N_BUCKETS // 2
    n = jnp.maximum(dist, 0)
    nf = jnp.maximum(n, 1).astype(jnp.float32)
    large = max_exact + (jnp.log(nf / max_exact) / math.log(MAX_DISTANCE / max_exact) * (N_BUCKETS - max_exact)).astype(jnp.int32)
    return jnp.where(n < max_exact, n, jnp.minimum(large, N_BUCKETS - 1))


def rope(x, pos):
    half = x.shape[-1] // 2
    inv = ROPE_THETA ** (-jnp.arange(half, dtype=jnp.float32) / half)
    ang = pos.astype(jnp.float32)[:, None] * inv
    ang = ang.reshape(ang.shape[:1] + (1,) * (x.ndim - 3) + (half,))
    cos, sin = jnp.cos(ang), jnp.sin(ang)
    xf = x.astype(jnp.float32)
    x1, x2 = xf[..., :half], xf[..., half:]
    return jnp.concatenate([x1 * cos - x2 * sin, x1 * sin + x2 * cos], -1).astype(x.dtype)


def banded_attention(q, k, v, max_dist, bias_tbl, dist_scale):
    N, L, G, Hg, D = q.shape
    blk = Q_BLOCK
    n_prev = -(-max_dist // blk)
    nb = -(-L // blk)
    pad = nb * blk - L
    qb = jnp.pad(q, ((0, 0), (0, pad), (0, 0), (0, 0), (0, 0))).reshape(N, nb, blk, G, Hg, D)
    kv_pad = ((0, 0), (n_prev * blk, pad), (0, 0), (0, 0))
    kb = jnp.pad(k, kv_pad).reshape(N, nb + n_prev, blk, G, D)
    vb = jnp.pad(v, kv_pad).reshape(N, nb + n_prev, blk, G, v.shape[-1])
    kw = jnp.concatenate([kb[:, j:j + nb] for j in range(n_prev + 1)], axis=2)
    vw = jnp.concatenate([vb[:, j:j + nb] for j in range(n_prev + 1)], axis=2)
    n_keys = (n_prev + 1) * blk
    kj = jnp.arange(n_keys)
    dist = jnp.arange(blk)[:, None] + n_prev * blk - kj[None, :]
    kpos = (jnp.arange(nb) * blk)[:, None, None] - n_prev * blk + kj[None, None, :]
    valid = (dist >= 0) & (dist <= max_dist) & (kpos >= 0)
    bias = bias_tbl[t5_bucket(dist * dist_scale)].transpose(2, 3, 0, 1)
    s = jnp.einsum('nbqghd,nbkgd->nbghqk', qb, kw).astype(jnp.float32) * (D ** -0.5) + bias
    s = jnp.where(valid[:, None, None], s, -jnp.inf)
    m = jnp.max(s, -1, keepdims=True)
    p = jnp.exp(s - m)
    den = jnp.sum(p, -1, keepdims=True)
    o = jnp.einsum('nbghqk,nbkgd->nbqghd', (p / den).astype(v.dtype), vw)
    lse = (m + jnp.log(den))[..., 0].transpose(0, 1, 4, 2, 3)
    o = o.reshape(N, nb * blk, G, Hg, v.shape[-1])[:, :L]
    lse = lse.reshape(N, nb * blk, G, Hg)[:, :L]
    return o, lse


def diff_attention(q, k, v, lam, bias_tbl):
    B, S, _, H, Dk = q.shape
    nb = S // Q_BLOCK
    kpos = jnp.arange(S)
    qb = q.reshape(B, nb, Q_BLOCK, 2, H, Dk).swapaxes(0, 1)

    def step(args):
        qi, start = args
        dist = (start + jnp.arange(Q_BLOCK))[:, None] - kpos[None, :]
        bias = bias_tbl[t5_bucket(dist)].transpose(2, 0, 1)
        s = jnp.einsum('bqmhd,bkmhd->bmhqk', qi, k).astype(jnp.float32) * (Dk ** -0.5) + bias
        s = jnp.where(dist >= 0, s, -jnp.inf)
        p = jax.nn.softmax(s, axis=-1)
        w = p[:, 0] - lam * p[:, 1]
        return jnp.einsum('bhqk,bkhd->bqhd', w.astype(v.dtype), v)

    o = lax.map(step, (qb, jnp.arange(nb) * Q_BLOCK))
    return o.swapaxes(0, 1).reshape(B, S, H, v.shape[-1])


def to_residue(t, d):
    B, S, H, D = t.shape
    return t.reshape(B, S // d, d, H, D).transpose(0, 2, 1, 3, 4).reshape(B * d, S // d, H, D)


def dilated_mixture(hb, bias_table):
    B, S = hb.shape[:2]
    outs, lses = [], []
    for i, (window, d) in enumerate(B_PAIRS):
        M = S // d
        col = A_HEADS + i * B_HEADS
        tbl = bias_table[:, col:col + B_HEADS][:, :, None]
        o, lse = banded_attention(to_residue(hb[:, :, i, 0], d)[:, :, :, None], to_residue(hb[:, :, i, 1], d),
                                  to_residue(hb[:, :, i, 2], d), window // d, tbl, d)
        outs.append(o.reshape(B, d, M, B_HEADS, B_DIM).transpose(0, 2, 1, 3, 4).reshape(B, S, B_HEADS, B_DIM))
        lses.append(lse.reshape(B, d, M, B_HEADS).transpose(0, 2, 1, 3).reshape(B, S, B_HEADS))
    wts = jax.nn.softmax(jnp.stack(lses), axis=0)
    return jnp.einsum('pbsh,pbshd->bshd', wts.astype(outs[0].dtype), jnp.stack(outs))


def nsa_compress(t, pe, w1, w2):
    B, S, G, D = t.shape
    ch = t.reshape(B, S // CMP_STRIDE, CMP_STRIDE, G, D)
    blocks = jnp.concatenate([ch[:, :-1], ch[:, 1:]], axis=2)
    blocks = blocks + pe[:, None, :]
    flat = blocks.transpose(0, 1, 3, 2, 4).reshape(B, blocks.shape[1], G, CMP_LEN * D)
    return jax.nn.gelu(flat @ w1) @ w2


def nsa_compressed(q, k, v, pe, w1, w2, tbl):
    B, S, G, Hg, D = q.shape
    kc = nsa_compress(k, pe[0], w1[0], w2[0])
    vc = nsa_compress(v, pe[1], w1[1], w2[1])
    ncb = kc.shape[1]
    kend = jnp.arange(ncb) * CMP_STRIDE + CMP_LEN - 1
    dist = jnp.arange(S)[:, None] - kend[None, :]
    bias = tbl[t5_bucket(dist)].transpose(2, 3, 0, 1)
    s = jnp.einsum('bqghd,bcgd->bghqc', q, kc).astype(jnp.float32) * (D ** -0.5) + bias
    s = jnp.where(dist >= 0, s, -jnp.inf)
    m = jnp.max(s, -1, keepdims=True)
    p = jnp.exp(s - jnp.where(jnp.isfinite(m), m, 0.0))
    den = jnp.sum(p, -1, keepdims=True)
    p = p / jnp.where(den > 0, den, 1.0)
    o = jnp.einsum('bghqc,bcgd->bqghd', p.astype(v.dtype), vc)
    n_sel = S // SLC_BLOCK
    r = SLC_BLOCK // CMP_STRIDE
    pg = jnp.sum(p, axis=2)
    pp = jnp.pad(pg, ((0, 0), (0, 0), (0, 0), (1, r * (n_sel + 1) - 1 - ncb))).reshape(B, G, S, n_sel + 1, r)
    score = 0.5 * pp[..., :-1, 0] + jnp.sum(pp[..., :-1, 1:], -1) + 0.5 * pp[..., 1:, 0]
    return o, score


def nsa_selected(q, k, v, slc_score, tbl):
    B, S, G, Hg, D = q.shape
    n_sel = S // SLC_BLOCK
    top_n = min(SLC_TOP_N, n_sel)
    qblk = jnp.arange(S) // SLC_BLOCK
    jb = jnp.arange(n_sel)[None, :]
    allowed = jb <= qblk[:, None]
    forced = (jb == 0) | (jb == qblk[:, None]) | (jb == qblk[:, None] - 1)
    score = jnp.where(forced, jnp.inf, jnp.where(allowed, slc_score.astype(jnp.float32), -jnp.inf))
    _, idx = lax.top_k(score, top_n)
    kb = k.reshape(B, n_sel, SLC_BLOCK, G, D).transpose(0, 3, 1, 2, 4)
    vb = v.reshape(B, n_sel, SLC_BLOCK, G, v.shape[-1]).transpose(0, 3, 1, 2, 4)
    nq = S // SLC_Q_CHUNK
    idx_c = idx.transpose(0, 2, 1, 3).reshape(B, nq, SLC_Q_CHUNK, G, top_n).swapaxes(0, 1)
    q_c = q.reshape(B, nq, SLC_Q_CHUNK, G, Hg, D).swapaxes(0, 1)
    bi = jnp.arange(B)[:, None, None, None]
    gi = jnp.arange(G)[None, None, :, None]
    tok = jnp.arange(SLC_BLOCK)

    def step(args):
        qi, ii, start = args
        kg = kb[bi, gi, ii]
        vg = vb[bi, gi, ii]
        qp = start + jnp.arange(SLC_Q_CHUNK)
        dist = qp[None, :, None, None, None] - (ii[..., None] * SLC_BLOCK + tok)
        bias = tbl[t5_bucket(dist), gi[..., None]]
        s = jnp.einsum('bqghd,bqgnkd->bqghnk', qi, kg).astype(jnp.float32) * (D ** -0.5) + jnp.moveaxis(bias, -1, 3)
        s = jnp.where((dist >= 0)[:, :, :, None], s, -jnp.inf)
        p = jax.nn.softmax(s.reshape(s.shape[:4] + (-1,)), axis=-1).astype(v.dtype)
        return jnp.einsum('bqghk,bqgkd->bqghd', p, vg.reshape(vg.shape[:3] + (-1, vg.shape[-1])))

    o = lax.map(step, (q_c, idx_c, jnp.arange(nq) * SLC_Q_CHUNK))
    return o.swapaxes(0, 1).reshape(B, S, G, Hg, v.shape[-1])


def mla_attention(q_nope, q_rope, k_nope, k_rope, v):
    B, S, H, _ = q_nope.shape
    nb = S // Q_BLOCK
    kpos = jnp.arange(S)
    scale = (D_NOPE + D_ROPE) ** -0.5

    def blocks(t):
        return t.reshape((B, nb, Q_BLOCK) + t.shape[2:]).swapaxes(0, 1)

    def step(args):
        qn, qr, start = args
        s = (jnp.einsum('bqhd,bkhd->bhqk', qn, k_nope) + jnp.einsum('bqhr,bkr->bhqk', qr, k_rope)).astype(jnp.float32) * scale
        s = jnp.where((start + jnp.arange(Q_BLOCK))[:, None] >= kpos[None, :], s, -jnp.inf)
        p = jax.nn.softmax(s, axis=-1).astype(v.dtype)
        return jnp.einsum('bhqk,bkhd->bqhd', p, v)

    o = lax.map(step, (blocks(q_nope), blocks(q_rope), jnp.arange(nb) * Q_BLOCK))
    return o.swapaxes(0, 1).reshape(B, S, H, v.shape[-1])


def even_mixer(x, w_in, w_out, lam_p, subln_g, bias_table, layer_idx):
    B, S, _ = x.shape
    qa, ka, va, hb = split_cols(x @ w_in, EVEN_SPLITS)
    qa = qa.reshape(B, S, 2, A_HEADS, A_QK)
    ka = ka.reshape(B, S, 2, A_HEADS, A_QK)
    va = va.reshape(B, S, A_HEADS, A_V)
    lam_init = 0.8 - 0.6 * math.exp(-0.3 * layer_idx)
    lp = lam_p.astype(jnp.float32)
    lam = jnp.exp(jnp.sum(lp[0] * lp[1])) - jnp.exp(jnp.sum(lp[2] * lp[3])) + lam_init
    o_a = diff_attention(qa, ka, va, lam, bias_table[:, :A_HEADS])
    o_a = rms_norm(o_a, subln_g) * (1.0 - lam_init)
    o_b = dilated_mixture(hb.reshape(B, S, len(B_PAIRS), 3, B_HEADS, B_DIM), bias_table)
    y = jnp.concatenate([o_a.reshape(B, S, -1), o_b.reshape(B, S, -1)], axis=-1)
    return y @ w_out


def odd_mixer(x, w_in, w_out, cmp_pe, cmp_w1, cmp_w2, q_norm_g, kv_norm_g, w_uq, w_uk, w_uv, bias_table):
    B, S, _ = x.shape
    G, Hg = C_KV_GROUPS, C_HEADS // C_KV_GROUPS
    q_c, kv_c, gate_c, cq, ckv, k_rope = split_cols(x @ w_in, ODD_SPLITS)
    q = q_c.reshape(B, S, G, Hg, C_DIM)
    kv = kv_c.reshape(B, S, 3, 2, G, C_DIM)
    gates = jax.nn.sigmoid(gate_c.astype(jnp.float32)).reshape(B, S, 3, G, Hg, 1).astype(x.dtype)
    tbl_c = bias_table[:, :C_HEADS].reshape(N_BUCKETS, G, Hg)
    o_cmp, slc_score = nsa_compressed(q, kv[:, :, 0, 0], kv[:, :, 0, 1], cmp_pe, cmp_w1, cmp_w2, tbl_c)
    o_slc = nsa_selected(q, kv[:, :, 1, 0], kv[:, :, 1, 1], slc_score, tbl_c)
    o_win, _ = banded_attention(q, kv[:, :, 2, 0], kv[:, :, 2, 1], WIN_SIZE - 1, tbl_c, 1)
    o_c = gates[:, :, 0] * o_cmp + gates[:, :, 1] * o_slc + gates[:, :, 2] * o_win
    pos = jnp.arange(S)
    qd = (rms_norm(cq, q_norm_g) @ w_uq).reshape(B, S, D_HEADS, D_NOPE + D_ROPE)
    c = rms_norm(ckv, kv_norm_g)
    k_nope = (c @ w_uk).reshape(B, S, D_HEADS, D_NOPE)
    v = (c @ w_uv).reshape(B, S, D_HEADS, D_V)
    o_d = mla_attention(qd[..., :D_NOPE], rope(qd[..., D_NOPE:], pos), k_nope, rope(k_rope, pos), v)
    y = jnp.concatenate([o_c.reshape(B, S, -1), o_d.reshape(B, S, -1)], axis=-1)
    return y @ w_out


def conv_ffn(x, w_up, conv_w, conv_b, w_down):
    a, g = jnp.split(x @ w_up, 2, axis=-1)
    g = lax.conv_general_dilated(g, conv_w[:, None, :].astype(g.dtype), window_strides=(1,),
                                 padding=((CONV_W - 1, 0),), dimension_numbers=('NWC', 'WIO', 'NWC'),
                                 feature_group_count=D_FF) + conv_b
    return (jax.nn.gelu(g) * a) @ w_down


def setup_inputs(seed: int = 0) -> dict:
    key = jax.random.key(seed)
    ks = jax.random.split(key, 22)
    f32 = jnp.float32
    n_ev, n_od = (DEPTH + 1) // 2, DEPTH // 2
    beta = (8 * DEPTH) ** -0.25

    def w(k, shape, fan_in, gain=1.0):
        return jax.random.normal(k, shape, f32) * (gain * fan_in ** -0.5)

    def near_one(k, shape):
        return 1.0 + 0.02 * jax.random.normal(k, shape, f32)

    def small(k, shape, std):
        return std * jax.random.normal(k, shape, f32)

    return {
        'x': jax.random.normal(ks[0], (BATCH, SEQ, D_MODEL), f32),
        'rel_bias': small(ks[1], (N_BUCKETS, N_BIAS_COLS), 0.2),
        'ev_w_in': w(ks[2], (n_ev, D_MODEL, EVEN_IN), D_MODEL),
        'ev_w_out': w(ks[3], (n_ev, EVEN_OUT, D_MODEL), EVEN_OUT, beta),
        'ev_lambda': small(ks[4], (n_ev, 4, A_QK), 0.1),
        'ev_subln': near_one(ks[5], (n_ev, A_V)),
        'od_w_in': w(ks[6], (n_od, D_MODEL, ODD_IN), D_MODEL),
        'od_w_out': w(ks[7], (n_od, ODD_OUT, D_MODEL), ODD_OUT, beta),
        'od_cmp_pe': small(ks[8], (n_od, 2, CMP_LEN, C_DIM), 0.1),
        'od_cmp_w1': w(ks[9], (n_od, 2, CMP_LEN * C_DIM, CMP_HIDDEN), CMP_LEN * C_DIM),
        'od_cmp_w2': w(ks[10], (n_od, 2, CMP_HIDDEN, C_DIM), CMP_HIDDEN),
        'od_q_norm': near_one(ks[11], (n_od, D_Q_LORA)),
        'od_kv_norm': near_one(ks[12], (n_od, D_KV_LORA)),
        'od_w_uq': w(ks[13], (n_od, D_Q_LORA, D_HEADS * (D_NOPE + D_ROPE)), D_Q_LORA),
        'od_w_uk': w(ks[14], (n_od, D_KV_LORA, D_HEADS * D_NOPE), D_KV_LORA),
        'od_w_uv': w(ks[15], (n_od, D_KV_LORA, D_HEADS * D_V), D_KV_LORA),
        'ffn_w_up': w(ks[16], (DEPTH, D_MODEL, 2 * D_FF), D_MODEL),
        'ffn_conv_w': w(ks[17], (DEPTH, CONV_W, D_FF), CONV_W),
        'ffn_conv_b': small(ks[18], (DEPTH, D_FF), 0.02),
        'ffn_w_down': w(ks[19], (DEPTH, D_FF, D_MODEL), D_FF, beta),
        'ln_g': near_one(ks[20], (DEPTH, 2, D_MODEL)),
        'ln_b': small(ks[21], (DEPTH, 2, D_MODEL), 0.02),
    }


def reference(x, rel_bias, ev_w_in, ev_w_out, ev_lambda, ev_subln, od_w_in, od_w_out, od_cmp_pe, od_cmp_w1,
              od_cmp_w2, od_q_norm, od_kv_norm, od_w_uq, od_w_uk, od_w_uv, ffn_w_up, ffn_conv_w, ffn_conv_b,
              ffn_w_down, ln_g, ln_b):
    alpha = (2 * DEPTH) ** 0.25
    for l in range(DEPTH):
        i = l // 2
        if l % 2 == 0:
            y = even_mixer(x, ev_w_in[i], ev_w_out[i], ev_lambda[i], ev_subln[i], rel_bias, l)
        else:
            y = odd_mixer(x, od_w_in[i], od_w_out[i], od_cmp_pe[i], od_cmp_w1[i], od_cmp_w2[i], od_q_norm[i],
                          od_kv_norm[i], od_w_uq[i], od_w_uk[i], od_w_uv[i], rel_bias)
        x = layer_norm(alpha * x + y, ln_g[l, 0], ln_b[l, 0])
        y = conv_ffn(x, ffn_w_up[l], ffn_conv_w[l], ffn_conv_b[l], ffn_w_down[l])
        x = layer_norm(alpha * x + y, ln_g[l, 1], ln_b[l, 1])
    return x
```

```python
import math
from contextlib import ExitStack

import numpy as np
import concourse.bass as bass
import concourse.mybir as mybir
from concourse.bass_utils import run_bass_kernel_spmd

F32 = mybir.dt.float32
BF16 = mybir.dt.bfloat16
I32 = mybir.dt.int32
AF = mybir.ActivationFunctionType
ALU = mybir.AluOpType
AX = mybir.AxisListType


class Res:
    __slots__ = ("lw", "rd", "name")

    def __init__(self, name=""):
        self.lw = None
        self.rd = {}
        self.name = name


class Ctx:
    NRING = 12

    def __init__(self, nc):
        self.nc = nc
        self.eng = {"pe": nc.tensor, "act": nc.scalar, "dve": nc.vector, "pool": nc.gpsimd, "sp": nc.sync}
        self.sem = {}
        self.cnt = {}
        for e in ("pe", "act", "dve", "pool"):
            self.sem[e] = nc.alloc_semaphore("s_" + e)
            self.cnt[e] = 0
        self.rings = {}
        for q in ("sp", "pool", "act"):
            keys = []
            for i in range(self.NRING):
                k = "d_%s_%d" % (q, i)
                self.sem[k] = nc.alloc_semaphore(k)
                self.cnt[k] = 0
                keys.append(k)
            self.rings[q] = [keys, 0]
        self.seen = {e: {} for e in self.eng}
        self.n_wait = 0
        self.n_ins = 0

    def _need(self, e, needs, ev):
        k, v, _ = ev
        if self.seen[e].get(k, 0) >= v:
            return
        if needs.get(k, 0) < v:
            needs[k] = v

    def _deps(self, e, reads, writes):
        needs = {}
        for r in reads:
            if r.lw is not None:
                if not (r.lw[2] == e and e == "pe" and r.lw[0] == "pe"):
                    self._need(e, needs, r.lw)
        for w in writes:
            if w.lw is not None and not (w.lw[0] == e):
                self._need(e, needs, w.lw)
            for k, (v, re_) in w.rd.items():
                if k != e:
                    self._need(e, needs, (k, v, re_))
        for k, v in needs.items():
            if not (getattr(self, "skip_pe_waits", False) and e == "pe"):
                self.eng[e].wait_ge(self.sem[k], v)
            self.seen[e][k] = v
            self.n_wait += 1

    def _commit(self, ev, reads, writes):
        k, v, e = ev
        for r in reads:
            r.rd[k] = (v, e)
        for w in writes:
            w.lw = ev
            w.rd = {}

    def op(self, e, reads, writes, fn):
        self._deps(e, reads, writes)
        ins = fn()
        self.cnt[e] += 1
        ins.then_inc(self.sem[e], 1)
        self.n_ins += 1
        self._commit((e, self.cnt[e], e), reads, writes)
        return ins

    def dma(self, q, reads, writes, out, in_, **kw):
        keys, idx = self.rings[q]
        k = keys[idx % self.NRING]
        self.rings[q][1] = idx + 1
        if self.cnt[k] > 0 and self.seen[q].get(k, 0) < self.cnt[k]:
            self.eng[q].wait_ge(self.sem[k], self.cnt[k])
            self.seen[q][k] = self.cnt[k]
        self._deps(q, reads, writes)
        ins = self.eng[q].dma_start(out=out, in_=in_, **kw)
        self.cnt[k] += 16
        ins.then_inc(self.sem[k], 16)
        self.n_ins += 1
        self._commit((k, self.cnt[k], "dma"), reads, writes)
        return ins

    def barrier(self):
        for e in self.eng:
            for k, v in self.cnt.items():
                if v > 0 and k != e and self.seen[e].get(k, 0) < v:
                    self.eng[e].wait_ge(self.sem[k], v)
                    self.seen[e][k] = v


S = 4096
DM = 1024
NTC = 8
TC = 512
PADL = 4112
LF = PADL + 4096
PADB = 127
LB = 384
ALPHA = (2 * 4) ** 0.25
DFF = 2816


class TT:
    __slots__ = ("h", "r")

    def __init__(self, h):
        self.h = h
        self.r = Res()

    def __getitem__(self, idx):
        return self.h[idx]


class DT:
    def __init__(self, h):
        self.h = h
        self.rs = {}

    def res(self, key=0):
        if key not in self.rs:
            self.rs[key] = Res()
        return self.rs[key]

    def all(self):
        return list(self.rs.values())

    def ap(self):
        return self.h.ap()


def _r(x):
    return x.r if isinstance(x, TT) else x


class XB:
    def __init__(self, tt):
        self.h = tt.h
        self.parts = [TT(tt.h) for _ in range(4)]

    def __getitem__(self, idx):
        return self.h[idx]

    def t(self, n):
        return self.parts[n // 2]

    def all(self):
        return list(self.parts)


class K:
    def __init__(self, nc):
        self.nc = nc
        self.c = Ctx(nc)
        self.es = ExitStack()
        self.ph = None
        self.uid = 0
        self.ps_all = [self._ps() for _ in range(8)]
        self.set_pools(3, 4, 1)

    def _ps(self):
        self.uid += 1
        return TT(self.es.enter_context(self.nc.psum_tensor("ps%d" % self.uid, [128, 512], F32)))

    def set_pools(self, ns, na, nm):
        assert ns + na + nm == 8
        self.ps_s = self.ps_all[0:ns]
        self.ps_a = self.ps_all[ns:ns + na]
        self.ps_m = self.ps_all[ns + na:]
        self.i_s = self.i_a = self.i_m = 0

    def pS(self):
        self.i_s += 1
        return self.ps_s[self.i_s % len(self.ps_s)]

    def pA(self):
        self.i_a += 1
        return self.ps_a[self.i_a % len(self.ps_a)]

    def pM(self):
        self.i_m += 1
        return self.ps_m[self.i_m % len(self.ps_m)]

    def begin(self):
        self.ph = ExitStack()

    def end(self):
        self.c.barrier()
        self.ph.close()
        self.ph = None
        self.set_pools(3, 4, 1)

    def sb(self, shape, dtype, glob=False, mid=None):
        self.uid += 1
        st = mid if mid is not None else (self.es if glob else self.ph)
        return TT(st.enter_context(self.nc.sbuf_tensor("t%d" % self.uid, list(shape), dtype)))

    def rot(self, n, shape, dtype):
        bufs = [self.sb(shape, dtype) for _ in range(n)]
        st = [0]

        def nxt():
            st[0] += 1
            return bufs[st[0] % n]
        return nxt

    def dram(self, shape, dtype, name=None):
        self.uid += 1
        if name is not None and name in getattr(self, "dbg", ()):
            return DT(self.nc.dram_tensor("dbg_" + name, list(shape), dtype, kind="ExternalOutput"))
        return DT(self.nc.dram_tensor("scr%d" % self.uid, list(shape), dtype))

    def op(self, e, reads, writes, fn):
        return self.c.op(e, [_r(x) for x in reads], [_r(x) for x in writes], fn)

    def ld(self, reads, writes, out, in_, q="sp", slow=False):
        kw = {"allow_slow_non_contiguous": True} if slow else {}
        return self.c.dma(q, [_r(x) for x in reads], [_r(x) for x in writes], out, in_, **kw)

    def st(self, reads, writes, out, in_, q="pool"):
        return self.c.dma(q, [_r(x) for x in reads], [_r(x) for x in writes], out, in_)

    def mm(self, out_t, out_ap, lhs_t, lhs_ap, rhs_t, rhs_ap, start, stop):
        nc = self.nc
        reads = (list(lhs_t) if isinstance(lhs_t, (list, tuple)) else [lhs_t]) + \
                (list(rhs_t) if isinstance(rhs_t, (list, tuple)) else [rhs_t])
        return self.op("pe", reads, [out_t],
                       lambda: nc.tensor.matmul(out_ap, lhsT=lhs_ap, rhs=rhs_ap, start=start, stop=stop))

    def act(self, out_t, out_ap, in_t, in_ap, func, bias=None, scale=1.0, extra=()):
        nc = self.nc
        kw = {}
        if bias is not None:
            kw["bias"] = bias
        return self.op("act", [in_t] + list(extra), [out_t],
                       lambda: nc.scalar.activation(out=out_ap, in_=in_ap, func=func, scale=scale, **kw))

    def tt(self, e, out_t, out_ap, a_t, a_ap, b_t, b_ap, op):
        eng = self.nc.vector if e == "dve" else self.nc.gpsimd
        return self.op(e, [a_t, b_t], [out_t], lambda: eng.tensor_tensor(out=out_ap, in0=a_ap, in1=b_ap, op=op))

    def ts(self, e, out_t, out_ap, a_t, a_ap, s1, op0, s2=None, op1=None, extra=()):
        eng = self.nc.vector if e == "dve" else self.nc.gpsimd
        if op1 is None:
            return self.op(e, [a_t] + list(extra), [out_t],
                           lambda: eng.tensor_scalar(out=out_ap, in0=a_ap, scalar1=s1, scalar2=None, op0=op0))
        return self.op(e, [a_t] + list(extra), [out_t],
                       lambda: eng.tensor_scalar(out=out_ap, in0=a_ap, scalar1=s1, scalar2=s2, op0=op0, op1=op1))

    def stt(self, e, out_t, out_ap, a_t, a_ap, scalar, b_t, b_ap, op0, op1, extra=()):
        eng = self.nc.vector if e == "dve" else self.nc.gpsimd
        return self.op(e, [a_t, b_t] + list(extra), [out_t],
                       lambda: eng.scalar_tensor_tensor(out=out_ap, in0=a_ap, scalar=scalar, in1=b_ap, op0=op0, op1=op1))

    def cp(self, e, out_t, out_ap, in_t, in_ap):
        nc = self.nc
        if e == "act":
            return self.op("act", [in_t], [out_t], lambda: nc.scalar.copy(out=out_ap, in_=in_ap))
        eng = nc.vector if e == "dve" else nc.gpsimd
        return self.op(e, [in_t], [out_t], lambda: eng.tensor_copy(out=out_ap, in_=in_ap))

    def memset(self, e, t, ap, val):
        eng = self.nc.vector if e == "dve" else self.nc.gpsimd
        return self.op(e, [], [t], lambda: eng.memset(ap, val))

    def recip(self, out_t, out_ap, in_t, in_ap):
        nc = self.nc
        return self.op("dve", [in_t], [out_t], lambda: nc.vector.reciprocal(out=out_ap, in_=in_ap))


def t5_bucket_np(dist):
    dist = np.asarray(dist, dtype=np.int64)
    n = np.maximum(dist, 0)
    nf = np.maximum(n, 1).astype(np.float32)
    large = 16 + (np.log(nf / np.float32(16)) / np.float32(math.log(2048 / 16)) * np.float32(16)).astype(np.int32)
    return np.where(n < 16, n, np.minimum(large, 31))


def host_consts():
    cs = {}
    oh = np.zeros((32, 4096), np.float32)
    oh[t5_bucket_np(np.arange(4096)), np.arange(4096)] = 1.0
    cs["c_ohF"] = oh
    ohb = np.zeros((3, 32, 129), np.float32)
    for p, d in enumerate((1, 4, 16)):
        idx = np.arange(129)
        ohb[p, t5_bucket_np(idx * d), idx] = 1.0
    cs["c_ohB"] = ohb
    cs["c_ident"] = np.eye(128, dtype=np.float32)
    cs["c_J"] = np.eye(128, dtype=np.float32)[::-1].copy()
    M = np.zeros((256, 64), np.float32)
    for j in range(64):
        for cc, w in ((4 * j - 1, .5), (4 * j, 1.), (4 * j + 1, 1.), (4 * j + 2, 1.), (4 * j + 3, .5)):
            if 0 <= cc < 255:
                M[cc, j] += w
    cs["c_Msel"] = M
    q = np.arange(4096)[:, None]
    jb = np.arange(64)[None, :]
    qb = q // 64
    fm = np.where(jb > qb, -1e4, 0.0) + np.where((jb == 0) | (jb == qb) | (jb == qb - 1), 1e4, 0.0)
    cs["c_Fm"] = fm.astype(np.float32)
    ex = np.zeros((64, 32, 128), np.float32)
    for i in range(32):
        for k in range(128):
            ex[2 * i + k // 64, i, k] = 1.0
    cs["c_Ex"] = ex
    sel = np.zeros((24, 24, 128), np.float32)
    for r in range(24):
        sel[r, r, :] = 1.0
    cs["c_Sel"] = sel
    half = 16
    inv = (np.float32(10000.0) ** (-np.arange(half, dtype=np.float32) / np.float32(half))).astype(np.float32)
    ang = np.arange(4096, dtype=np.float32)[None, :] * inv[:, None]
    cs["c_cos"] = np.concatenate([np.cos(ang), np.cos(ang)], 0).astype(np.float32)
    cs["c_sin"] = np.concatenate([np.sin(ang), np.sin(ang)], 0).astype(np.float32)
    return cs


CONST_SHAPES = {"c_ohF": [32, 4096], "c_ohB": [3, 32, 129], "c_ident": [128, 128], "c_J": [128, 128],
                "c_Msel": [256, 64], "c_Fm": [4096, 64], "c_Ex": [64, 32, 128], "c_Sel": [24, 24, 128],
                "c_cos": [32, 4096], "c_sin": [32, 4096]}


def setup_globals(k, rel_bias, cin):
    nc = k.nc
    g = {}
    for nm in ("ident", "J", "ones", "zeros"):
        g[nm] = k.sb([128, 128], BF16, glob=True)
    g["ones32"] = k.sb([128, 128], F32, glob=True)
    g["rsel0"] = k.sb([128, 128], BF16, glob=True)
    g["rsel1"] = k.sb([128, 128], BF16, glob=True)
    g["eps"] = k.sb([128, 1], F32, glob=True)
    g["b31"] = k.sb([128, 16], F32, glob=True)
    k.begin()
    st32r = k.rot(2, [128, 128], F32)
    for nm in ("ident", "J"):
        st32 = st32r()
        k.ld([], [st32], st32[:], cin["c_" + nm].ap())
        k.cp("dve", g[nm], g[nm][:], st32, st32[:])
    k.memset("pool", g["ones"], g["ones"][:], 1.0)
    k.memset("pool", g["ones32"], g["ones32"][:], 1.0)
    k.memset("pool", g["zeros"], g["zeros"][:], 0.0)
    k.memset("pool", g["rsel0"], g["rsel0"][:], 0.0)
    k.memset("pool", g["rsel1"], g["rsel1"][:], 0.0)
    k.memset("pool", g["rsel0"], g["rsel0"][64:65, :], 1.0)
    k.memset("pool", g["rsel1"], g["rsel1"][0:1, :], 1.0)
    k.memset("pool", g["eps"], g["eps"][:], 1e-5)
    k.ld([], [g["b31"]], g["b31"][:], bass.AP(rel_bias, 31 * 16, [[0, 128], [1, 16]]))
    tbl = k.sb([32, 16], F32)
    k.ld([], [tbl], tbl[:], rel_bias.ap())
    oh = k.sb([32, 4096], F32)
    k.ld([], [oh], oh[:], cin["c_ohF"].ap())
    stg = k.sb([16, LF], BF16)
    k.memset("pool", stg, stg[:], 0.0)
    stl = k.sb([16, LF], BF16)
    k.memset("pool", stl, stl[:], -30000.0)
    for n in range(8):
        ps = k.pM()
        k.mm(ps, ps[0:16, :], tbl, tbl[:], oh, oh[:, n * 512:(n + 1) * 512], True, True)
        k.act(stg, stg[:, PADL + n * 512:PADL + (n + 1) * 512], ps, ps[0:16, :], AF.Exp)
        k.act(stl, stl[:, PADL + n * 512:PADL + (n + 1) * 512], ps, ps[0:16, :], AF.Copy, scale=8.0)
    vecF = k.dram([16, LF], BF16)
    k.st([stg], [vecF.res()], vecF.ap(), stg[:])
    vecFl = k.dram([16, LF], BF16)
    k.st([stl], [vecFl.res()], vecFl.ap(), stl[:])
    stw = k.sb([16, LF], BF16)
    k.memset("pool", stw, stw[:], -30000.0)
    k.cp("dve", stw, stw[:, PADL:PADL + 512], stl, stl[:, PADL:PADL + 512])
    vecW = k.dram([16, LF], BF16)
    k.st([stw], [vecW.res()], vecW.ap(), stw[:])
    stm = k.sb([16, LF], BF16)
    k.memset("pool", stm, stm[:], -30000.0)
    k.memset("pool", stm, stm[:, PADL:], 0.0)
    vecM = k.dram([16, LF], BF16)
    k.st([stm], [vecM.res()], vecM.ap(), stm[:])
    g["vecF"], g["vecW"], g["vecM"] = vecF, vecW, vecM
    vecB = []
    for p in range(3):
        ohb = k.sb([32, 129], F32)
        k.ld([], [ohb], ohb[:], cin["c_ohB"].ap()[p])
        sb_ = k.sb([16, LB], BF16)
        k.memset("pool", sb_, sb_[:], 0.0)
        ps = k.pM()
        k.mm(ps, ps[0:16, 0:129], tbl, tbl[:], ohb, ohb[:], True, True)
        k.act(sb_, sb_[:, PADB:PADB + 129], ps, ps[0:16, 0:129], AF.Exp)
        vb = k.dram([16, LB], BF16)
        k.st([sb_], [vb.res()], vb.ap(), sb_[:])
        vecB.append(vb)
    g["vecB"] = vecB
    EF = k.dram([8, 16, 128, TC], BF16)
    EW = k.dram([8, 8, 128, TC], BF16)
    EC = k.dram([8, 16, 128, TC], BF16)
    EM = k.dram([4, 128, TC], BF16)
    hrot = k.rot(4, [128, TC], BF16)
    est = k.rot(4, [128, TC], BF16)
    jobs = []
    for h in range(8):
        for dl in range(-3, 13):
            jobs.append((vecFl, h, toep_off(dl), 1, EF, EF.ap()[h, dl + 3]))
        for dl in range(-3, 5):
            jobs.append((vecW, h, toep_off(dl), 1, EW, EW.ap()[h, dl + 3]))
        for j in range(8):
            for cbk in range(2):
                jobs.append((vecF, h, PADL - 31 + 512 * j - 2048 * cbk - 2032, 16, EC, EC.ap()[h, 2 * j + cbk]))
    for dl in range(-3, 1):
        jobs.append((vecM, 0, toep_off(dl), 1, EM, EM.ap()[dl + 3]))
    for ji, (vec, row, off, pstep, dst, dap) in enumerate(jobs):
        H = hrot()
        k.ld([vec.res()], [H], H[:], bass.AP(vec.h, row * LF + off, [[pstep, 128], [1, TC]]))
        ps = k.pA()
        k.mm(ps, ps[:], g["J"], g["J"][:], H, H[:], True, True)
        e_ = est()
        k.cp("act" if ji % 2 else "dve", e_, e_[:], ps, ps[:])
        k.st([e_], [dst.res(ji)], dap, e_[:])
    g["EF"], g["EW"], g["EC"], g["EM"] = EF, EW, EC, EM
    exs = k.sb([64, 32, 128], F32)
    exb = k.sb([64, 32, 128], BF16)
    k.ld([], [exs], exs[:], cin["c_Ex"].ap())
    k.cp("pool", exb, exb[:], exs, exs[:])
    exbf = k.dram([64, 32, 128], BF16)
    k.st([exb], [exbf.res()], exbf.ap(), exb[:])
    g["exbf"] = exbf
    k.end()
    return g


def load_E(k, g, dst, dst_ap, vec, row, L, off, pstep, W, hrot):
    H = hrot()
    k.ld([vec.res()], [H], H[:, 0:W], bass.AP(vec.h, row * L + off, [[pstep, 128], [1, W]]))
    ps = k.pM()
    k.mm(ps, ps[:, 0:W], g["J"], g["J"][:], H, H[:, 0:W], True, True)
    k.cp("act", dst, dst_ap, ps, ps[:, 0:W])


def toep_off(delta):
    return PADL + 128 * delta - 127


def load_xb(k, xin, xb_tt):
    xb = XB(xb_tt)
    if getattr(xin, "bf", None) is not None:
        xv = xin.bf.ap().rearrange("(kc p) t -> p kc t", p=128)
        for n in range(4):
            k.ld(xin.bf.all(), [xb.parts[n]], xb[:, :, n * 1024:(n + 1) * 1024], xv[:, :, n * 1024:(n + 1) * 1024])
        return xb
    stg = k.rot(2, [128, 1024], F32)
    xv = xin.ap().rearrange("(kc p) t -> p kc t", p=128)
    i = 0
    for q4 in range(4):
        for kc in range(8):
            s = stg()
            k.ld([xin.res(kc)], [s], s[:], xv[:, kc, q4 * 1024:(q4 + 1) * 1024])
            k.cp("dve" if i % 2 == 0 else "act", xb.parts[q4], xb[:, kc, q4 * 1024:(q4 + 1) * 1024], s, s[:])
            i += 1
    return xb


def load_w_bf16(k, w_ap, nk, ncols, dst, dst_ap, stg_rot, eng="pool"):
    s = stg_rot()
    k.ld([], [s], s[:, 0:nk, 0:ncols], w_ap.rearrange("(kc p) m -> p kc m", p=128))
    k.cp(eng, dst, dst_ap, s, s[:, 0:nk, 0:ncols])


def ln_block(k, g, z, nchunk, gam, bet, dst_fn):
    nc = k.nc
    zb = k.ln_zb()
    sq = k.ln_sq()
    s1 = k.pA()
    s2 = k.pA()
    for m in range(nchunk):
        k.cp("act", zb, zb[:, m, :], z, z[:, m, :])
        k.op("act", [z], [sq], lambda m=m: nc.scalar.activation(out=sq[:, m, :], in_=z[:, m, :], func=AF.Square))
    for m in range(nchunk):
        k.mm(s1, s1[:], g["ones"], g["ones"][:], zb, zb[:, m, :], m == 0, m == nchunk - 1)
    for m in range(nchunk):
        k.mm(s2, s2[:], g["ones"], g["ones"][:], sq, sq[:, m, :], m == 0, m == nchunk - 1)
    nf = float(nchunk * 128)
    mean = k.ln_s()
    k.op("act", [s1], [mean], lambda: nc.scalar.mul(out=mean[:], in_=s1[:], mul=1.0 / nf))
    msq = k.ln_s()
    k.tt("dve", msq, msq[:], mean, mean[:], mean, mean[:], ALU.mult)
    var = k.ln_s()
    k.stt("dve", var, var[:], s2, s2[:], 1.0 / nf, msq, msq[:], ALU.mult, ALU.subtract)
    sd = k.ln_s()
    k.act(sd, sd[:], var, var[:], AF.Sqrt, bias=g["eps"][:], extra=[g["eps"]])
    rstd = k.ln_s()
    k.recip(rstd, rstd[:], sd, sd[:])
    if hasattr(k, "ln_dump"):
        for nm_, t_ in (("mean", mean), ("msq", msq), ("var", var), ("sd", sd), ("rstd", rstd)):
            k.ln_dump(nm_, t_)
    for m in range(nchunk):
        t = k.ln_t()
        k.tt("dve", t, t[:], z, z[:, m, :], mean, mean[:], ALU.subtract)
        t2 = k.ln_t()
        k.tt("pool", t2, t2[:], t, t[:], rstd, rstd[:], ALU.mult)
        o, oap = dst_fn(m)
        k.op("act", [t2, gam, bet], [o],
             lambda m=m, t2=t2, oap=oap: nc.scalar.activation(out=oap, in_=t2[:], func=AF.Identity,
                                                              scale=gam[:, m:m + 1], bias=bet[:, m:m + 1]))


def proj_resid_ln(k, g, yT_fn, nk, w_ap, xres, gam_ap, bet_ap, xout, w_pre=None):
    nc = k.nc
    if w_pre is not None:
        w = w_pre
    else:
        w = k.sb([128, nk, DM], BF16)
        wst = k.rot(1, [128, nk, 128], F32)
        for cc in range(8):
            load_w_bf16(k, w_ap[:, cc * 128:(cc + 1) * 128], nk, 128, w, w[:, :, cc * 128:(cc + 1) * 128], wst,
                        "pool" if cc % 2 else "dve")
    gam = k.sb([128, 8], F32)
    bet = k.sb([128, 8], F32)
    k.ld([], [gam], gam[:], gam_ap.rearrange("(m p) -> p m", p=128), slow=True)
    k.ld([], [bet], bet[:], bet_ap.rearrange("(m p) -> p m", p=128), slow=True)
    xr = k.rot(3, [128, 8, TC], F32)
    k.ln_sq = k.rot(1, [128, 8, TC], BF16)
    k.ln_zb = k.rot(1, [128, 8, TC], BF16)
    k.ln_t = k.rot(4, [128, TC], F32)
    k.ln_s = k.rot(5, [128, TC], F32)
    ost = k.rot(1, [128, 8, TC], F32)
    obt = k.rot(1, [128, 8, TC], BF16)
    xv = xres.ap().rearrange("(m p) t -> p m t", p=128)
    ov = xout.ap().rearrange("(m p) t -> p m t", p=128)
    obv = xout.bf.ap().rearrange("(m p) t -> p m t", p=128) if getattr(xout, "bf", None) is not None else None
    pend = Pend(1)
    for n in range(NTC):
        yt, yap = yT_fn(n)
        xt = xr()
        k.ld(xres.all(), [xt], xt[:], xv[:, :, n * TC:(n + 1) * TC])
        z = xt
        for m in range(8):
            ps = k.pS()
            for kc in range(nk):
                k.mm(ps, ps[:], w, w[:, kc, m * 128:(m + 1) * 128], yt, yap(kc), kc == 0, kc == nk - 1)
            k.stt("dve", z, z[:, m, :], xt, xt[:, m, :], ALPHA, ps, ps[:], ALU.mult, ALU.add)

        def fin(z=z, n=n):
            o = ost()
            ln_block(k, g, z, 8, gam, bet, lambda m, o=o: (o, o[:, m, :]))
            k.st([o], [xout.res(kc) for kc in range(8)], ov[:, :, n * TC:(n + 1) * TC], o[:])
            if obv is not None:
                ob = obt()
                k.cp("pool", ob, ob[:], o, o[:])
                k.st([ob], [xout.bf.res(n)], obv[:, :, n * TC:(n + 1) * TC], ob[:])
            return None
        pend.push(fin)
    pend.flush()


def ffn_phase(k, g, x1, w_up, conv_w, conv_b, w_down, ln_g, ln_b, xout):
    nc = k.nc
    hT = k.dram([DFF, S], BF16, "hT")
    mid = ExitStack()
    wdn = k.sb([128, 22, DM], BF16, mid=mid)
    k.begin()
    wdst = k.rot(1, [128, 22, 128], F32)
    xb = load_xb(k, x1, k.sb([128, 8, S], BF16))
    cw = k.sb([128, 3, 22], F32)
    k.ld([], [cw], cw[:], conv_w.rearrange("j (c p) -> p j c", p=128), slow=True)
    cb = k.sb([128, 22], F32)
    k.ld([], [cb], cb[:], conv_b.rearrange("(c p) -> p c", p=128), slow=True)
    wst = k.rot(3, [128, 8, 128], F32)
    wa_r = k.rot(2, [128, 8, 128], BF16)
    wg_r = k.rot(2, [128, 8, 128], BF16)
    gb_r = k.rot(2, [128, S + 2], F32)
    for _ in range(2):
        gb = gb_r()
        k.memset("pool", gb, gb[:, 0:2], 0.0)
    t_r = k.rot(5, [128, TC], F32)
    hst_r = k.rot(2, [128, S], BF16)

    def wload(cc):
        wa = wa_r()
        wg = wg_r()
        load_w_bf16(k, w_up[:, cc * 128:(cc + 1) * 128], 8, 128, wa, wa[:], wst, "pool")
        load_w_bf16(k, w_up[:, DFF + cc * 128:DFF + (cc + 1) * 128], 8, 128, wg, wg[:], wst, "pool")
        return wa, wg
    wcur = wload(0)
    for cc in range(22):
        wa, wg = wcur
        if cc + 1 < 22:
            wcur = wload(cc + 1)
        hst = hst_r()
        gb = gb_r()
        if cc % 2 == 1 and cc // 2 < 8:
            c8 = cc // 2
            load_w_bf16(k, w_down[:, c8 * 128:(c8 + 1) * 128], 22, 128, wdn, wdn[:, :, c8 * 128:(c8 + 1) * 128], wdst, "pool")
        for n in range(NTC):
            pa = k.pS()
            pg = k.pA()
            for kc in range(8):
                k.mm(pg, pg[:], wg, wg[:, kc, :], xb.t(n), xb[:, kc, n * TC:(n + 1) * TC], kc == 0, kc == 7)
            for kc in range(8):
                k.mm(pa, pa[:], wa, wa[:, kc, :], xb.t(n), xb[:, kc, n * TC:(n + 1) * TC], kc == 0, kc == 7)
            o = 2 + n * TC
            k.cp("act", gb, gb[:, o:o + TC], pg, pg[:])
            t1 = t_r()
            k.op("act", [pg, cw, cb], [t1],
                 lambda t1=t1, pg=pg, cc=cc: nc.scalar.activation(out=t1[:], in_=pg[:], func=AF.Identity,
                                                                 scale=cw[:, 2, cc:cc + 1], bias=cb[:, cc:cc + 1]))
            t2 = t_r()
            k.stt("dve", t2, t2[:], gb, gb[:, o - 1:o - 1 + TC], cw[:, 1, cc:cc + 1], t1, t1[:], ALU.mult, ALU.add, extra=[cw])
            t3 = t_r()
            k.stt("dve", t3, t3[:], gb, gb[:, o - 2:o - 2 + TC], cw[:, 0, cc:cc + 1], t2, t2[:], ALU.mult, ALU.add, extra=[cw])
            t4 = t_r()
            k.act(t4, t4[:], t3, t3[:], AF.Gelu_apprx_tanh)
            k.tt("dve", hst, hst[:, n * TC:(n + 1) * TC], t4, t4[:], pa, pa[:], ALU.mult)
        k.st([hst], [hT.res(cc)], hT.ap()[cc * 128:(cc + 1) * 128, :], hst[:])
    k.end()
    k.begin()
    hr = k.rot(2, [128, 22, TC], BF16)
    hv = hT.ap().rearrange("(c p) t -> p c t", p=128)

    def yT_fn(n):
        h = hr()
        k.ld(hT.all(), [h], h[:], hv[:, :, n * TC:(n + 1) * TC])
        return h, (lambda kc, h=h: h[:, kc, :])
    proj_resid_ln(k, g, yT_fn, 22, w_down, x1, ln_g, ln_b, xout, w_pre=wdn)
    k.end()
    mid.close()


def ssl(t0, n, d):
    return slice(t0, t0 + (n - 1) * d + 1, d)


class Pend:
    def __init__(self, depth=1):
        self.q = []
        self.depth = depth

    def _run(self, fn):
        r = fn()
        if callable(r):
            self.q.append(r)

    def push(self, fn):
        self.q.append(fn)
        while len(self.q) > self.depth:
            self._run(self.q.pop(0))

    def flush(self):
        while self.q:
            self._run(self.q.pop(0))


def proj_fm_load(k, w_in, col0, ncols, wst, wbf):
    w = wbf()
    load_w_bf16(k, w_in[:, col0:col0 + ncols], 8, ncols, w, w[:, :, 0:ncols], wst, "pool")
    return w


def proj_fm_compute(k, xb, w, ncols, out_dt, row0, stg_r, key):
    stg = stg_r()
    for n in range(NTC):
        ps = k.pS()
        for kc in range(8):
            k.mm(ps, ps[0:ncols, :], w, w[:, kc, 0:ncols], xb.t(n), xb[:, kc, n * TC:(n + 1) * TC], kc == 0, kc == 7)
        k.cp("act" if n % 2 == 0 else "dve", stg, stg[0:ncols, n * TC:(n + 1) * TC], ps, ps[0:ncols, :])
    k.st([stg], [out_dt.res(key)], out_dt.ap()[row0:row0 + ncols, :], stg[0:ncols, :])


def proj_fm_all(k, xb, w_in, cols, out_dt, wst, wbf, stg_r):
    w = proj_fm_load(k, w_in, cols[0], 128, wst, wbf)
    for ci, c0 in enumerate(cols):
        wn = proj_fm_load(k, w_in, cols[ci + 1], 128, wst, wbf) if ci + 1 < len(cols) else None
        proj_fm_compute(k, xb, w, 128, out_dt, ci * 128, stg_r, ci)
        w = wn


def even_proj(k, g, xin, w_in):
    hT = k.dram([2560, S], BF16, "ev_hT")
    vA = k.dram([S, 512], BF16, "ev_vA")
    vB = k.dram([3, 32, 128, 256], BF16, "ev_vB")
    k.begin()
    xb = load_xb(k, xin, k.sb([128, 8, S], BF16))
    wst = k.rot(2, [128, 8, 512], F32)
    wbf = k.rot(2, [128, 8, 128], BF16)
    stg_r = k.rot(2, [128, S], BF16)
    cols = list(range(0, 1024, 128))
    for p in range(3):
        base = 1536 + p * 768
        cols += [base, base + 128, base + 256, base + 384]
    proj_fm_all(k, xb, w_in, cols, hT, wst, wbf, stg_r)
    wv = k.sb([128, 8, 512], BF16)
    load_w_bf16(k, w_in[:, 1024:1536], 8, 512, wv, wv[:], wst, "pool")
    vst = k.rot(3, [128, 512], BF16)
    for b in range(32):
        ps = k.pS()
        for kc in range(8):
            k.mm(ps, ps[:], xb.t(b // 4), xb[:, kc, b * 128:(b + 1) * 128], wv, wv[:, kc, :], kc == 0, kc == 7)
        s = vst()
        k.cp("act" if b % 2 == 0 else "dve", s, s[:], ps, ps[:])
        k.st([s], [vA.res(b)], vA.ap()[b * 128:(b + 1) * 128, :], s[:])
    for p, d in enumerate((1, 4, 16)):
        base = 1536 + p * 768 + 512
        load_w_bf16(k, w_in[:, base:base + 256], 8, 256, wv, wv[:, :, 0:256], wst, "pool")
        nb = 32 // d
        for r in range(d):
            for b in range(nb):
                t0 = r + d * 128 * b
                ps = k.pS()
                for kc in range(8):
                    k.mm(ps, ps[:, 0:256], xb.all(), xb[:, kc, ssl(t0, 128, d)], wv, wv[:, kc, 0:256], kc == 0, kc == 7)
                s = vst()
                k.cp("act" if b % 2 == 0 else "dve", s, s[:, 0:256], ps, ps[:, 0:256])
                k.st([s], [vB.res((p, r * nb + b))], vB.ap()[p, r * nb + b], s[:, 0:256])
    k.end()
    return hT, vA, vB


def attn_A(k, g, hT, vA, yT, lam_p, subln, layer_idx):
    nc = k.nc
    lam_init = 0.8 - 0.6 * math.exp(-0.3 * layer_idx)
    k.begin()
    lpb = k.sb([128, 256], F32)
    k.ld([], [lpb], lpb[:], bass.AP(lam_p.tensor, lam_p.offset, [[0, 128], [1, 256]]))
    pr = k.sb([128, 128], F32)
    k.tt("dve", pr, pr[:, 0:64], lpb, lpb[:, 0:64], lpb, lpb[:, 64:128], ALU.mult)
    k.tt("dve", pr, pr[:, 64:128], lpb, lpb[:, 128:192], lpb, lpb[:, 192:256], ALU.mult)
    sm = k.sb([128, 2], F32)
    k.op("dve", [pr], [sm], lambda: nc.vector.reduce_sum(out=sm[:, 0:1], in_=pr[:, 0:64], axis=AX.X))
    k.op("dve", [pr], [sm], lambda: nc.vector.reduce_sum(out=sm[:, 1:2], in_=pr[:, 64:128], axis=AX.X))
    ex = k.sb([128, 2], F32)
    k.act(ex, ex[:], sm, sm[:], AF.Exp)
    neglam = k.sb([128, 1], F32)
    k.stt("dve", neglam, neglam[:], ex, ex[:, 1:2], -lam_init, ex, ex[:, 0:1], ALU.add, ALU.subtract)
    gsc = k.sb([128, 1], F32)
    k.ld([], [gsc], gsc[:], subln.rearrange("(p o) -> p o", o=1), slow=True)
    k.ts("dve", gsc, gsc[:], gsc, gsc[:], 1.0 - lam_init, ALU.mult)
    eps = g["eps"]
    Eh = k.sb([128, 16, TC], BF16)
    hrot = k.rot(2, [128, TC], BF16)
    vt_r = k.rot(2, [128, 32, 128], BF16)
    qk_r = k.rot(4, [128, S], BF16)
    for _ in range(4):
        t_ = qk_r()
        k.memset("pool", t_, t_[64:128, :], 0.0)
    p_r = k.rot(6, [128, TC], BF16)
    f_r = k.rot(12, [128, TC], F32)
    om_r = k.rot(6, [128, TC], F32)
    yst_r = k.rot(2, [128, S], BF16)
    vAv = vA.ap().rearrange("(b p) c -> p b c", p=128)
    pend = Pend(2)
    tile_i = 0
    for h in range(4):
        k.ld(g["EF"].all(), [Eh], Eh[:], g["EF"].ap()[h].rearrange("d p q -> p d q"))
        vt = vt_r()
        k.ld(vA.all(), [vt], vt[:], vAv[:, :, h * 128:(h + 1) * 128])
        QK = []
        for m in range(2):
            QT = qk_r()
            KT = qk_r()
            rq = m * 256 + h * 64
            k.ld(hT.all(), [QT], QT[0:64, :], hT.ap()[rq:rq + 64, :])
            k.ld(hT.all(), [KT], KT[0:64, :], hT.ap()[512 + rq:512 + rq + 64, :])
            QK.append((QT, KT))
        yst = yst_r()
        for j in range(NTC):
            oms = []
            for m in range(2):
                QT, KT = QK[m]
                O = k.pA()
                Dn = k.pA()
                nblk = 4 * j + 4
                om = om_r()
                oms.append(om)
                for i in range(nblk):
                    dl = 4 * j - i
                    near = dl < 13
                    ps = k.pS()
                    k.mm(ps, ps[:], KT, KT[:, i * 128:(i + 1) * 128], QT, QT[:, j * TC:(j + 1) * TC], True, not near)
                    P = p_r()
                    if near:
                        k.mm(ps, ps[:], g["ident"], g["ident"][:], Eh, Eh[:, dl + 3, :], False, True)
                        k.act(P, P[:], ps, ps[:], AF.Exp, scale=0.125)
                    else:
                        k.act(P, P[:], ps, ps[:], AF.Exp, bias=g["b31"][:, h:h + 1], scale=0.125, extra=[g["b31"]])

                    def fin(P=P, i=i, O=O, Dn=Dn, nblk=nblk, om=om, m=m, j=j, oms=oms, yst=yst, vt=vt):
                        k.mm(O, O[:], vt, vt[:, i, :], P, P[:], i == 0, i == nblk - 1)
                        k.mm(Dn, Dn[:], g["ones"], g["ones"][:], P, P[:], i == 0, i == nblk - 1)
                        if i < nblk - 1:
                            return None

                        def evac1():
                            rd = f_r()
                            k.recip(rd, rd[:], Dn, Dn[:])
                            k.tt("dve", om, om[:], O, O[:], rd, rd[:], ALU.mult)
                            if m == 0:
                                return None
                            o = f_r()
                            k.stt("dve", o, o[:], oms[1], oms[1][:], neglam[:, 0:1], oms[0], oms[0][:], ALU.mult, ALU.add,
                                  extra=[neglam])
                            sq = f_r()
                            k.act(sq, sq[:], o, o[:], AF.Square)

                            def evac2():
                                ss = k.pM()
                                k.mm(ss, ss[:], g["ones32"], g["ones32"][:], sq, sq[:], True, True)
                                sd = f_r()
                                k.act(sd, sd[:], ss, ss[:], AF.Sqrt, bias=eps[:], scale=1.0 / 128.0, extra=[eps])
                                rs = f_r()
                                k.recip(rs, rs[:], sd, sd[:])
                                k.stt("dve", yst, yst[:, j * TC:(j + 1) * TC], o, o[:], gsc[:, 0:1], rs, rs[:], ALU.mult,
                                      ALU.mult, extra=[gsc])
                                return None
                            return evac2
                        return evac1
                    pend.push(fin)
        pend.flush()
        k.st([yst], [yT.res(h)], yT.ap()[h * 128:(h + 1) * 128, :], yst[:])
    k.end()


def attn_B(k, g, hT, vB, yT):
    nc = k.nc
    k.begin()
    accO = k.sb([128, S], F32)
    accD = k.sb([128, S], F32)
    Vp_r = k.rot(1, [128, 32, 256], BF16)
    qk_r = k.rot(4, [128, S], BF16)
    for _ in range(4):
        t_ = qk_r()
        k.memset("pool", t_, t_[64:128, :], 0.0)
    E_r = k.rot(4, [128, 4, 128], BF16)
    hrot = k.rot(2, [128, TC], BF16)
    p0_r = k.rot(4, [128, TC], BF16)
    p_r = k.rot(4, [128, TC], BF16)
    yst_r = k.rot(1, [128, S], BF16)
    rd_r = k.rot(2, [128, TC], F32)
    pend = Pend()
    for hp in range(2):
        for p, d in enumerate((1, 4, 16)):
            nb = 32 // d
            G = min(4, nb)
            W = G * 128
            Vp = Vp_r()
            k.ld(vB.all(), [Vp], Vp[:], vB.ap()[p].rearrange("b t c -> t b c"))
            for hh in range(2):
                h = 2 * hp + hh
                R0 = hh * 64
                QT = qk_r()
                KT = qk_r()
                rq = 1024 + p * 512 + h * 64
                k.ld(hT.all(), [QT], QT[0:64, :], hT.ap()[rq:rq + 64, :])
                k.ld(hT.all(), [KT], KT[0:64, :], hT.ap()[rq + 256:rq + 320, :])
                Es = E_r()
                Ep = E_r()
                for Et, dl in ((Es, 0), (Ep, 1)):
                    load_E(k, g, Et, Et[:, 0, :], g["vecB"][p], 4 + 4 * p + h, LB, 128 * dl, 1, 128, hrot)
                    for gi in range(1, 4):
                        k.cp("pool", Et, Et[:, gi, :], Et, Et[:, 0, :])
                for r in range(d):
                    for b0 in range(0, nb, G):
                        def blk(b):
                            t0 = r + d * 128 * b
                            return ssl(t0, 128, d)
                        Ss = k.pS()
                        Sp = k.pS()
                        for gi in range(G):
                            b = b0 + gi
                            k.mm(Ss, Ss[:, gi * 128:(gi + 1) * 128], KT, KT[:, blk(b)], QT, QT[:, blk(b)], True, True)
                        for gi in range(G):
                            b = b0 + gi
                            if b >= 1:
                                k.mm(Sp, Sp[:, gi * 128:(gi + 1) * 128], KT, KT[:, blk(b - 1)], QT, QT[:, blk(b)], True, True)
                        c0 = 128 if b0 == 0 else 0
                        Ps0 = p0_r()
                        Ps = p_r()
                        k.act(Ps0, Ps0[:, 0:W], Ss, Ss[:, 0:W], AF.Exp, scale=0.125)
                        k.tt("dve", Ps, Ps[:, 0:W], Ps0, Ps0[:, 0:W], Es, Es[:, 0:G, :], ALU.mult)
                        Pp = None
                        if W > c0:
                            Pp0 = p0_r()
                            Pp = p_r()
                            k.act(Pp0, Pp0[:, c0:W], Sp, Sp[:, c0:W], AF.Exp, scale=0.125)
                            k.tt("pool", Pp, Pp[:, c0:W], Pp0, Pp0[:, c0:W], Ep, Ep[:, c0 // 128:G, :], ALU.mult)

                        def fin(Ps=Ps, Pp=Pp, b0=b0, r=r, d=d, nb=nb, G=G, W=W, Vp=Vp, hp=hp, p=p, R0=R0):
                            O = k.pA()
                            Dn = k.pA()
                            for gi in range(G):
                                b = b0 + gi
                                cs = slice(gi * 128, (gi + 1) * 128)
                                vs = Vp[:, r * nb + b, hp * 128:(hp + 1) * 128]
                                k.mm(O, O[:, cs], Vp, vs, Ps, Ps[:, cs], True, b == 0)
                                if b >= 1:
                                    vp_ = Vp[:, r * nb + b - 1, hp * 128:(hp + 1) * 128]
                                    k.mm(O, O[:, cs], Vp, vp_, Pp, Pp[:, cs], False, True)
                            for gi in range(G):
                                b = b0 + gi
                                cs = slice(gi * 128, (gi + 1) * 128)
                                k.mm(Dn, Dn[:, cs], g["ones"], g["ones"][:], Ps, Ps[:, cs], True, b == 0)
                                if b >= 1:
                                    k.mm(Dn, Dn[:, cs], g["ones"], g["ones"][:], Pp, Pp[:, cs], False, True)
                            t0 = r + d * 128 * b0
                            tsl = ssl(t0, W, d)
                            rows = slice(R0, R0 + 64)
                            if p == 0:
                                k.cp("act", accO, accO[rows, tsl], O, O[rows, 0:W])
                                k.cp("dve", accD, accD[rows, tsl], Dn, Dn[rows, 0:W])
                            else:
                                k.tt("dve", accO, accO[rows, tsl], accO, accO[rows, tsl], O, O[rows, 0:W], ALU.add)
                                k.tt("dve", accD, accD[rows, tsl], accD, accD[rows, tsl], Dn, Dn[rows, 0:W], ALU.add)
                        pend.push(fin)
                pend.flush()
        yst = yst_r()
        for n in range(NTC):
            rd = rd_r()
            k.recip(rd, rd[:], accD, accD[:, n * TC:(n + 1) * TC])
            k.tt("dve", yst, yst[:, n * TC:(n + 1) * TC], accO, accO[:, n * TC:(n + 1) * TC], rd, rd[:], ALU.mult)
        k.st([yst], [yT.res(4 + hp)], yT.ap()[512 + hp * 128:512 + (hp + 1) * 128, :], yst[:])
    k.end()


def out_phase(k, g, yT, nk, w_out, xin, gam, bet, x1):
    k.begin()
    yr = k.rot(2, [128, nk, TC], BF16)
    yv = yT.ap().rearrange("(c p) t -> p c t", p=128)

    def yT_fn(n):
        y = yr()
        k.ld(yT.all(), [y], y[:], yv[:, :, n * TC:(n + 1) * TC])
        return y, (lambda kc, y=y: y[:, kc, :])
    proj_resid_ln(k, g, yT_fn, nk, w_out, xin, gam, bet, x1)
    k.end()


def even_layer(k, g, xin, xout, W, l):
    hT, vA, vB = even_proj(k, g, xin, W["w_in"])
    yT = k.dram([768, S], BF16, "ev_yT")
    attn_A(k, g, hT, vA, yT, W["lam"], W["subln"], l)
    attn_B(k, g, hT, vB, yT)
    x1 = k.dram([DM, S], F32, "x1")
    x1.bf = k.dram([DM, S], BF16)
    out_phase(k, g, yT, 6, W["w_out"], xin, W["ln_g0"], W["ln_b0"], x1)
    ffn_phase(k, g, x1, W["w_up"], W["conv_w"], W["conv_b"], W["w_down"], W["ln_g1"], W["ln_b1"], xout)


def rms_fm(k, g, z, nchunk, gam, out_t, f_r, eps, sq_r):
    nc = k.nc
    sq = sq_r()
    ss = k.pM()
    for m in range(nchunk):
        k.op("act", [z], [sq], lambda m=m: nc.scalar.activation(out=sq[:, m, :], in_=z[:, m, :], func=AF.Square))
    for m in range(nchunk):
        k.mm(ss, ss[:], g["ones"], g["ones"][:], sq, sq[:, m, :], m == 0, m == nchunk - 1)
    sd = f_r()
    k.act(sd, sd[:, 0, :], ss, ss[:], AF.Sqrt, bias=eps[:], scale=1.0 / (128.0 * nchunk), extra=[eps])
    k.recip(sd, sd[:, 1, :], sd, sd[:, 0, :])
    for m in range(nchunk):
        k.stt("dve", out_t, out_t[:, m, :], z, z[:, m, :], gam[:, m:m + 1], sd, sd[:, 1, :], ALU.mult, ALU.mult, extra=[gam])


def odd_proj(k, g, xin, W, cin):
    nc = k.nc
    w_in = W["w_in"]
    hT = k.dram([1024, S], BF16, "od_hT")
    vSW = k.dram([S, 256], BF16, "od_vSW")
    gateT = k.dram([24, S], F32, "od_gate")
    qdT = k.dram([8, 96, S], BF16, "od_qd")
    kdT = k.dram([8, 64, S], BF16, "od_kd")
    krr = k.dram([32, S], BF16, "od_krr")
    vD = k.dram([S, 512], BF16, "od_vD")
    k.begin()
    xb = load_xb(k, xin, k.sb([128, 8, S], BF16))
    wst = k.rot(2, [128, 8, 384], F32)
    wbf = k.rot(2, [128, 8, 128], BF16)
    stg_r = k.rot(1, [128, S], BF16)
    proj_fm_all(k, xb, w_in, [0, 128, 256, 384, 512, 640, 768, 1024], hT, wst, wbf, stg_r)
    wv2 = k.sb([128, 8, 256], BF16)
    load_w_bf16(k, w_in[:, 896:1024], 8, 128, wv2, wv2[:, :, 0:128], wst, "pool")
    load_w_bf16(k, w_in[:, 1152:1280], 8, 128, wv2, wv2[:, :, 128:256], wst, "pool")
    vst = k.rot(3, [128, 512], BF16)
    for b in range(32):
        ps = k.pS()
        for kc in range(8):
            k.mm(ps, ps[:, 0:256], xb.t(b // 4), xb[:, kc, b * 128:(b + 1) * 128], wv2, wv2[:, kc, :], kc == 0, kc == 7)
        s = vst()
        k.cp("act" if b % 2 == 0 else "dve", s, s[:, 0:256], ps, ps[:, 0:256])
        k.st([s], [vSW.res(b)], vSW.ap()[b * 128:(b + 1) * 128, :], s[:, 0:256])
    wg = wbf()
    load_w_bf16(k, w_in[:, 1280:1304], 8, 24, wg, wg[:, :, 0:24], wst, "pool")
    gst_r = k.rot(2, [24, TC], F32)
    for n in range(NTC):
        ps = k.pS()
        for kc in range(8):
            k.mm(ps, ps[0:24, :], wg, wg[:, kc, 0:24], xb.t(n), xb[:, kc, n * TC:(n + 1) * TC], kc == 0, kc == 7)
        gst = gst_r()
        k.act(gst, gst[:], ps, ps[0:24, :], AF.Sigmoid)
        k.st([gst], [gateT.res(n)], gateT.ap()[:, n * TC:(n + 1) * TC], gst[:])
    k.end()
    k.begin()
    xb = load_xb(k, xin, k.sb([128, 8, S], BF16))
    wst = k.rot(1, [128, 8, 384], F32)
    vst = k.rot(3, [128, 512], BF16)
    wcq = k.sb([128, 8, 384], BF16)
    load_w_bf16(k, w_in[:, 1304:1688], 8, 384, wcq, wcq[:], wst, "pool")
    wckv = k.sb([128, 8, 256], BF16)
    load_w_bf16(k, w_in[:, 1688:1944], 8, 256, wckv, wckv[:], wst, "pool")
    wkr = k.sb([128, 8, 96], BF16)
    wkrr = k.sb([128, 8, 96], BF16)
    k.memset("pool", wkr, wkr[:], 0.0)
    k.memset("pool", wkrr, wkrr[:], 0.0)
    s = wst()
    k.ld([], [s], s[:, :, 0:32], w_in[:, 1944:1976].rearrange("(kc p) m -> p kc m", p=128))
    k.cp("pool", wkr, wkr[:, :, 64:96], s, s[:, :, 0:32])
    k.cp("pool", wkrr, wkrr[:, :, 80:96], s, s[:, :, 0:16])
    k.ts("dve", wkrr, wkrr[:, :, 64:80], s, s[:, :, 16:32], -1.0, ALU.mult)
    wq = k.sb([128, 3, 768], BF16)
    wqr = k.sb([128, 3, 8, 96], BF16)
    k.memset("pool", wqr, wqr[:], 0.0)
    wst2 = k.rot(1, [128, 3, 768], F32)
    s = wst2()
    k.ld([], [s], s[:], W["w_uq"].rearrange("(kc p) m -> p kc m", p=128))
    k.cp("pool", wq, wq[:], s, s[:])
    for h in range(8):
        k.cp("pool", wqr, wqr[:, :, h, 80:96], s, s[:, :, h * 96 + 64:h * 96 + 80])
        k.ts("dve", wqr, wqr[:, :, h, 64:80], s, s[:, :, h * 96 + 80:h * 96 + 96], -1.0, ALU.mult)
    wk = k.sb([128, 2, 512], BF16)
    wv = k.sb([128, 2, 512], BF16)
    for wt, nm in ((wk, "w_uk"), (wv, "w_uv")):
        s = wst2()
        k.ld([], [s], s[:, 0:2, 0:512], W[nm].rearrange("(kc p) m -> p kc m", p=128))
        k.cp("pool", wt, wt[:], s, s[:, 0:2, 0:512])
    gq = k.sb([128, 3], F32)
    k.ld([], [gq], gq[:], W["q_norm"].rearrange("(m p) -> p m", p=128), slow=True)
    gkv = k.sb([128, 2], F32)
    k.ld([], [gkv], gkv[:], W["kv_norm"].rearrange("(m p) -> p m", p=128), slow=True)
    cs_r = k.rot(3, [96, 2, TC], F32)
    z_r = k.rot(2, [128, 3, TC], F32)
    f_r = k.rot(2, [128, 2, TC], F32)
    sq_r = k.rot(2, [128, 3, TC], BF16)
    cqn_r = k.rot(3, [128, 3, TC], BF16)
    cn_r = k.rot(3, [128, 2, TC], BF16)
    t_r = k.rot(4, [96, TC], F32)
    qst_r = k.rot(3, [96, TC], BF16)
    kst_r = k.rot(3, [64, TC], BF16)
    eps = g["eps"]
    k.set_pools(3, 3, 2)
    pend = Pend(1)

    def rope_from(psA, psB, dst, dst_ap, cst):
        t = t_r()
        u = t_r()
        k.tt("dve", t, t[64:96, :], psA, psA[64:96, :], cst, cst[64:96, 0, :], ALU.mult)
        k.tt("dve", u, u[64:96, :], psB, psB[64:96, :], cst, cst[64:96, 1, :], ALU.mult)
        k.tt("pool", dst, dst_ap, t, t[64:96, :], u, u[64:96, :], ALU.add)

    def heads_and_v(n, tsl, cqn, cn, cst):
        for h in range(8):
            psA = k.pA()
            psB = k.pA()
            for kc in range(3):
                k.mm(psA, psA[0:96, :], wq, wq[:, kc, h * 96:(h + 1) * 96], cqn, cqn[:, kc, :], kc == 0, kc == 2)
            for kc in range(3):
                k.mm(psB, psB[0:96, :], wqr, wqr[:, kc, h, :], cqn, cqn[:, kc, :], kc == 0, kc == 2)
            qs = qst_r()
            k.cp("act", qs, qs[0:64, :], psA, psA[0:64, :])
            rope_from(psA, psB, qs, qs[64:96, :], cst)
            k.st([qs], [qdT.res((h, n))], qdT.ap()[h][:, tsl], qs[:])
            ps = k.pS()
            for kc in range(2):
                k.mm(ps, ps[0:64, :], wk, wk[:, kc, h * 64:(h + 1) * 64], cn, cn[:, kc, :], kc == 0, kc == 1)
            ks = kst_r()
            k.cp("act", ks, ks[:], ps, ps[0:64, :])
            k.st([ks], [kdT.res((h, n))], kdT.ap()[h][:, tsl], ks[:])
        for bb in range(4):
            ps = k.pS()
            for kc in range(2):
                k.mm(ps, ps[:], cn, cn[:, kc, bb * 128:(bb + 1) * 128], wv, wv[:, kc, :], kc == 0, kc == 1)
            s = vst()
            k.cp("dve", s, s[:], ps, ps[:])
            b = n * 4 + bb
            k.st([s], [vD.res(b)], vD.ap()[b * 128:(b + 1) * 128, :], s[:])

    for n in range(NTC):
        tsl = slice(n * TC, (n + 1) * TC)
        cst = cs_r()
        k.ld([], [cst], cst[64:96, 0, :], cin["c_cos"].ap()[:, tsl])
        k.ld([], [cst], cst[64:96, 1, :], cin["c_sin"].ap()[:, tsl])
        z = z_r()
        for m in range(3):
            ps = k.pS()
            for kc in range(8):
                k.mm(ps, ps[:], wcq, wcq[:, kc, m * 128:(m + 1) * 128], xb.t(n), xb[:, kc, tsl], kc == 0, kc == 7)
            k.cp("act", z, z[:, m, :], ps, ps[:])
        cqn = cqn_r()
        rms_fm(k, g, z, 3, gq, cqn, f_r, eps, sq_r)
        z = z_r()
        for m in range(2):
            ps = k.pS()
            for kc in range(8):
                k.mm(ps, ps[:], wckv, wckv[:, kc, m * 128:(m + 1) * 128], xb.t(n), xb[:, kc, tsl], kc == 0, kc == 7)
            k.cp("act", z, z[:, m, :], ps, ps[:])
        cn = cn_r()
        rms_fm(k, g, z, 2, gkv, cn, f_r, eps, sq_r)
        psA = k.pA()
        psB = k.pA()
        for kc in range(8):
            k.mm(psA, psA[0:96, :], wkr, wkr[:, kc, :], xb.t(n), xb[:, kc, tsl], kc == 0, kc == 7)
        for kc in range(8):
            k.mm(psB, psB[0:96, :], wkrr, wkrr[:, kc, :], xb.t(n), xb[:, kc, tsl], kc == 0, kc == 7)
        qs = qst_r()
        rope_from(psA, psB, qs, qs[64:96, :], cst)
        k.st([qs], [krr.res(n)], krr.ap()[:, tsl], qs[64:96, :])
        pend.push(lambda n=n, tsl=tsl, cqn=cqn, cn=cn, cst=cst: heads_and_v(n, tsl, cqn, cn, cst))
    pend.flush()
    k.end()
    return hT, vSW, gateT, qdT, kdT, krr, vD


def attn_D(k, g, qdT, kdT, krr, vD, yT):
    k.begin()
    k.set_pools(4, 2, 2)
    Em = k.sb([128, 4, TC], BF16)
    k.ld(g["EM"].all(), [Em], Em[:], g["EM"].ap().rearrange("d p q -> p d q"))
    vt_r = k.rot(2, [128, 32, 128], BF16)
    qk_r = k.rot(4, [96, S], BF16)
    p_r = k.rot(7, [128, TC], BF16)
    f_r = k.rot(4, [128, TC], F32)
    od_r = k.rot(4, [128, TC], F32)
    odb_r = k.rot(4, [128, TC], BF16)
    yst_r = k.rot(2, [128, S], BF16)
    vDv = vD.ap().rearrange("(b p) c -> p b c", p=128)
    scale = float(96 ** -0.5)
    pend = Pend(3)
    for hp in range(4):
        yst = yst_r()
        for hh in range(2):
            h = 2 * hp + hh
            rows = slice(hh * 64, hh * 64 + 64)
            rsel = g["rsel0"] if hh == 0 else g["rsel1"]
            vt = vt_r()
            k.ld(vD.all(), [vt], vt[:, :, hh * 64:hh * 64 + 64], vDv[:, :, h * 64:(h + 1) * 64])
            k.memset("pool", vt, vt[:, :, (1 - hh) * 64:(1 - hh) * 64 + 64], 1.0)
            QT = qk_r()
            KT = qk_r()
            k.ld(qdT.all(), [QT], QT[:], qdT.ap()[h])
            k.ld(kdT.all(), [KT], KT[0:64, :], kdT.ap()[h])
            k.ld(krr.all(), [KT], KT[64:96, :], krr.ap())
            for j in range(NTC):
                O = k.pA()
                nblk = 4 * j + 4
                for i in range(nblk):
                    dl = 4 * j - i
                    near = dl <= 0
                    ps = k.pS()
                    k.mm(ps, ps[:], KT, KT[:, i * 128:(i + 1) * 128], QT, QT[:, j * TC:(j + 1) * TC], True, not near)
                    if near:
                        k.mm(ps, ps[:], g["ident"], g["ident"][:], Em, Em[:, dl + 3, :], False, True)
                    P = p_r()
                    k.act(P, P[:], ps, ps[:], AF.Exp, scale=scale)

                    def fin(P=P, i=i, O=O, nblk=nblk, j=j, yst=yst, vt=vt, rows=rows, rsel=rsel):
                        k.mm(O, O[:], vt, vt[:, i, :], P, P[:], i == 0, i == nblk - 1)
                        if i < nblk - 1:
                            return None

                        def evac1():
                            od = od_r()
                            k.cp("act", od, od[:], O, O[:])
                            odb = odb_r()
                            k.cp("pool", odb, odb[:], od, od[:])

                            def evac2():
                                dn = k.pM()
                                k.mm(dn, dn[:], rsel, rsel[:], odb, odb[:], True, True)
                                rd = f_r()
                                k.recip(rd, rd[rows, :], dn, dn[rows, :])
                                k.tt("dve", yst, yst[rows, j * TC:(j + 1) * TC], od, od[rows, :], rd, rd[rows, :], ALU.mult)
                                return None
                            return evac2
                        return evac1
                    pend.push(fin)
            pend.flush()
        k.st([yst], [yT.res(4 + hp)], yT.ap()[512 + hp * 128:512 + (hp + 1) * 128, :], yst[:])
    k.end()


def nsa_compress(k, g, hT, W):
    nc = k.nc
    kcT = k.dram([2, 64, 256], BF16, "od_kc")
    vcd = k.dram([2, 256, 64], BF16, "od_vc")
    k.begin()
    w1s_r = k.rot(1, [64, 32, 256], F32)
    w1_r = k.rot(1, [64, 32, 256], BF16)
    w2_r = k.rot(1, [128, 2, 64], BF16)
    w2s_r = k.rot(1, [128, 2, 64], F32)
    pes = k.sb([64, 32], F32)
    peT = k.sb([64, 32], BF16)
    cb = k.sb([128, 2], F32)
    tt_r = k.rot(2, [64, S], BF16)
    hid_r = k.rot(2, [128, 2, 256], BF16)
    st_r = k.rot(2, [128, 256], BF16)
    for kv in range(2):
        w1s = w1s_r()
        k.ld([], [w1s], w1s[:], W["cmp_w1"][kv].rearrange("(pos d) h -> d pos h", d=64))
        w1 = w1_r()
        k.cp("pool", w1, w1[:], w1s, w1s[:])
        w2s = w2s_r()
        k.ld([], [w2s], w2s[:], W["cmp_w2"][kv].rearrange("(hh p) d -> p hh d", p=128))
        w2 = w2_r()
        k.cp("pool", w2, w2[:], w2s, w2s[:])
        k.ld([], [pes], pes[:], W["cmp_pe"][kv].rearrange("pos d -> d pos"), slow=True)
        k.cp("dve", peT, peT[:], pes, pes[:])
        for hh in range(2):
            ps = k.pM()
            for pos in range(32):
                k.mm(ps, ps[:, 0:1], w1, w1[:, pos, hh * 128:(hh + 1) * 128], peT, peT[:, pos:pos + 1], pos == 0, pos == 31)
            k.cp("dve", cb, cb[:, hh:hh + 1], ps, ps[:, 0:1])
        for gI in range(2):
            T = tt_r()
            r0 = 512 + kv * 128 + gI * 64
            k.ld(hT.all(), [T], T[:], hT.ap()[r0:r0 + 64, :])
            hid = hid_r()
            k.memset("pool", hid, hid[:], 0.0)
            for hh in range(2):
                ps = k.pS()
                for pos in range(32):
                    k.mm(ps, ps[:, 0:255], w1, w1[:, pos, hh * 128:(hh + 1) * 128], T, T[:, ssl(pos, 255, 16)], pos == 0, pos == 31)
                k.act(hid, hid[:, hh, 0:255], ps, ps[:, 0:255], AF.Gelu_apprx_tanh, bias=cb[:, hh:hh + 1], extra=[cb])
            s = st_r()
            if kv == 0:
                ps = k.pM()
                for hh in range(2):
                    k.mm(ps, ps[0:64, 0:256], w2, w2[:, hh, :], hid, hid[:, hh, :], hh == 0, hh == 1)
                k.cp("dve", s, s[0:64, :], ps, ps[0:64, 0:256])
                k.memset("pool", s, s[0:64, 255:256], 0.0)
                k.st([s], [kcT.res(gI)], kcT.ap()[gI], s[0:64, :])
            else:
                for cbk in range(2):
                    ps = k.pM()
                    for hh in range(2):
                        k.mm(ps, ps[:, 0:64], hid, hid[:, hh, cbk * 128:(cbk + 1) * 128], w2, w2[:, hh, :], hh == 0, hh == 1)
                    s = st_r()
                    k.cp("dve", s, s[:, 0:64], ps, ps[:, 0:64])
                    k.st([s], [vcd.res((gI, cbk))], vcd.ap()[gI][cbk * 128:(cbk + 1) * 128, :], s[:, 0:64])
    k.end()
    return kcT, vcd


def nsa_cmp_select(k, g, hT, kcT, vcd, cin):
    nc = k.nc
    ocmp = k.dram([8, 128, S], F32, "od_ocmp")
    negm = k.dram([2, 64, S], BF16, "od_negm")
    k.begin()
    Msel = k.sb([128, 2, 64], F32)
    k.ld([], [Msel], Msel[:], cin["c_Msel"].ap().rearrange("(cb p) j -> p cb j", p=128))
    Fm = k.sb([128, 32, 64], F32)
    k.ld([], [Fm], Fm[:], cin["c_Fm"].ap().rearrange("(t p) j -> p t j", p=128))
    kc_r = k.rot(1, [128, 256], BF16)
    vc_r = k.rot(1, [128, 2, 128], BF16)
    q_r = k.rot(4, [128, S], BF16)
    for _ in range(4):
        t_ = q_r()
        k.memset("pool", t_, t_[64:128, :], 0.0)
    t_ = kc_r()
    k.memset("pool", t_, t_[64:128, :], 0.0)
    hrot = k.rot(2, [128, TC], BF16)
    k.set_pools(3, 3, 2)
    E_r = k.rot(4, [128, TC], BF16)
    p0_r = k.rot(4, [128, TC], BF16)
    p_r = k.rot(8, [128, TC], BF16)
    f_r = k.rot(8, [128, TC], F32)
    oc_r = k.rot(2, [128, TC], F32)
    sc_r = k.rot(4, [128, 64], F32)
    m8_r = k.rot(2, [128, 16], F32)
    nm_r = k.rot(2, [128, 64], BF16)
    nmT_r = k.rot(2, [64, TC], BF16)
    pend = Pend(1)
    pgs_r = k.rot(6, [128, TC], F32)
    for gI in range(2):
        kc = kc_r()
        k.ld(kcT.all(), [kc], kc[0:64, :], kcT.ap()[gI])
        vc = vc_r()
        vv = vcd.ap()[gI].rearrange("(cb p) d -> p cb d", p=128)
        k.ld(vcd.all(), [vc], vc[:, :, 0:64], vv)
        k.ld(vcd.all(), [vc], vc[:, :, 64:128], vv)
        QTs = []
        for hg in range(4):
            QT = q_r()
            head = 4 * gI + hg
            k.ld(hT.all(), [QT], QT[0:64, :], hT.ap()[head * 64:(head + 1) * 64, :])
            QTs.append(QT)
        for j in range(NTC):
            ncb = 1 if j < 4 else 2
            tsl = slice(j * TC, (j + 1) * TC)
            pg = [pgs_r() for _ in range(ncb)]
            for hg in range(4):
                head = 4 * gI + hg
                Ps = []
                for cbk in range(ncb):
                    E = E_r()
                    k.ld(g["EC"].all(), [E], E[:], g["EC"].ap()[head, 2 * j + cbk])
                    ps = k.pS()
                    k.mm(ps, ps[:], kc, kc[:, cbk * 128:(cbk + 1) * 128], QTs[hg], QTs[hg][:, tsl], True, True)
                    P0 = p0_r()
                    k.act(P0, P0[:], ps, ps[:], AF.Exp, scale=0.125)
                    P = p_r()
                    k.tt("dve", P, P[:], P0, P0[:], E, E[:], ALU.mult)
                    Ps.append(P)

                def fin(Ps=Ps, ncb=ncb, hg=hg, head=head, pg=pg, tsl=tsl, j=j):
                    O = k.pA()
                    Dn = k.pA()
                    for cbk in range(ncb):
                        k.mm(O, O[:], vc, vc[:, cbk, :], Ps[cbk], Ps[cbk][:], cbk == 0, cbk == ncb - 1)
                        k.mm(Dn, Dn[:], g["ones"], g["ones"][:], Ps[cbk], Ps[cbk][:], cbk == 0, cbk == ncb - 1)
                    dm = f_r()
                    k.ts("dve", dm, dm[:], Dn, Dn[:], 1e-30, ALU.max)
                    ln_ = f_r()
                    k.act(ln_, ln_[:], dm, dm[:], AF.Ln)
                    rd = f_r()
                    k.act(rd, rd[:], ln_, ln_[:], AF.Exp, scale=-1.0)
                    oc = oc_r()
                    k.tt("dve", oc, oc[:], O, O[:], rd, rd[:], ALU.mult)
                    k.st([oc], [ocmp.res((head, j))], ocmp.ap()[head][:, tsl], oc[:])
                    for cbk in range(ncb):
                        if hg == 0:
                            k.tt("pool", pg[cbk], pg[cbk][:], Ps[cbk], Ps[cbk][:], rd, rd[:], ALU.mult)
                        else:
                            tmp = f_r()
                            k.tt("pool", tmp, tmp[:], Ps[cbk], Ps[cbk][:], rd, rd[:], ALU.mult)
                            k.tt("dve", pg[cbk], pg[cbk][:], pg[cbk], pg[cbk][:], tmp, tmp[:], ALU.add)
                    if hg < 3:
                        return None

                    def topk():
                        nmT = nmT_r()
                        for t in range(4):
                            qt = 4 * j + t
                            ps = k.pM()
                            for cbk in range(ncb):
                                k.mm(ps, ps[:, 0:64], pg[cbk], pg[cbk][:, t * 128:(t + 1) * 128], Msel, Msel[:, cbk, :],
                                     cbk == 0, cbk == ncb - 1)
                            sc = sc_r()
                            k.tt("dve", sc, sc[:], ps, ps[:, 0:64], Fm, Fm[:, qt, :], ALU.add)
                            m8 = m8_r()
                            k.op("dve", [sc], [m8], lambda sc=sc, m8=m8: nc.vector.max(out=m8[:, 0:8], in_=sc[:]))
                            sc2 = sc_r()
                            k.op("dve", [sc, m8], [sc2], lambda sc=sc, m8=m8, sc2=sc2: nc.vector.match_replace(
                                out=sc2[:], in_to_replace=m8[:, 0:8], in_values=sc[:], imm_value=-1e9))
                            k.op("dve", [sc2], [m8], lambda sc2=sc2, m8=m8: nc.vector.max(out=m8[:, 8:16], in_=sc2[:]))
                            nm = nm_r()
                            k.ts("dve", nm, nm[:], sc, sc[:], m8[:, 15:16], ALU.is_lt, -30000.0, ALU.mult, extra=[m8])
                            ps2 = k.pM()
                            k.mm(ps2, ps2[0:64, 0:128], nm, nm[:], g["ident"], g["ident"][:], True, True)
                            k.cp("act", nmT, nmT[:, t * 128:(t + 1) * 128], ps2, ps2[0:64, 0:128])
                        k.st([nmT], [negm.res((gI, j))], negm.ap()[gI][:, tsl], nmT[:])
                        return None
                    return topk
                pend.push(fin)
        pend.flush()
    k.end()
    return ocmp, negm


def nsa_slc_win(k, g, hT, vSW, gateT, ocmp, negm, yT, cin):
    nc = k.nc
    k.begin()
    k.set_pools(4, 2, 2)
    Sel = k.sb([128, 24, 128], BF16)
    gt = k.sb([128, S], BF16)
    k.memset("pool", Sel, Sel[:], 0.0)
    k.memset("pool", gt, gt[:], 0.0)
    gst32 = k.sb([24, S], F32)
    k.ld([], [gst32], gst32[:, 0:3072], cin["c_Sel"].ap().rearrange("r a m -> r (a m)"))
    k.cp("dve", Sel, Sel[0:24, :, :], gst32, gst32[:, 0:3072].rearrange("r (a m) -> r a m", m=128))
    k.ld(gateT.all(), [gst32], gst32[:], gateT.ap())
    k.cp("dve", gt, gt[0:24, :], gst32, gst32[:])
    Es = k.sb([128, 16, TC], BF16)
    Ew = k.sb([128, 8, TC], BF16)
    KE = k.sb([128, 32, 128], BF16)
    k.ld([g["exbf"].res()], [KE], KE[64:128, :, :], g["exbf"].ap())
    KW = k.sb([128, S], BF16)
    k.memset("pool", KW, KW[64:128, :], 0.0)
    vts = [k.sb([128, 32, 128], BF16) for _ in range(4)]
    QN_r = k.rot(2, [128, S], BF16)
    p_r = k.rot(7, [128, TC], BF16)
    f_r = k.rot(8, [128, TC], F32)
    gs_r = k.rot(6, [128, TC], F32)
    od_r = k.rot(3, [128, TC], F32)
    odb_r = k.rot(3, [128, TC], BF16)
    oc_r = k.rot(2, [128, TC], F32)
    yst_r = k.rot(2, [128, S], BF16)
    vv = vSW.ap().rearrange("(b p) c -> p b c", p=128)
    pend = Pend(3)
    for gI in range(2):
        k.ld(hT.all(), [KE], KE[0:64, :, :], hT.ap()[768 + gI * 64:768 + gI * 64 + 64, :].rearrange("d (b t) -> d b t", t=128))
        k.ld(hT.all(), [KW], KW[0:64, :], hT.ap()[896 + gI * 64:896 + gI * 64 + 64, :])
        for bi, c0 in ((0, gI * 64), (1, 128 + gI * 64)):
            for par in range(2):
                vt = vts[bi * 2 + par]
                k.ld(vSW.all(), [vt], vt[:, :, par * 64:par * 64 + 64], vv[:, :, c0:c0 + 64])
                k.memset("pool", vt, vt[:, :, (1 - par) * 64:(1 - par) * 64 + 64], 1.0)
        for hg in range(4):
            head = 4 * gI + hg
            par = head % 2
            rows = slice(par * 64, par * 64 + 64)
            rsel = g["rsel0"] if par == 0 else g["rsel1"]
            if par == 0:
                yst = yst_r()
            QN = QN_r()
            k.ld(hT.all(), [QN], QN[0:64, :], hT.ap()[head * 64:(head + 1) * 64, :])
            k.ld(negm.all(), [QN], QN[64:128, :], negm.ap()[gI])
            k.ld(g["EF"].all(), [Es], Es[:], g["EF"].ap()[head].rearrange("d p q -> p d q"))
            k.ld(g["EW"].all(), [Ew], Ew[:], g["EW"].ap()[head].rearrange("d p q -> p d q"))
            for j in range(NTC):
                tsl = slice(j * TC, (j + 1) * TC)
                gsb = []
                for b3 in range(3):
                    r = (b3 * 2 + gI) * 4 + hg
                    gp = k.pM()
                    k.mm(gp, gp[:], Sel, Sel[:, r, :], gt, gt[:, tsl], True, True)
                    gs_ = gs_r()
                    k.cp("act", gs_, gs_[rows, :], gp, gp[rows, :])
                    gsb.append(gs_)
                res_sw = {}
                for br in (1, 2):
                    O = k.pA()
                    vt = vts[(br - 1) * 2 + par]
                    ilist = list(range(4 * j + 4)) if br == 1 else list(range(max(0, 4 * j - 4), 4 * j + 4))
                    for ii, i in enumerate(ilist):
                        dl = 4 * j - i
                        near = (br == 2) or dl < 13
                        ps = k.pS()
                        if br == 1:
                            k.mm(ps, ps[:], KE, KE[:, i, :], QN, QN[:, tsl], True, not near)
                        else:
                            k.mm(ps, ps[:], KW, KW[:, i * 128:(i + 1) * 128], QN, QN[:, tsl], True, not near)
                        P = p_r()
                        if near:
                            Et = Es if br == 1 else Ew
                            k.mm(ps, ps[:], g["ident"], g["ident"][:], Et, Et[:, dl + 3, :], False, True)
                            k.act(P, P[:], ps, ps[:], AF.Exp, scale=0.125)
                        else:
                            k.act(P, P[:], ps, ps[:], AF.Exp, bias=g["b31"][:, head:head + 1], scale=0.125, extra=[g["b31"]])
                        first = ii == 0
                        last = ii == len(ilist) - 1

                        def fin(P=P, i=i, O=O, first=first, last=last, vt=vt, br=br, res_sw=res_sw, j=j,
                                head=head, rows=rows, yst=yst, tsl=tsl, rsel=rsel, gsb=gsb):
                            k.mm(O, O[:], vt, vt[:, i, :], P, P[:], first, last)
                            if not last:
                                return None

                            def evac1():
                                od = od_r()
                                k.cp("act", od, od[:], O, O[:])
                                odb = odb_r()
                                k.cp("pool", odb, odb[:], od, od[:])

                                def evac2():
                                    dn = k.pM()
                                    k.mm(dn, dn[:], rsel, rsel[:], odb, odb[:], True, True)
                                    rd = f_r()
                                    k.recip(rd, rd[rows, :], dn, dn[rows, :])
                                    fac = f_r()
                                    k.tt("pool", fac, fac[rows, :], rd, rd[rows, :], gsb[br], gsb[br][rows, :], ALU.mult)
                                    on = f_r()
                                    k.tt("dve", on, on[rows, :], od, od[rows, :], fac, fac[rows, :], ALU.mult)
                                    res_sw[br] = on
                                    if br == 1:
                                        return None
                                    oc = oc_r()
                                    k.ld(ocmp.all(), [oc], oc[rows, :], ocmp.ap()[head][rows, tsl])
                                    tcm = f_r()
                                    k.tt("pool", tcm, tcm[rows, :], oc, oc[rows, :], gsb[0], gsb[0][rows, :], ALU.mult)
                                    a2 = f_r()
                                    k.tt("pool", a2, a2[rows, :], tcm, tcm[rows, :], res_sw[1], res_sw[1][rows, :], ALU.add)
                                    k.tt("dve", yst, yst[rows, tsl], a2, a2[rows, :], on, on[rows, :], ALU.add)
                                    return None
                                return evac2
                            return evac1
                        pend.push(fin)
            pend.flush()
            if par == 1:
                k.st([yst], [yT.res(head // 2)], yT.ap()[(head // 2) * 128:(head // 2 + 1) * 128, :], yst[:])
    k.end()


def odd_layer(k, g, xin, xout, W, cin):
    hT, vSW, gateT, qdT, kdT, krr, vD = odd_proj(k, g, xin, W, cin)
    yT = k.dram([1024, S], BF16, "od_yT")
    attn_D(k, g, qdT, kdT, krr, vD, yT)
    kcT, vcd = nsa_compress(k, g, hT, W)
    ocmp, negm = nsa_cmp_select(k, g, hT, kcT, vcd, cin)
    nsa_slc_win(k, g, hT, vSW, gateT, ocmp, negm, yT, cin)
    x1 = k.dram([DM, S], F32, "x1")
    x1.bf = k.dram([DM, S], BF16)
    out_phase(k, g, yT, 8, W["w_out"], xin, W["ln_g0"], W["ln_b0"], x1)
    ffn_phase(k, g, x1, W["w_up"], W["conv_w"], W["conv_b"], W["w_down"], W["ln_g1"], W["ln_b1"], xout)


EV_W = {"w_in": [1024, 3840], "w_out": [768, 1024], "lam": [4, 64], "subln": [128]}
OD_W = {"w_in": [1024, 1976], "w_out": [1024, 1024], "cmp_pe": [2, 32, 64], "cmp_w1": [2, 2048, 256], "cmp_w2": [2, 256, 64],
        "q_norm": [384], "kv_norm": [256], "w_uq": [384, 768], "w_uk": [256, 512], "w_uv": [256, 512]}
FF_W = {"w_up": [1024, 5632], "conv_w": [3, 2816], "conv_b": [2816], "w_down": [2816, 1024], "ln_g0": [1024], "ln_b0": [1024],
        "ln_g1": [1024], "ln_b1": [1024]}
FUSED = True


def layer_shapes(l):
    sh = dict(EV_W if l % 2 == 0 else OD_W)
    sh.update(FF_W)
    return sh


def build_program(layers):
    nc = bass.Bass("TRN2", target_bir_lowering=False)
    k = K(nc)
    cin = {n: nc.dram_tensor(n, s, F32, kind="ExternalInput") for n, s in CONST_SHAPES.items()}
    rel_bias = nc.dram_tensor("rel_bias", [32, 16], F32, kind="ExternalInput")
    xin = DT(nc.dram_tensor("xin", [DM, S], F32, kind="ExternalInput"))
    xo = DT(nc.dram_tensor("xo", [DM, S], F32, kind="ExternalOutput"))
    Ws = {}
    for l in layers:
        Ws[l] = {n: nc.dram_tensor("L%d_%s" % (l, n), s, F32, kind="ExternalInput").ap() for n, s in layer_shapes(l).items()}
    g = setup_globals(k, rel_bias, cin)
    cur = xin
    for li, l in enumerate(layers):
        nxt = xo if li == len(layers) - 1 else k.dram([DM, S], F32)
        if nxt is not xo:
            nxt.bf = k.dram([DM, S], BF16)
        if l % 2 == 0:
            even_layer(k, g, cur, nxt, Ws[l], l)
        else:
            odd_layer(k, g, cur, nxt, Ws[l], cin)
        cur = nxt
    k.c.barrier()
    return nc


def layer_inputs(inp, l):
    i = l // 2
    m = {}
    if l % 2 == 0:
        m.update({"w_in": inp["ev_w_in"][i], "w_out": inp["ev_w_out"][i], "lam": inp["ev_lambda"][i], "subln": inp["ev_subln"][i]})
    else:
        m.update({"w_in": inp["od_w_in"][i], "w_out": inp["od_w_out"][i], "cmp_pe": inp["od_cmp_pe"][i], "cmp_w1": inp["od_cmp_w1"][i],
                  "cmp_w2": inp["od_cmp_w2"][i], "q_norm": inp["od_q_norm"][i], "kv_norm": inp["od_kv_norm"][i],
                  "w_uq": inp["od_w_uq"][i], "w_uk": inp["od_w_uk"][i], "w_uv": inp["od_w_uv"][i]})
    m.update({"w_up": inp["ffn_w_up"][l], "conv_w": inp["ffn_conv_w"][l], "conv_b": inp["ffn_conv_b"][l], "w_down": inp["ffn_w_down"][l],
              "ln_g0": inp["ln_g"][l, 0], "ln_b0": inp["ln_b"][l, 0], "ln_g1": inp["ln_g"][l, 1], "ln_b1": inp["ln_b"][l, 1]})
    return {"L%d_%s" % (l, n): np.ascontiguousarray(np.asarray(v, dtype=np.float32)) for n, v in m.items()}


def kernel(**inputs):
    inp = {n: np.asarray(v) for n, v in inputs.items()}
    x = inp["x"].astype(np.float32, copy=False)
    nb = x.shape[0]
    consts = host_consts()
    xT = [np.ascontiguousarray(x[b].T) for b in range(nb)]
    groups = [[0, 1, 2, 3]] if FUSED else [[0], [1], [2], [3]]
    for layers in groups:
        nc = build_program(layers)
        shared = dict(consts)
        shared["rel_bias"] = np.ascontiguousarray(inp["rel_bias"].astype(np.float32))
        for l in layers:
            shared.update(layer_inputs(inp, l))
        in_maps = []
        for b in range(nb):
            m = dict(shared)
            m["xin"] = xT[b]
            in_maps.append(m)
        res = run_bass_kernel_spmd(nc, in_maps, core_ids=list(range(nb)))
        xT = [np.asarray(res.results[b]["xo"]) for b in range(nb)]
    out = np.stack([xT[b].T for b in range(nb)], axis=0).astype(np.float32)
    return np.ascontiguousarray(out)
```

```python
import math
from contextlib import ExitStack

import numpy as np
import concourse.bass as bass
import concourse.mybir as mybir
from concourse.bass_utils import run_bass_kernel_spmd

F32 = mybir.dt.float32
BF16 = mybir.dt.bfloat16
I32 = mybir.dt.int32
AF = mybir.ActivationFunctionType
ALU = mybir.AluOpType
AX = mybir.AxisListType


class Res:
    __slots__ = ("lw", "rd", "name")

    def __init__(self, name=""):
        self.lw = None
        self.rd = {}
        self.name = name


class Ctx:
    NRING = 12

    def __init__(self, nc):
        self.nc = nc
        self.eng = {"pe": nc.tensor, "act": nc.scalar, "dve": nc.vector, "pool": nc.gpsimd, "sp": nc.sync}
        self.sem = {}
        self.cnt = {}
        for e in ("pe", "act", "dve", "pool"):
            self.sem[e] = nc.alloc_semaphore("s_" + e)
            self.cnt[e] = 0
        self.rings = {}
        for q in ("sp", "pool", "act"):
            keys = []
            for i in range(self.NRING):
                k = "d_%s_%d" % (q, i)
                self.sem[k] = nc.alloc_semaphore(k)
                self.cnt[k] = 0
                keys.append(k)
            self.rings[q] = [keys, 0]
        self.seen = {e: {} for e in self.eng}
        self.n_wait = 0
        self.n_ins = 0

    def _need(self, e, needs, ev):
        k, v, _ = ev
        if self.seen[e].get(k, 0) >= v:
            return
        if needs.get(k, 0) < v:
            needs[k] = v

    def _deps(self, e, reads, writes):
        needs = {}
        for r in reads:
            if r.lw is not None:
                if not (r.lw[2] == e and e == "pe" and r.lw[0] == "pe"):
                    self._need(e, needs, r.lw)
        for w in writes:
            if w.lw is not None and not (w.lw[0] == e):
                self._need(e, needs, w.lw)
            for k, (v, re_) in w.rd.items():
                if k != e:
                    self._need(e, needs, (k, v, re_))
        for k, v in needs.items():
            if not (getattr(self, "skip_pe_waits", False) and e == "pe"):
                self.eng[e].wait_ge(self.sem[k], v)
            self.seen[e][k] = v
            self.n_wait += 1

    def _commit(self, ev, reads, writes):
        k, v, e = ev
        for r in reads:
            r.rd[k] = (v, e)
        for w in writes:
            w.lw = ev
            w.rd = {}

    def op(self, e, reads, writes, fn):
        self._deps(e, reads, writes)
        ins = fn()
        self.cnt[e] += 1
        ins.then_inc(self.sem[e], 1)
        self.n_ins += 1
        self._commit((e, self.cnt[e], e), reads, writes)
        return ins

    def dma(self, q, reads, writes, out, in_, **kw):
        keys, idx = self.rings[q]
        k = keys[idx % self.NRING]
        self.rings[q][1] = idx + 1
        if self.cnt[k] > 0 and self.seen[q].get(k, 0) < self.cnt[k]:
            self.eng[q].wait_ge(self.sem[k], self.cnt[k])
            self.seen[q][k] = self.cnt[k]
        self._deps(q, reads, writes)
        ins = self.eng[q].dma_start(out=out, in_=in_, **kw)
        self.cnt[k] += 16
        ins.then_inc(self.sem[k], 16)
        self.n_ins += 1
        self._commit((k, self.cnt[k], "dma"), reads, writes)
        return ins

    def barrier(self):
        for e in self.eng:
            for k, v in self.cnt.items():
                if v > 0 and k != e and self.seen[e].get(k, 0) < v:
                    self.eng[e].wait_ge(self.sem[k], v)
                    self.seen[e][k] = v


S = 4096
DM = 1024
NTC = 8
TC = 512
PADL = 4112
LF = PADL + 4096
PADB = 127
LB = 384
ALPHA = (2 * 4) ** 0.25
DFF = 2816


class TT:
    __slots__ = ("h", "r")

    def __init__(self, h):
        self.h = h
        self.r = Res()

    def __getitem__(self, idx):
        return self.h[idx]


class DT:
    def __init__(self, h):
        self.h = h
        self.rs = {}

    def res(self, key=0):
        if key not in self.rs:
            self.rs[key] = Res()
        return self.rs[key]

    def all(self):
        return list(self.rs.values())

    def ap(self):
        return self.h.ap()


def _r(x):
    return x.r if isinstance(x, TT) else x


class XB:
    def __init__(self, tt):
        self.h = tt.h
        self.parts = [TT(tt.h) for _ in range(4)]

    def __getitem__(self, idx):
        return self.h[idx]

    def t(self, n):
        return self.parts[n // 2]

    def all(self):
        return list(self.parts)


class K:
    def __init__(self, nc):
        self.nc = nc
        self.c = Ctx(nc)
        self.es = ExitStack()
        self.ph = None
        self.uid = 0
        self.ps_all = [self._ps() for _ in range(8)]
        self.set_pools(3, 4, 1)

    def _ps(self):
        self.uid += 1
        return TT(self.es.enter_context(self.nc.psum_tensor("ps%d" % self.uid, [128, 512], F32)))

    def set_pools(self, ns, na, nm):
        assert ns + na + nm == 8
        self.ps_s = self.ps_all[0:ns]
        self.ps_a = self.ps_all[ns:ns + na]
        self.ps_m = self.ps_all[ns + na:]
        self.i_s = self.i_a = self.i_m = 0

    def pS(self):
        self.i_s += 1
        return self.ps_s[self.i_s % len(self.ps_s)]

    def pA(self):
        self.i_a += 1
        return self.ps_a[self.i_a % len(self.ps_a)]

    def pM(self):
        self.i_m += 1
        return self.ps_m[self.i_m % len(self.ps_m)]

    def begin(self):
        self.ph = ExitStack()

    def end(self):
        self.c.barrier()
        self.ph.close()
        self.ph = None
        self.set_pools(3, 4, 1)

    def sb(self, shape, dtype, glob=False, mid=None):
        self.uid += 1
        st = mid if mid is not None else (self.es if glob else self.ph)
        return TT(st.enter_context(self.nc.sbuf_tensor("t%d" % self.uid, list(shape), dtype)))

    def rot(self, n, shape, dtype):
        bufs = [self.sb(shape, dtype) for _ in range(n)]
        st = [0]

        def nxt():
            st[0] += 1
            return bufs[st[0] % n]
        return nxt

    def dram(self, shape, dtype, name=None):
        self.uid += 1
        if name is not None and name in getattr(self, "dbg", ()):
            return DT(self.nc.dram_tensor("dbg_" + name, list(shape), dtype, kind="ExternalOutput"))
        return DT(self.nc.dram_tensor("scr%d" % self.uid, list(shape), dtype))

    def op(self, e, reads, writes, fn):
        return self.c.op(e, [_r(x) for x in reads], [_r(x) for x in writes], fn)

    def ld(self, reads, writes, out, in_, q="sp", slow=False):
        kw = {"allow_slow_non_contiguous": True} if slow else {}
        return self.c.dma(q, [_r(x) for x in reads], [_r(x) for x in writes], out, in_, **kw)

    def st(self, reads, writes, out, in_, q="pool"):
        return self.c.dma(q, [_r(x) for x in reads], [_r(x) for x in writes], out, in_)

    def mm(self, out_t, out_ap, lhs_t, lhs_ap, rhs_t, rhs_ap, start, stop):
        nc = self.nc
        reads = (list(lhs_t) if isinstance(lhs_t, (list, tuple)) else [lhs_t]) + \
                (list(rhs_t) if isinstance(rhs_t, (list, tuple)) else [rhs_t])
        return self.op("pe", reads, [out_t],
                       lambda: nc.tensor.matmul(out_ap, lhsT=lhs_ap, rhs=rhs_ap, start=start, stop=stop))

    def act(self, out_t, out_ap, in_t, in_ap, func, bias=None, scale=1.0, extra=()):
        nc = self.nc
        kw = {}
        if bias is not None:
            kw["bias"] = bias
        return self.op("act", [in_t] + list(extra), [out_t],
                       lambda: nc.scalar.activation(out=out_ap, in_=in_ap, func=func, scale=scale, **kw))

    def tt(self, e, out_t, out_ap, a_t, a_ap, b_t, b_ap, op):
        eng = self.nc.vector if e == "dve" else self.nc.gpsimd
        return self.op(e, [a_t, b_t], [out_t], lambda: eng.tensor_tensor(out=out_ap, in0=a_ap, in1=b_ap, op=op))

    def ts(self, e, out_t, out_ap, a_t, a_ap, s1, op0, s2=None, op1=None, extra=()):
        eng = self.nc.vector if e == "dve" else self.nc.gpsimd
        if op1 is None:
            return self.op(e, [a_t] + list(extra), [out_t],
                           lambda: eng.tensor_scalar(out=out_ap, in0=a_ap, scalar1=s1, scalar2=None, op0=op0))
        return self.op(e, [a_t] + list(extra), [out_t],
                       lambda: eng.tensor_scalar(out=out_ap, in0=a_ap, scalar1=s1, scalar2=s2, op0=op0, op1=op1))

    def stt(self, e, out_t, out_ap, a_t, a_ap, scalar, b_t, b_ap, op0, op1, extra=()):
        eng = self.nc.vector if e == "dve" else self.nc.gpsimd
        return self.op(e, [a_t, b_t] + list(extra), [out_t],
                       lambda: eng.scalar_tensor_tensor(out=out_ap, in0=a_ap, scalar=scalar, in1=b_ap, op0=op0, op1=op1))

    def cp(self, e, out_t, out_ap, in_t, in_ap):
        nc = self.nc
        if e == "act":
            return self.op("act", [in_t], [out_t], lambda: nc.scalar.copy(out=out_ap, in_=in_ap))
        eng = nc.vector if e == "dve" else nc.gpsimd
        return self.op(e, [in_t], [out_t], lambda: eng.tensor_copy(out=out_ap, in_=in_ap))

    def memset(self, e, t, ap, val):
        eng = self.nc.vector if e == "dve" else self.nc.gpsimd
        return self.op(e, [], [t], lambda: eng.memset(ap, val))

    def recip(self, out_t, out_ap, in_t, in_ap):
        nc = self.nc
        return self.op("dve", [in_t], [out_t], lambda: nc.vector.reciprocal(out=out_ap, in_=in_ap))


def t5_bucket_np(dist):
    dist = np.asarray(dist, dtype=np.int64)
    n = np.maximum(dist, 0)
    nf = np.maximum(n, 1).astype(np.float32)
    large = 16 + (np.log(nf / np.float32(16)) / np.float32(math.log(2048 / 16)) * np.float32(16)).astype(np.int32)
    return np.where(n < 16, n, np.minimum(large, 31))


def host_consts():
    cs = {}
    oh = np.zeros((32, 4096), np.float32)
    oh[t5_bucket_np(np.arange(4096)), np.arange(4096)] = 1.0
    cs["c_ohF"] = oh
    ohb = np.zeros((3, 32, 129), np.float32)
    for p, d in enumerate((1, 4, 16)):
        idx = np.arange(129)
        ohb[p, t5_bucket_np(idx * d), idx] = 1.0
    cs["c_ohB"] = ohb
    cs["c_ident"] = np.eye(128, dtype=np.float32)
    cs["c_J"] = np.eye(128, dtype=np.float32)[::-1].copy()
    M = np.zeros((256, 64), np.float32)
    for j in range(64):
        for cc, w in ((4 * j - 1, .5), (4 * j, 1.), (4 * j + 1, 1.), (4 * j + 2, 1.), (4 * j + 3, .5)):
            if 0 <= cc < 255:
                M[cc, j] += w
    cs["c_Msel"] = M
    q = np.arange(4096)[:, None]
    jb = np.arange(64)[None, :]
    qb = q // 64
    fm = np.where(jb > qb, -1e4, 0.0) + np.where((jb == 0) | (jb == qb) | (jb == qb - 1), 1e4, 0.0)
    cs["c_Fm"] = fm.astype(np.float32)
    ex = np.zeros((64, 32, 128), np.float32)
    for i in range(32):
        for k in range(128):
            ex[2 * i + k // 64, i, k] = 1.0
    cs["c_Ex"] = ex
    sel = np.zeros((24, 24, 128), np.float32)
    for r in range(24):
        sel[r, r, :] = 1.0
    cs["c_Sel"] = sel
    half = 16
    inv = (np.float32(10000.0) ** (-np.arange(half, dtype=np.float32) / np.float32(half))).astype(np.float32)
    ang = np.arange(4096, dtype=np.float32)[None, :] * inv[:, None]
    cs["c_cos"] = np.concatenate([np.cos(ang), np.cos(ang)], 0).astype(np.float32)
    cs["c_sin"] = np.concatenate([np.sin(ang), np.sin(ang)], 0).astype(np.float32)
    return cs


CONST_SHAPES = {"c_ohF": [32, 4096], "c_ohB": [3, 32, 129], "c_ident": [128, 128], "c_J": [128, 128],
                "c_Msel": [256, 64], "c_Fm": [4096, 64], "c_Ex": [64, 32, 128], "c_Sel": [24, 24, 128],
                "c_cos": [32, 4096], "c_sin": [32, 4096]}


def setup_globals(k, rel_bias, cin):
    nc = k.nc
    g = {}
    for nm in ("ident", "J", "ones", "zeros"):
        g[nm] = k.sb([128, 128], BF16, glob=True)
    g["ones32"] = k.sb([128, 128], F32, glob=True)
    g["rsel0"] = k.sb([128, 128], F32, glob=True)
    g["rsel1"] = k.sb([128, 128], F32, glob=True)
    g["eps"] = k.sb([128, 1], F32, glob=True)
    g["b31"] = k.sb([128, 16], F32, glob=True)
    k.begin()
    st32r = k.rot(2, [128, 128], F32)
    for nm in ("ident", "J"):
        st32 = st32r()
        k.ld([], [st32], st32[:], cin["c_" + nm].ap())
        k.cp("dve", g[nm], g[nm][:], st32, st32[:])
    k.memset("pool", g["ones"], g["ones"][:], 1.0)
    k.memset("pool", g["ones32"], g["ones32"][:], 1.0)
    k.memset("pool", g["zeros"], g["zeros"][:], 0.0)
    k.memset("pool", g["rsel0"], g["rsel0"][:], 0.0)
    k.memset("pool", g["rsel1"], g["rsel1"][:], 0.0)
    k.memset("pool", g["rsel0"], g["rsel0"][64:65, :], 1.0)
    k.memset("pool", g["rsel1"], g["rsel1"][0:1, :], 1.0)
    k.memset("pool", g["eps"], g["eps"][:], 1e-5)
    k.ld([], [g["b31"]], g["b31"][:], bass.AP(rel_bias, 31 * 16, [[0, 128], [1, 16]]))
    tbl = k.sb([32, 16], F32)
    k.ld([], [tbl], tbl[:], rel_bias.ap())
    oh = k.sb([32, 4096], F32)
    k.ld([], [oh], oh[:], cin["c_ohF"].ap())
    stg = k.sb([16, LF], BF16)
    k.memset("pool", stg, stg[:], 0.0)
    stl = k.sb([16, LF], BF16)
    k.memset("pool", stl, stl[:], -30000.0)
    for n in range(8):
        ps = k.pM()
        k.mm(ps, ps[0:16, :], tbl, tbl[:], oh, oh[:, n * 512:(n + 1) * 512], True, True)
        k.act(stg, stg[:, PADL + n * 512:PADL + (n + 1) * 512], ps, ps[0:16, :], AF.Exp)
        k.act(stl, stl[:, PADL + n * 512:PADL + (n + 1) * 512], ps, ps[0:16, :], AF.Copy, scale=8.0)
    vecF = k.dram([16, LF], BF16)
    k.st([stg], [vecF.res()], vecF.ap(), stg[:])
    vecFl = k.dram([16, LF], BF16)
    k.st([stl], [vecFl.res()], vecFl.ap(), stl[:])
    stw = k.sb([16, LF], BF16)
    k.memset("pool", stw, stw[:], -30000.0)
    k.cp("dve", stw, stw[:, PADL:PADL + 512], stl, stl[:, PADL:PADL + 512])
    vecW = k.dram([16, LF], BF16)
    k.st([stw], [vecW.res()], vecW.ap(), stw[:])
    stm = k.sb([16, LF], BF16)
    k.memset("pool", stm, stm[:], -30000.0)
    k.memset("pool", stm, stm[:, PADL:], 0.0)
    vecM = k.dram([16, LF], BF16)
    k.st([stm], [vecM.res()], vecM.ap(), stm[:])
    g["vecF"], g["vecW"], g["vecM"] = vecF, vecW, vecM
    vecB = []
    for p in range(3):
        ohb = k.sb([32, 129], F32)
        k.ld([], [ohb], ohb[:], cin["c_ohB"].ap()[p])
        sb_ = k.sb([16, LB], BF16)
        k.memset("pool", sb_, sb_[:], 0.0)
        ps = k.pM()
        k.mm(ps, ps[0:16, 0:129], tbl, tbl[:], ohb, ohb[:], True, True)
        k.act(sb_, sb_[:, PADB:PADB + 129], ps, ps[0:16, 0:129], AF.Exp)
        vb = k.dram([16, LB], BF16)
        k.st([sb_], [vb.res()], vb.ap(), sb_[:])
        vecB.append(vb)
    g["vecB"] = vecB
    EF = k.dram([8, 16, 128, TC], BF16)
    EW = k.dram([8, 8, 128, TC], BF16)
    EC = k.dram([8, 16, 128, TC], BF16)
    EM = k.dram([4, 128, TC], BF16)
    hrot = k.rot(4, [128, TC], BF16)
    est = k.rot(4, [128, TC], BF16)
    jobs = []
    for h in range(8):
        for dl in range(-3, 13):
            jobs.append((vecFl, h, toep_off(dl), 1, EF, EF.ap()[h, dl + 3]))
        for dl in range(-3, 5):
            jobs.append((vecW, h, toep_off(dl), 1, EW, EW.ap()[h, dl + 3]))
        for j in range(8):
            for cbk in range(2):
                jobs.append((vecF, h, PADL - 31 + 512 * j - 2048 * cbk - 2032, 16, EC, EC.ap()[h, 2 * j + cbk]))
    for dl in range(-3, 1):
        jobs.append((vecM, 0, toep_off(dl), 1, EM, EM.ap()[dl + 3]))
    for ji, (vec, row, off, pstep, dst, dap) in enumerate(jobs):
        H = hrot()
        k.ld([vec.res()], [H], H[:], bass.AP(vec.h, row * LF + off, [[pstep, 128], [1, TC]]))
        ps = k.pA()
        k.mm(ps, ps[:], g["J"], g["J"][:], H, H[:], True, True)
        e_ = est()
        k.cp("act" if ji % 2 else "dve", e_, e_[:], ps, ps[:])
        k.st([e_], [dst.res(ji)], dap, e_[:])
    g["EF"], g["EW"], g["EC"], g["EM"] = EF, EW, EC, EM
    exs = k.sb([64, 32, 128], F32)
    exb = k.sb([64, 32, 128], BF16)
    k.ld([], [exs], exs[:], cin["c_Ex"].ap())
    k.cp("pool", exb, exb[:], exs, exs[:])
    exbf = k.dram([64, 32, 128], BF16)
    k.st([exb], [exbf.res()], exbf.ap(), exb[:])
    g["exbf"] = exbf
    k.end()
    return g


def load_E(k, g, dst, dst_ap, vec, row, L, off, pstep, W, hrot):
    H = hrot()
    k.ld([vec.res()], [H], H[:, 0:W], bass.AP(vec.h, row * L + off, [[pstep, 128], [1, W]]))
    ps = k.pM()
    k.mm(ps, ps[:, 0:W], g["J"], g["J"][:], H, H[:, 0:W], True, True)
    k.cp("act", dst, dst_ap, ps, ps[:, 0:W])


def toep_off(delta):
    return PADL + 128 * delta - 127


def load_xb(k, xin, xb_tt):
    xb = XB(xb_tt)
    if getattr(xin, "bf", None) is not None:
        xv = xin.bf.ap().rearrange("(kc p) t -> p kc t", p=128)
        for n in range(4):
            k.ld(xin.bf.all(), [xb.parts[n]], xb[:, :, n * 1024:(n + 1) * 1024], xv[:, :, n * 1024:(n + 1) * 1024])
        return xb
    stg = k.rot(2, [128, 1024], F32)
    xv = xin.ap().rearrange("(kc p) t -> p kc t", p=128)
    i = 0
    for q4 in range(4):
        for kc in range(8):
            s = stg()
            k.ld([xin.res(kc)], [s], s[:], xv[:, kc, q4 * 1024:(q4 + 1) * 1024])
            k.cp("dve" if i % 2 == 0 else "act", xb.parts[q4], xb[:, kc, q4 * 1024:(q4 + 1) * 1024], s, s[:])
            i += 1
    return xb


def load_w_bf16(k, w_ap, nk, ncols, dst, dst_ap, stg_rot, eng="pool"):
    s = stg_rot()
    k.ld([], [s], s[:, 0:nk, 0:ncols], w_ap.rearrange("(kc p) m -> p kc m", p=128))
    k.cp(eng, dst, dst_ap, s, s[:, 0:nk, 0:ncols])


def ln_block(k, g, z, nchunk, gam, bet, dst_fn):
    nc = k.nc
    zb = k.ln_zb()
    sq = k.ln_sq()
    s1 = k.pA()
    s2 = k.pA()
    for m in range(nchunk):
        k.cp("act", zb, zb[:, m, :], z, z[:, m, :])
        k.op("act", [z], [sq], lambda m=m: nc.scalar.activation(out=sq[:, m, :], in_=z[:, m, :], func=AF.Square))
    for m in range(nchunk):
        k.mm(s1, s1[:], g["ones"], g["ones"][:], zb, zb[:, m, :], m == 0, m == nchunk - 1)
    for m in range(nchunk):
        k.mm(s2, s2[:], g["ones"], g["ones"][:], sq, sq[:, m, :], m == 0, m == nchunk - 1)
    nf = float(nchunk * 128)
    mean = k.ln_s()
    k.op("act", [s1], [mean], lambda: nc.scalar.mul(out=mean[:], in_=s1[:], mul=1.0 / nf))
    msq = k.ln_s()
    k.tt("dve", msq, msq[:], mean, mean[:], mean, mean[:], ALU.mult)
    var = k.ln_s()
    k.stt("dve", var, var[:], s2, s2[:], 1.0 / nf, msq, msq[:], ALU.mult, ALU.subtract)
    sd = k.ln_s()
    k.act(sd, sd[:], var, var[:], AF.Sqrt, bias=g["eps"][:], extra=[g["eps"]])
    rstd = k.ln_s()
    k.recip(rstd, rstd[:], sd, sd[:])
    if hasattr(k, "ln_dump"):
        for nm_, t_ in (("mean", mean), ("msq", msq), ("var", var), ("sd", sd), ("rstd", rstd)):
            k.ln_dump(nm_, t_)
    for m in range(nchunk):
        t = k.ln_t()
        k.tt("dve", t, t[:], z, z[:, m, :], mean, mean[:], ALU.subtract)
        t2 = k.ln_t()
        k.tt("pool", t2, t2[:], t, t[:], rstd, rstd[:], ALU.mult)
        o, oap = dst_fn(m)
        k.op("act", [t2, gam, bet], [o],
             lambda m=m, t2=t2, oap=oap: nc.scalar.activation(out=oap, in_=t2[:], func=AF.Identity,
                                                              scale=gam[:, m:m + 1], bias=bet[:, m:m + 1]))


def proj_resid_ln(k, g, yT_fn, nk, w_ap, xres, gam_ap, bet_ap, xout, w_pre=None):
    nc = k.nc
    if w_pre is not None:
        w = w_pre
    else:
        w = k.sb([128, nk, DM], BF16)
        wst = k.rot(1, [128, nk, 128], F32)
        for cc in range(8):
            load_w_bf16(k, w_ap[:, cc * 128:(cc + 1) * 128], nk, 128, w, w[:, :, cc * 128:(cc + 1) * 128], wst,
                        "pool" if cc % 2 else "dve")
    gam = k.sb([128, 8], F32)
    bet = k.sb([128, 8], F32)
    k.ld([], [gam], gam[:], gam_ap.rearrange("(m p) -> p m", p=128), slow=True)
    k.ld([], [bet], bet[:], bet_ap.rearrange("(m p) -> p m", p=128), slow=True)
    xr = k.rot(3, [128, 8, TC], F32)
    k.ln_sq = k.rot(1, [128, 8, TC], BF16)
    k.ln_zb = k.rot(1, [128, 8, TC], BF16)
    k.ln_t = k.rot(4, [128, TC], F32)
    k.ln_s = k.rot(5, [128, TC], F32)
    ost = k.rot(1, [128, 8, TC], F32)
    obt = k.rot(1, [128, 8, TC], BF16)
    xv = xres.ap().rearrange("(m p) t -> p m t", p=128)
    ov = xout.ap().rearrange("(m p) t -> p m t", p=128)
    obv = xout.bf.ap().rearrange("(m p) t -> p m t", p=128) if getattr(xout, "bf", None) is not None else None
    pend = Pend(1)
    for n in range(NTC):
        yt, yap = yT_fn(n)
        xt = xr()
        k.ld(xres.all(), [xt], xt[:], xv[:, :, n * TC:(n + 1) * TC])
        z = xt
        for m in range(8):
            ps = k.pS()
            for kc in range(nk):
                k.mm(ps, ps[:], w, w[:, kc, m * 128:(m + 1) * 128], yt, yap(kc), kc == 0, kc == nk - 1)
            k.stt("dve", z, z[:, m, :], xt, xt[:, m, :], ALPHA, ps, ps[:], ALU.mult, ALU.add)

        def fin(z=z, n=n):
            o = ost()
            ln_block(k, g, z, 8, gam, bet, lambda m, o=o: (o, o[:, m, :]))
            k.st([o], [xout.res(kc) for kc in range(8)], ov[:, :, n * TC:(n + 1) * TC], o[:])
            if obv is not None:
                ob = obt()
                k.cp("pool", ob, ob[:], o, o[:])
                k.st([ob], [xout.bf.res(n)], obv[:, :, n * TC:(n + 1) * TC], ob[:])
            return None
        pend.push(fin)
    pend.flush()


def ffn_phase(k, g, x1, w_up, conv_w, conv_b, w_down, ln_g, ln_b, xout):
    nc = k.nc
    hT = k.dram([DFF, S], BF16, "hT")
    mid = ExitStack()
    wdn = k.sb([128, 22, DM], BF16, mid=mid)
    k.begin()
    wdst = k.rot(1, [128, 22, 128], F32)
    xb = load_xb(k, x1, k.sb([128, 8, S], BF16))
    cw = k.sb([128, 3, 22], F32)
    k.ld([], [cw], cw[:], conv_w.rearrange("j (c p) -> p j c", p=128), slow=True)
    cb = k.sb([128, 22], F32)
    k.ld([], [cb], cb[:], conv_b.rearrange("(c p) -> p c", p=128), slow=True)
    wst = k.rot(3, [128, 8, 128], F32)
    wa_r = k.rot(2, [128, 8, 128], BF16)
    wg_r = k.rot(2, [128, 8, 128], BF16)
    gb_r = k.rot(2, [128, S + 2], F32)
    for _ in range(2):
        gb = gb_r()
        k.memset("pool", gb, gb[:, 0:2], 0.0)
    t_r = k.rot(5, [128, TC], F32)
    hst_r = k.rot(2, [128, S], BF16)

    def wload(cc):
        wa = wa_r()
        wg = wg_r()
        load_w_bf16(k, w_up[:, cc * 128:(cc + 1) * 128], 8, 128, wa, wa[:], wst, "pool")
        load_w_bf16(k, w_up[:, DFF + cc * 128:DFF + (cc + 1) * 128], 8, 128, wg, wg[:], wst, "pool")
        return wa, wg
    wcur = wload(0)
    for cc in range(22):
        wa, wg = wcur
        if cc + 1 < 22:
            wcur = wload(cc + 1)
        hst = hst_r()
        gb = gb_r()
        if cc % 2 == 1 and cc // 2 < 8:
            c8 = cc // 2
            load_w_bf16(k, w_down[:, c8 * 128:(c8 + 1) * 128], 22, 128, wdn, wdn[:, :, c8 * 128:(c8 + 1) * 128], wdst, "act")
        for n in range(NTC):
            pa = k.pS()
            pg = k.pA()
            for kc in range(8):
                k.mm(pg, pg[:], wg, wg[:, kc, :], xb.t(n), xb[:, kc, n * TC:(n + 1) * TC], kc == 0, kc == 7)
            for kc in range(8):
                k.mm(pa, pa[:], wa, wa[:, kc, :], xb.t(n), xb[:, kc, n * TC:(n + 1) * TC], kc == 0, kc == 7)
            o = 2 + n * TC
            k.cp("act", gb, gb[:, o:o + TC], pg, pg[:])
            t1 = t_r()
            k.op("act", [pg, cw, cb], [t1],
                 lambda t1=t1, pg=pg, cc=cc: nc.scalar.activation(out=t1[:], in_=pg[:], func=AF.Identity,
                                                                 scale=cw[:, 2, cc:cc + 1], bias=cb[:, cc:cc + 1]))
            t2 = t_r()
            k.stt("dve", t2, t2[:], gb, gb[:, o - 1:o - 1 + TC], cw[:, 1, cc:cc + 1], t1, t1[:], ALU.mult, ALU.add, extra=[cw])
            t3 = t_r()
            k.stt("dve", t3, t3[:], gb, gb[:, o - 2:o - 2 + TC], cw[:, 0, cc:cc + 1], t2, t2[:], ALU.mult, ALU.add, extra=[cw])
            t4 = t_r()
            k.act(t4, t4[:], t3, t3[:], AF.Gelu_apprx_tanh)
            k.tt("dve", hst, hst[:, n * TC:(n + 1) * TC], t4, t4[:], pa, pa[:], ALU.mult)
        k.st([hst], [hT.res(cc)], hT.ap()[cc * 128:(cc + 1) * 128, :], hst[:])
    k.end()
    k.begin()
    hr = k.rot(2, [128, 22, TC], BF16)
    hv = hT.ap().rearrange("(c p) t -> p c t", p=128)

    def yT_fn(n):
        h = hr()
        k.ld(hT.all(), [h], h[:], hv[:, :, n * TC:(n + 1) * TC])
        return h, (lambda kc, h=h: h[:, kc, :])
    proj_resid_ln(k, g, yT_fn, 22, w_down, x1, ln_g, ln_b, xout, w_pre=wdn)
    k.end()
    mid.close()


def ssl(t0, n, d):
    return slice(t0, t0 + (n - 1) * d + 1, d)


class Pend:
    def __init__(self, depth=1):
        self.q = []
        self.depth = depth

    def _run(self, fn):
        r = fn()
        if callable(r):
            self.q.append(r)

    def push(self, fn):
        self.q.append(fn)
        while len(self.q) > self.depth:
            self._run(self.q.pop(0))

    def flush(self):
        while self.q:
            self._run(self.q.pop(0))


def proj_fm_load(k, w_in, col0, ncols, wst, wbf):
    w = wbf()
    load_w_bf16(k, w_in[:, col0:col0 + ncols], 8, ncols, w, w[:, :, 0:ncols], wst, "pool")
    return w


def proj_fm_compute(k, xb, w, ncols, out_dt, row0, stg_r, key):
    stg = stg_r()
    for n in range(NTC):
        ps = k.pS()
        for kc in range(8):
            k.mm(ps, ps[0:ncols, :], w, w[:, kc, 0:ncols], xb.t(n), xb[:, kc, n * TC:(n + 1) * TC], kc == 0, kc == 7)
        k.cp("act" if n % 2 == 0 else "dve", stg, stg[0:ncols, n * TC:(n + 1) * TC], ps, ps[0:ncols, :])
    k.st([stg], [out_dt.res(key)], out_dt.ap()[row0:row0 + ncols, :], stg[0:ncols, :])


def proj_fm_all(k, xb, w_in, cols, out_dt, wst, wbf, stg_r):
    w = proj_fm_load(k, w_in, cols[0], 128, wst, wbf)
    for ci, c0 in enumerate(cols):
        wn = proj_fm_load(k, w_in, cols[ci + 1], 128, wst, wbf) if ci + 1 < len(cols) else None
        proj_fm_compute(k, xb, w, 128, out_dt, ci * 128, stg_r, ci)
        w = wn


def even_proj(k, g, xin, w_in):
    hT = k.dram([2560, S], BF16, "ev_hT")
    vA = k.dram([S, 512], BF16, "ev_vA")
    vB = k.dram([3, 32, 128, 256], BF16, "ev_vB")
    k.begin()
    xb = load_xb(k, xin, k.sb([128, 8, S], BF16))
    wst = k.rot(2, [128, 8, 512], F32)
    wbf = k.rot(2, [128, 8, 128], BF16)
    stg_r = k.rot(2, [128, S], BF16)
    cols = list(range(0, 1024, 128))
    for p in range(3):
        base = 1536 + p * 768
        cols += [base, base + 128, base + 256, base + 384]
    proj_fm_all(k, xb, w_in, cols, hT, wst, wbf, stg_r)
    wv = k.sb([128, 8, 512], BF16)
    load_w_bf16(k, w_in[:, 1024:1536], 8, 512, wv, wv[:], wst, "pool")
    vst = k.rot(3, [128, 512], BF16)
    for b in range(32):
        ps = k.pS()
        for kc in range(8):
            k.mm(ps, ps[:], xb.t(b // 4), xb[:, kc, b * 128:(b + 1) * 128], wv, wv[:, kc, :], kc == 0, kc == 7)
        s = vst()
        k.cp("act" if b % 2 == 0 else "dve", s, s[:], ps, ps[:])
        k.st([s], [vA.res(b)], vA.ap()[b * 128:(b + 1) * 128, :], s[:])
    for p, d in enumerate((1, 4, 16)):
        base = 1536 + p * 768 + 512
        load_w_bf16(k, w_in[:, base:base + 256], 8, 256, wv, wv[:, :, 0:256], wst, "pool")
        nb = 32 // d
        for r in range(d):
            for b in range(nb):
                t0 = r + d * 128 * b
                ps = k.pS()
                for kc in range(8):
                    k.mm(ps, ps[:, 0:256], xb.all(), xb[:, kc, ssl(t0, 128, d)], wv, wv[:, kc, 0:256], kc == 0, kc == 7)
                s = vst()
                k.cp("act" if b % 2 == 0 else "dve", s, s[:, 0:256], ps, ps[:, 0:256])
                k.st([s], [vB.res((p, r * nb + b))], vB.ap()[p, r * nb + b], s[:, 0:256])
    k.end()
    return hT, vA, vB


def attn_A(k, g, hT, vA, yT, lam_p, subln, layer_idx):
    nc = k.nc
    lam_init = 0.8 - 0.6 * math.exp(-0.3 * layer_idx)
    k.begin()
    lpb = k.sb([128, 256], F32)
    k.ld([], [lpb], lpb[:], bass.AP(lam_p.tensor, lam_p.offset, [[0, 128], [1, 256]]))
    pr = k.sb([128, 128], F32)
    k.tt("dve", pr, pr[:, 0:64], lpb, lpb[:, 0:64], lpb, lpb[:, 64:128], ALU.mult)
    k.tt("dve", pr, pr[:, 64:128], lpb, lpb[:, 128:192], lpb, lpb[:, 192:256], ALU.mult)
    sm = k.sb([128, 2], F32)
    k.op("dve", [pr], [sm], lambda: nc.vector.reduce_sum(out=sm[:, 0:1], in_=pr[:, 0:64], axis=AX.X))
    k.op("dve", [pr], [sm], lambda: nc.vector.reduce_sum(out=sm[:, 1:2], in_=pr[:, 64:128], axis=AX.X))
    ex = k.sb([128, 2], F32)
    k.act(ex, ex[:], sm, sm[:], AF.Exp)
    neglam = k.sb([128, 1], F32)
    k.stt("dve", neglam, neglam[:], ex, ex[:, 1:2], -lam_init, ex, ex[:, 0:1], ALU.add, ALU.subtract)
    gsc = k.sb([128, 1], F32)
    k.ld([], [gsc], gsc[:], subln.rearrange("(p o) -> p o", o=1), slow=True)
    k.ts("dve", gsc, gsc[:], gsc, gsc[:], 1.0 - lam_init, ALU.mult)
    eps = g["eps"]
    Eh = k.sb([128, 16, TC], BF16)
    hrot = k.rot(2, [128, TC], BF16)
    vt_r = k.rot(2, [128, 32, 128], BF16)
    qk_r = k.rot(4, [128, S], BF16)
    for _ in range(4):
        t_ = qk_r()
        k.memset("pool", t_, t_[64:128, :], 0.0)
    p_r = k.rot(6, [128, TC], BF16)
    f_r = k.rot(12, [128, TC], F32)
    om_r = k.rot(6, [128, TC], F32)
    yst_r = k.rot(2, [128, S], BF16)
    vAv = vA.ap().rearrange("(b p) c -> p b c", p=128)
    pend = Pend(2)
    tile_i = 0
    for h in range(4):
        k.ld(g["EF"].all(), [Eh], Eh[:], g["EF"].ap()[h].rearrange("d p q -> p d q"))
        vt = vt_r()
        k.ld(vA.all(), [vt], vt[:], vAv[:, :, h * 128:(h + 1) * 128])
        QK = []
        for m in range(2):
            QT = qk_r()
            KT = qk_r()
            rq = m * 256 + h * 64
            k.ld(hT.all(), [QT], QT[0:64, :], hT.ap()[rq:rq + 64, :])
            k.ld(hT.all(), [KT], KT[0:64, :], hT.ap()[512 + rq:512 + rq + 64, :])
            QK.append((QT, KT))
        yst = yst_r()
        for j in range(NTC):
            oms = []
            for m in range(2):
                QT, KT = QK[m]
                O = k.pA()
                Dn = k.pA()
                nblk = 4 * j + 4
                om = om_r()
                oms.append(om)
                for i in range(nblk):
                    dl = 4 * j - i
                    near = dl < 13
                    ps = k.pS()
                    k.mm(ps, ps[:], KT, KT[:, i * 128:(i + 1) * 128], QT, QT[:, j * TC:(j + 1) * TC], True, not near)
                    P = p_r()
                    if near:
                        k.mm(ps, ps[:], g["ident"], g["ident"][:], Eh, Eh[:, dl + 3, :], False, True)
                        k.act(P, P[:], ps, ps[:], AF.Exp, scale=0.125)
                    else:
                        k.act(P, P[:], ps, ps[:], AF.Exp, bias=g["b31"][:, h:h + 1], scale=0.125, extra=[g["b31"]])

                    def fin(P=P, i=i, O=O, Dn=Dn, nblk=nblk, om=om, m=m, j=j, oms=oms, yst=yst, vt=vt):
                        k.mm(O, O[:], vt, vt[:, i, :], P, P[:], i == 0, i == nblk - 1)
                        k.mm(Dn, Dn[:], g["ones"], g["ones"][:], P, P[:], i == 0, i == nblk - 1)
                        if i < nblk - 1:
                            return None

                        def evac1():
                            rd = f_r()
                            k.recip(rd, rd[:], Dn, Dn[:])
                            k.tt("dve", om, om[:], O, O[:], rd, rd[:], ALU.mult)
                            if m == 0:
                                return None
                            o = f_r()
                            k.stt("dve", o, o[:], oms[1], oms[1][:], neglam[:, 0:1], oms[0], oms[0][:], ALU.mult, ALU.add,
                                  extra=[neglam])
                            sq = f_r()
                            k.act(sq, sq[:], o, o[:], AF.Square)

                            def evac2():
                                ss = k.pM()
                                k.mm(ss, ss[:], g["ones32"], g["ones32"][:], sq, sq[:], True, True)
                                sd = f_r()
                                k.act(sd, sd[:], ss, ss[:], AF.Sqrt, bias=eps[:], scale=1.0 / 128.0, extra=[eps])
                                rs = f_r()
                                k.recip(rs, rs[:], sd, sd[:])
                                k.stt("dve", yst, yst[:, j * TC:(j + 1) * TC], o, o[:], gsc[:, 0:1], rs, rs[:], ALU.mult,
                                      ALU.mult, extra=[gsc])
                                return None
                            return evac2
                        return evac1
                    pend.push(fin)
        pend.flush()
        k.st([yst], [yT.res(h)], yT.ap()[h * 128:(h + 1) * 128, :], yst[:])
    k.end()


def attn_B(k, g, hT, vB, yT):
    nc = k.nc
    k.begin()
    accO = k.sb([128, S], F32)
    accD = k.sb([128, S], F32)
    Vp_r = k.rot(1, [128, 32, 256], BF16)
    qk_r = k.rot(4, [128, S], BF16)
    for _ in range(4):
        t_ = qk_r()
        k.memset("pool", t_, t_[64:128, :], 0.0)
    E_r = k.rot(4, [128, 4, 128], BF16)
    hrot = k.rot(2, [128, TC], BF16)
    p0_r = k.rot(4, [128, TC], BF16)
    p_r = k.rot(4, [128, TC], BF16)
    yst_r = k.rot(1, [128, S], BF16)
    rd_r = k.rot(2, [128, TC], F32)
    pend = Pend()
    for hp in range(2):
        for p, d in enumerate((1, 4, 16)):
            nb = 32 // d
            G = min(4, nb)
            W = G * 128
            Vp = Vp_r()
            k.ld(vB.all(), [Vp], Vp[:], vB.ap()[p].rearrange("b t c -> t b c"))
            for hh in range(2):
                h = 2 * hp + hh
                R0 = hh * 64
                QT = qk_r()
                KT = qk_r()
                rq = 1024 + p * 512 + h * 64
                k.ld(hT.all(), [QT], QT[0:64, :], hT.ap()[rq:rq + 64, :])
                k.ld(hT.all(), [KT], KT[0:64, :], hT.ap()[rq + 256:rq + 320, :])
                Es = E_r()
                Ep = E_r()
                for Et, dl in ((Es, 0), (Ep, 1)):
                    load_E(k, g, Et, Et[:, 0, :], g["vecB"][p], 4 + 4 * p + h, LB, 128 * dl, 1, 128, hrot)
                    for gi in range(1, 4):
                        k.cp("pool", Et, Et[:, gi, :], Et, Et[:, 0, :])
                for r in range(d):
                    for b0 in range(0, nb, G):
                        def blk(b):
                            t0 = r + d * 128 * b
                            return ssl(t0, 128, d)
                        Ss = k.pS()
                        Sp = k.pS()
                        for gi in range(G):
                            b = b0 + gi
                            k.mm(Ss, Ss[:, gi * 128:(gi + 1) * 128], KT, KT[:, blk(b)], QT, QT[:, blk(b)], True, True)
                        for gi in range(G):
                            b = b0 + gi
                            if b >= 1:
                                k.mm(Sp, Sp[:, gi * 128:(gi + 1) * 128], KT, KT[:, blk(b - 1)], QT, QT[:, blk(b)], True, True)
                        c0 = 128 if b0 == 0 else 0
                        Ps0 = p0_r()
                        Ps = p_r()
                        k.act(Ps0, Ps0[:, 0:W], Ss, Ss[:, 0:W], AF.Exp, scale=0.125)
                        k.tt("dve", Ps, Ps[:, 0:W], Ps0, Ps0[:, 0:W], Es, Es[:, 0:G, :], ALU.mult)
                        Pp = None
                        if W > c0:
                            Pp0 = p0_r()
                            Pp = p_r()
                            k.act(Pp0, Pp0[:, c0:W], Sp, Sp[:, c0:W], AF.Exp, scale=0.125)
                            k.tt("pool", Pp, Pp[:, c0:W], Pp0, Pp0[:, c0:W], Ep, Ep[:, c0 // 128:G, :], ALU.mult)

                        def fin(Ps=Ps, Pp=Pp, b0=b0, r=r, d=d, nb=nb, G=G, W=W, Vp=Vp, hp=hp, p=p, R0=R0):
                            O = k.pA()
                            Dn = k.pA()
                            for gi in range(G):
                                b = b0 + gi
                                cs = slice(gi * 128, (gi + 1) * 128)
                                vs = Vp[:, r * nb + b, hp * 128:(hp + 1) * 128]
                                k.mm(O, O[:, cs], Vp, vs, Ps, Ps[:, cs], True, b == 0)
                                if b >= 1:
                                    vp_ = Vp[:, r * nb + b - 1, hp * 128:(hp + 1) * 128]
                                    k.mm(O, O[:, cs], Vp, vp_, Pp, Pp[:, cs], False, True)
                            for gi in range(G):
                                b = b0 + gi
                                cs = slice(gi * 128, (gi + 1) * 128)
                                k.mm(Dn, Dn[:, cs], g["ones"], g["ones"][:], Ps, Ps[:, cs], True, b == 0)
                                if b >= 1:
                                    k.mm(Dn, Dn[:, cs], g["ones"], g["ones"][:], Pp, Pp[:, cs], False, True)
                            t0 = r + d * 128 * b0
                            tsl = ssl(t0, W, d)
                            rows = slice(R0, R0 + 64)
                            if p == 0:
                                k.cp("act", accO, accO[rows, tsl], O, O[rows, 0:W])
                                k.cp("dve", accD, accD[rows, tsl], Dn, Dn[rows, 0:W])
                            else:
                                k.tt("dve", accO, accO[rows, tsl], accO, accO[rows, tsl], O, O[rows, 0:W], ALU.add)
                                k.tt("dve", accD, accD[rows, tsl], accD, accD[rows, tsl], Dn, Dn[rows, 0:W], ALU.add)
                        pend.push(fin)
                pend.flush()
        yst = yst_r()
        for n in range(NTC):
            rd = rd_r()
            k.recip(rd, rd[:], accD, accD[:, n * TC:(n + 1) * TC])
            k.tt("dve", yst, yst[:, n * TC:(n + 1) * TC], accO, accO[:, n * TC:(n + 1) * TC], rd, rd[:], ALU.mult)
        k.st([yst], [yT.res(4 + hp)], yT.ap()[512 + hp * 128:512 + (hp + 1) * 128, :], yst[:])
    k.end()


def out_phase(k, g, yT, nk, w_out, xin, gam, bet, x1):
    k.begin()
    yr = k.rot(2, [128, nk, TC], BF16)
    yv = yT.ap().rearrange("(c p) t -> p c t", p=128)

    def yT_fn(n):
        y = yr()
        k.ld(yT.all(), [y], y[:], yv[:, :, n * TC:(n + 1) * TC])
        return y, (lambda kc, y=y: y[:, kc, :])
    proj_resid_ln(k, g, yT_fn, nk, w_out, xin, gam, bet, x1)
    k.end()


def even_layer(k, g, xin, xout, W, l):
    hT, vA, vB = even_proj(k, g, xin, W["w_in"])
    yT = k.dram([768, S], BF16, "ev_yT")
    attn_A(k, g, hT, vA, yT, W["lam"], W["subln"], l)
    attn_B(k, g, hT, vB, yT)
    x1 = k.dram([DM, S], F32, "x1")
    x1.bf = k.dram([DM, S], BF16)
    out_phase(k, g, yT, 6, W["w_out"], xin, W["ln_g0"], W["ln_b0"], x1)
    ffn_phase(k, g, x1, W["w_up"], W["conv_w"], W["conv_b"], W["w_down"], W["ln_g1"], W["ln_b1"], xout)


def rms_fm(k, g, z, nchunk, gam, out_t, f_r, eps, sq_r):
    nc = k.nc
    sq = sq_r()
    ss = k.pM()
    for m in range(nchunk):
        k.op("act", [z], [sq], lambda m=m: nc.scalar.activation(out=sq[:, m, :], in_=z[:, m, :], func=AF.Square))
    for m in range(nchunk):
        k.mm(ss, ss[:], g["ones"], g["ones"][:], sq, sq[:, m, :], m == 0, m == nchunk - 1)
    sd = f_r()
    k.act(sd, sd[:, 0, :], ss, ss[:], AF.Sqrt, bias=eps[:], scale=1.0 / (128.0 * nchunk), extra=[eps])
    k.recip(sd, sd[:, 1, :], sd, sd[:, 0, :])
    for m in range(nchunk):
        k.stt("dve", out_t, out_t[:, m, :], z, z[:, m, :], gam[:, m:m + 1], sd, sd[:, 1, :], ALU.mult, ALU.mult, extra=[gam])


def odd_proj(k, g, xin, W, cin):
    nc = k.nc
    w_in = W["w_in"]
    hT = k.dram([1024, S], BF16, "od_hT")
    vSW = k.dram([S, 256], BF16, "od_vSW")
    gateT = k.dram([24, S], F32, "od_gate")
    qdT = k.dram([8, 96, S], BF16, "od_qd")
    kdT = k.dram([8, 64, S], BF16, "od_kd")
    krr = k.dram([32, S], BF16, "od_krr")
    vD = k.dram([S, 512], BF16, "od_vD")
    k.begin()
    xb = load_xb(k, xin, k.sb([128, 8, S], BF16))
    wst = k.rot(2, [128, 8, 384], F32)
    wbf = k.rot(2, [128, 8, 128], BF16)
    stg_r = k.rot(1, [128, S], BF16)
    proj_fm_all(k, xb, w_in, [0, 128, 256, 384, 512, 640, 768, 1024], hT, wst, wbf, stg_r)
    wv2 = k.sb([128, 8, 256], BF16)
    load_w_bf16(k, w_in[:, 896:1024], 8, 128, wv2, wv2[:, :, 0:128], wst, "pool")
    load_w_bf16(k, w_in[:, 1152:1280], 8, 128, wv2, wv2[:, :, 128:256], wst, "pool")
    vst = k.rot(3, [128, 512], BF16)
    for b in range(32):
        ps = k.pS()
        for kc in range(8):
            k.mm(ps, ps[:, 0:256], xb.t(b // 4), xb[:, kc, b * 128:(b + 1) * 128], wv2, wv2[:, kc, :], kc == 0, kc == 7)
        s = vst()
        k.cp("act" if b % 2 == 0 else "dve", s, s[:, 0:256], ps, ps[:, 0:256])
        k.st([s], [vSW.res(b)], vSW.ap()[b * 128:(b + 1) * 128, :], s[:, 0:256])
    wg = wbf()
    load_w_bf16(k, w_in[:, 1280:1304], 8, 24, wg, wg[:, :, 0:24], wst, "pool")
    gst_r = k.rot(2, [24, TC], F32)
    for n in range(NTC):
        ps = k.pS()
        for kc in range(8):
            k.mm(ps, ps[0:24, :], wg, wg[:, kc, 0:24], xb.t(n), xb[:, kc, n * TC:(n + 1) * TC], kc == 0, kc == 7)
        gst = gst_r()
        k.act(gst, gst[:], ps, ps[0:24, :], AF.Sigmoid)
        k.st([gst], [gateT.res(n)], gateT.ap()[:, n * TC:(n + 1) * TC], gst[:])
    k.end()
    k.begin()
    xb = load_xb(k, xin, k.sb([128, 8, S], BF16))
    wst = k.rot(1, [128, 8, 384], F32)
    vst = k.rot(3, [128, 512], BF16)
    wcq = k.sb([128, 8, 384], BF16)
    load_w_bf16(k, w_in[:, 1304:1688], 8, 384, wcq, wcq[:], wst, "pool")
    wckv = k.sb([128, 8, 256], BF16)
    load_w_bf16(k, w_in[:, 1688:1944], 8, 256, wckv, wckv[:], wst, "pool")
    wkr = k.sb([128, 8, 96], BF16)
    wkrr = k.sb([128, 8, 96], BF16)
    k.memset("pool", wkr, wkr[:], 0.0)
    k.memset("pool", wkrr, wkrr[:], 0.0)
    s = wst()
    k.ld([], [s], s[:, :, 0:32], w_in[:, 1944:1976].rearrange("(kc p) m -> p kc m", p=128))
    k.cp("pool", wkr, wkr[:, :, 64:96], s, s[:, :, 0:32])
    k.cp("pool", wkrr, wkrr[:, :, 80:96], s, s[:, :, 0:16])
    k.ts("dve", wkrr, wkrr[:, :, 64:80], s, s[:, :, 16:32], -1.0, ALU.mult)
    wq = k.sb([128, 3, 768], BF16)
    wqr = k.sb([128, 3, 8, 96], BF16)
    k.memset("pool", wqr, wqr[:], 0.0)
    wst2 = k.rot(1, [128, 3, 768], F32)
    s = wst2()
    k.ld([], [s], s[:], W["w_uq"].rearrange("(kc p) m -> p kc m", p=128))
    k.cp("pool", wq, wq[:], s, s[:])
    for h in range(8):
        k.cp("pool", wqr, wqr[:, :, h, 80:96], s, s[:, :, h * 96 + 64:h * 96 + 80])
        k.ts("dve", wqr, wqr[:, :, h, 64:80], s, s[:, :, h * 96 + 80:h * 96 + 96], -1.0, ALU.mult)
    wk = k.sb([128, 2, 512], BF16)
    wv = k.sb([128, 2, 512], BF16)
    for wt, nm in ((wk, "w_uk"), (wv, "w_uv")):
        s = wst2()
        k.ld([], [s], s[:, 0:2, 0:512], W[nm].rearrange("(kc p) m -> p kc m", p=128))
        k.cp("pool", wt, wt[:], s, s[:, 0:2, 0:512])
    gq = k.sb([128, 3], F32)
    k.ld([], [gq], gq[:], W["q_norm"].rearrange("(m p) -> p m", p=128), slow=True)
    gkv = k.sb([128, 2], F32)
    k.ld([], [gkv], gkv[:], W["kv_norm"].rearrange("(m p) -> p m", p=128), slow=True)
    cs_r = k.rot(3, [96, 2, TC], F32)
    z_r = k.rot(2, [128, 3, TC], F32)
    f_r = k.rot(2, [128, 2, TC], F32)
    sq_r = k.rot(2, [128, 3, TC], BF16)
    cqn_r = k.rot(3, [128, 3, TC], BF16)
    cn_r = k.rot(3, [128, 2, TC], BF16)
    t_r = k.rot(4, [96, TC], F32)
    qst_r = k.rot(3, [96, TC], BF16)
    kst_r = k.rot(3, [64, TC], BF16)
    eps = g["eps"]
    k.set_pools(3, 3, 2)
    pend = Pend(1)

    def rope_from(psA, psB, dst, dst_ap, cst):
        t = t_r()
        u = t_r()
        k.tt("dve", t, t[64:96, :], psA, psA[64:96, :], cst, cst[64:96, 0, :], ALU.mult)
        k.tt("dve", u, u[64:96, :], psB, psB[64:96, :], cst, cst[64:96, 1, :], ALU.mult)
        k.tt("pool", dst, dst_ap, t, t[64:96, :], u, u[64:96, :], ALU.add)

    def heads_and_v(n, tsl, cqn, cn, cst):
        for h in range(8):
            psA = k.pA()
            psB = k.pA()
            for kc in range(3):
                k.mm(psA, psA[0:96, :], wq, wq[:, kc, h * 96:(h + 1) * 96], cqn, cqn[:, kc, :], kc == 0, kc == 2)
            for kc in range(3):
                k.mm(psB, psB[0:96, :], wqr, wqr[:, kc, h, :], cqn, cqn[:, kc, :], kc == 0, kc == 2)
            qs = qst_r()
            k.cp("act", qs, qs[0:64, :], psA, psA[0:64, :])
            rope_from(psA, psB, qs, qs[64:96, :], cst)
            k.st([qs], [qdT.res((h, n))], qdT.ap()[h][:, tsl], qs[:])
            ps = k.pS()
            for kc in range(2):
                k.mm(ps, ps[0:64, :], wk, wk[:, kc, h * 64:(h + 1) * 64], cn, cn[:, kc, :], kc == 0, kc == 1)
            ks = kst_r()
            k.cp("act", ks, ks[:], ps, ps[0:64, :])
            k.st([ks], [kdT.res((h, n))], kdT.ap()[h][:, tsl], ks[:])
        for bb in range(4):
            ps = k.pS()
            for kc in range(2):
                k.mm(ps, ps[:], cn, cn[:, kc, bb * 128:(bb + 1) * 128], wv, wv[:, kc, :], kc == 0, kc == 1)
            s = vst()
            k.cp("dve", s, s[:], ps, ps[:])
            b = n * 4 + bb
            k.st([s], [vD.res(b)], vD.ap()[b * 128:(b + 1) * 128, :], s[:])

    for n in range(NTC):
        tsl = slice(n * TC, (n + 1) * TC)
        cst = cs_r()
        k.ld([], [cst], cst[64:96, 0, :], cin["c_cos"].ap()[:, tsl])
        k.ld([], [cst], cst[64:96, 1, :], cin["c_sin"].ap()[:, tsl])
        z = z_r()
        for m in range(3):
            ps = k.pS()
            for kc in range(8):
                k.mm(ps, ps[:], wcq, wcq[:, kc, m * 128:(m + 1) * 128], xb.t(n), xb[:, kc, tsl], kc == 0, kc == 7)
            k.cp("act", z, z[:, m, :], ps, ps[:])
        cqn = cqn_r()
        rms_fm(k, g, z, 3, gq, cqn, f_r, eps, sq_r)
        z = z_r()
        for m in range(2):
            ps = k.pS()
            for kc in range(8):
                k.mm(ps, ps[:], wckv, wckv[:, kc, m * 128:(m + 1) * 128], xb.t(n), xb[:, kc, tsl], kc == 0, kc == 7)
            k.cp("act", z, z[:, m, :], ps, ps[:])
        cn = cn_r()
        rms_fm(k, g, z, 2, gkv, cn, f_r, eps, sq_r)
        psA = k.pA()
        psB = k.pA()
        for kc in range(8):
            k.mm(psA, psA[0:96, :], wkr, wkr[:, kc, :], xb.t(n), xb[:, kc, tsl], kc == 0, kc == 7)
        for kc in range(8):
            k.mm(psB, psB[0:96, :], wkrr, wkrr[:, kc, :], xb.t(n), xb[:, kc, tsl], kc == 0, kc == 7)
        qs = qst_r()
        rope_from(psA, psB, qs, qs[64:96, :], cst)
        k.st([qs], [krr.res(n)], krr.ap()[:, tsl], qs[64:96, :])
        pend.push(lambda n=n, tsl=tsl, cqn=cqn, cn=cn, cst=cst: heads_and_v(n, tsl, cqn, cn, cst))
    pend.flush()
    k.end()
    return hT, vSW, gateT, qdT, kdT, krr, vD


def attn_D(k, g, qdT, kdT, krr, vD, yT):
    k.begin()
    k.set_pools(4, 2, 2)
    Em = k.sb([128, 4, TC], BF16)
    k.ld(g["EM"].all(), [Em], Em[:], g["EM"].ap().rearrange("d p q -> p d q"))
    vt_r = k.rot(2, [128, 32, 128], BF16)
    qk_r = k.rot(4, [96, S], BF16)
    p_r = k.rot(7, [128, TC], BF16)
    f_r = k.rot(4, [128, TC], F32)
    od_r = k.rot(4, [128, TC], F32)
    odb_r = k.rot(4, [128, TC], BF16)
    yst_r = k.rot(2, [128, S], BF16)
    vDv = vD.ap().rearrange("(b p) c -> p b c", p=128)
    scale = float(96 ** -0.5)
    pend = Pend(3)
    for hp in range(4):
        yst = yst_r()
        for hh in range(2):
            h = 2 * hp + hh
            rows = slice(hh * 64, hh * 64 + 64)
            rsel = g["rsel0"] if hh == 0 else g["rsel1"]
            vt = vt_r()
            k.ld(vD.all(), [vt], vt[:, :, hh * 64:hh * 64 + 64], vDv[:, :, h * 64:(h + 1) * 64])
            k.memset("pool", vt, vt[:, :, (1 - hh) * 64:(1 - hh) * 64 + 64], 1.0)
            QT = qk_r()
            KT = qk_r()
            k.ld(qdT.all(), [QT], QT[:], qdT.ap()[h])
            k.ld(kdT.all(), [KT], KT[0:64, :], kdT.ap()[h])
            k.ld(krr.all(), [KT], KT[64:96, :], krr.ap())
            for j in range(NTC):
                O = k.pA()
                nblk = 4 * j + 4
                for i in range(nblk):
                    dl = 4 * j - i
                    near = dl <= 0
                    ps = k.pS()
                    k.mm(ps, ps[:], KT, KT[:, i * 128:(i + 1) * 128], QT, QT[:, j * TC:(j + 1) * TC], True, not near)
                    if near:
                        k.mm(ps, ps[:], g["ident"], g["ident"][:], Em, Em[:, dl + 3, :], False, True)
                    P = p_r()
                    k.act(P, P[:], ps, ps[:], AF.Exp, scale=scale)

                    def fin(P=P, i=i, O=O, nblk=nblk, j=j, yst=yst, vt=vt, rows=rows, rsel=rsel):
                        k.mm(O, O[:], vt, vt[:, i, :], P, P[:], i == 0, i == nblk - 1)
                        if i < nblk - 1:
                            return None

                        def evac1():
                            od = od_r()
                            k.cp("act", od, od[:], O, O[:])

                            def evac2():
                                dn = k.pM()
                                k.mm(dn, dn[:], rsel, rsel[:], od, od[:], True, True)
                                rd = f_r()
                                k.recip(rd, rd[rows, :], dn, dn[rows, :])
                                k.tt("dve", yst, yst[rows, j * TC:(j + 1) * TC], od, od[rows, :], rd, rd[rows, :], ALU.mult)
                                return None
                            return evac2
                        return evac1
                    pend.push(fin)
            pend.flush()
        k.st([yst], [yT.res(4 + hp)], yT.ap()[512 + hp * 128:512 + (hp + 1) * 128, :], yst[:])
    k.end()


def nsa_compress(k, g, hT, W):
    nc = k.nc
    kcT = k.dram([2, 64, 256], BF16, "od_kc")
    vcd = k.dram([2, 256, 64], BF16, "od_vc")
    k.begin()
    w1s_r = k.rot(1, [64, 32, 256], F32)
    w1_r = k.rot(1, [64, 32, 256], BF16)
    w2_r = k.rot(1, [128, 2, 64], BF16)
    w2s_r = k.rot(1, [128, 2, 64], F32)
    pes = k.sb([64, 32], F32)
    peT = k.sb([64, 32], BF16)
    cb = k.sb([128, 2], F32)
    tt_r = k.rot(2, [64, S], BF16)
    hid_r = k.rot(2, [128, 2, 256], BF16)
    st_r = k.rot(2, [128, 256], BF16)
    for kv in range(2):
        w1s = w1s_r()
        k.ld([], [w1s], w1s[:], W["cmp_w1"][kv].rearrange("(pos d) h -> d pos h", d=64))
        w1 = w1_r()
        k.cp("pool", w1, w1[:], w1s, w1s[:])
        w2s = w2s_r()
        k.ld([], [w2s], w2s[:], W["cmp_w2"][kv].rearrange("(hh p) d -> p hh d", p=128))
        w2 = w2_r()
        k.cp("pool", w2, w2[:], w2s, w2s[:])
        k.ld([], [pes], pes[:], W["cmp_pe"][kv].rearrange("pos d -> d pos"), slow=True)
        k.cp("dve", peT, peT[:], pes, pes[:])
        for hh in range(2):
            ps = k.pM()
            for pos in range(32):
                k.mm(ps, ps[:, 0:1], w1, w1[:, pos, hh * 128:(hh + 1) * 128], peT, peT[:, pos:pos + 1], pos == 0, pos == 31)
            k.cp("dve", cb, cb[:, hh:hh + 1], ps, ps[:, 0:1])
        for gI in range(2):
            T = tt_r()
            r0 = 512 + kv * 128 + gI * 64
            k.ld(hT.all(), [T], T[:], hT.ap()[r0:r0 + 64, :])
            hid = hid_r()
            k.memset("pool", hid, hid[:], 0.0)
            for hh in range(2):
                ps = k.pS()
                for pos in range(32):
                    k.mm(ps, ps[:, 0:255], w1, w1[:, pos, hh * 128:(hh + 1) * 128], T, T[:, ssl(pos, 255, 16)], pos == 0, pos == 31)
                k.act(hid, hid[:, hh, 0:255], ps, ps[:, 0:255], AF.Gelu_apprx_tanh, bias=cb[:, hh:hh + 1], extra=[cb])
            s = st_r()
            if kv == 0:
                ps = k.pM()
                for hh in range(2):
                    k.mm(ps, ps[0:64, 0:256], w2, w2[:, hh, :], hid, hid[:, hh, :], hh == 0, hh == 1)
                k.cp("dve", s, s[0:64, :], ps, ps[0:64, 0:256])
                k.memset("pool", s, s[0:64, 255:256], 0.0)
                k.st([s], [kcT.res(gI)], kcT.ap()[gI], s[0:64, :])
            else:
                for cbk in range(2):
                    ps = k.pM()
                    for hh in range(2):
                        k.mm(ps, ps[:, 0:64], hid, hid[:, hh, cbk * 128:(cbk + 1) * 128], w2, w2[:, hh, :], hh == 0, hh == 1)
                    s = st_r()
                    k.cp("dve", s, s[:, 0:64], ps, ps[:, 0:64])
                    k.st([s], [vcd.res((gI, cbk))], vcd.ap()[gI][cbk * 128:(cbk + 1) * 128, :], s[:, 0:64])
    k.end()
    return kcT, vcd


def nsa_cmp_select(k, g, hT, kcT, vcd, cin):
    nc = k.nc
    ocmp = k.dram([8, 128, S], F32, "od_ocmp")
    negm = k.dram([2, 64, S], BF16, "od_negm")
    k.begin()
    Msel = k.sb([128, 2, 64], F32)
    k.ld([], [Msel], Msel[:], cin["c_Msel"].ap().rearrange("(cb p) j -> p cb j", p=128))
    Fm = k.sb([128, 32, 64], F32)
    k.ld([], [Fm], Fm[:], cin["c_Fm"].ap().rearrange("(t p) j -> p t j", p=128))
    kc_r = k.rot(1, [128, 256], BF16)
    vc_r = k.rot(1, [128, 2, 128], BF16)
    q_r = k.rot(4, [128, S], BF16)
    for _ in range(4):
        t_ = q_r()
        k.memset("pool", t_, t_[64:128, :], 0.0)
    t_ = kc_r()
    k.memset("pool", t_, t_[64:128, :], 0.0)
    hrot = k.rot(2, [128, TC], BF16)
    k.set_pools(3, 3, 2)
    E_r = k.rot(4, [128, TC], BF16)
    p0_r = k.rot(4, [128, TC], BF16)
    p_r = k.rot(8, [128, TC], BF16)
    f_r = k.rot(8, [128, TC], F32)
    oc_r = k.rot(2, [128, TC], F32)
    sc_r = k.rot(4, [128, 64], F32)
    m8_r = k.rot(2, [128, 16], F32)
    nm_r = k.rot(2, [128, 64], BF16)
    nmT_r = k.rot(2, [64, TC], BF16)
    pend = Pend(1)
    pgs_r = k.rot(6, [128, TC], F32)
    for gI in range(2):
        kc = kc_r()
        k.ld(kcT.all(), [kc], kc[0:64, :], kcT.ap()[gI])
        vc = vc_r()
        vv = vcd.ap()[gI].rearrange("(cb p) d -> p cb d", p=128)
        k.ld(vcd.all(), [vc], vc[:, :, 0:64], vv)
        k.ld(vcd.all(), [vc], vc[:, :, 64:128], vv)
        QTs = []
        for hg in range(4):
            QT = q_r()
            head = 4 * gI + hg
            k.ld(hT.all(), [QT], QT[0:64, :], hT.ap()[head * 64:(head + 1) * 64, :])
            QTs.append(QT)
        for j in range(NTC):
            ncb = 1 if j < 4 else 2
            tsl = slice(j * TC, (j + 1) * TC)
            pg = [pgs_r() for _ in range(ncb)]
            for hg in range(4):
                head = 4 * gI + hg
                Ps = []
                for cbk in range(ncb):
                    E = E_r()
                    k.ld(g["EC"].all(), [E], E[:], g["EC"].ap()[head, 2 * j + cbk])
                    ps = k.pS()
                    k.mm(ps, ps[:], kc, kc[:, cbk * 128:(cbk + 1) * 128], QTs[hg], QTs[hg][:, tsl], True, True)
                    P0 = p0_r()
                    k.act(P0, P0[:], ps, ps[:], AF.Exp, scale=0.125)
                    P = p_r()
                    k.tt("dve", P, P[:], P0, P0[:], E, E[:], ALU.mult)
                    Ps.append(P)

                def fin(Ps=Ps, ncb=ncb, hg=hg, head=head, pg=pg, tsl=tsl, j=j):
                    O = k.pA()
                    Dn = k.pA()
                    for cbk in range(ncb):
                        k.mm(O, O[:], vc, vc[:, cbk, :], Ps[cbk], Ps[cbk][:], cbk == 0, cbk == ncb - 1)
                        k.mm(Dn, Dn[:], g["ones"], g["ones"][:], Ps[cbk], Ps[cbk][:], cbk == 0, cbk == ncb - 1)
                    dm = f_r()
                    k.ts("dve", dm, dm[:], Dn, Dn[:], 1e-30, ALU.max)
                    ln_ = f_r()
                    k.act(ln_, ln_[:], dm, dm[:], AF.Ln)
                    rd = f_r()
                    k.act(rd, rd[:], ln_, ln_[:], AF.Exp, scale=-1.0)
                    oc = oc_r()
                    k.tt("dve", oc, oc[:], O, O[:], rd, rd[:], ALU.mult)
                    k.st([oc], [ocmp.res((head, j))], ocmp.ap()[head][:, tsl], oc[:])
                    for cbk in range(ncb):
                        if hg == 0:
                            k.tt("pool", pg[cbk], pg[cbk][:], Ps[cbk], Ps[cbk][:], rd, rd[:], ALU.mult)
                        else:
                            tmp = f_r()
                            k.tt("pool", tmp, tmp[:], Ps[cbk], Ps[cbk][:], rd, rd[:], ALU.mult)
                            k.tt("dve", pg[cbk], pg[cbk][:], pg[cbk], pg[cbk][:], tmp, tmp[:], ALU.add)
                    if hg < 3:
                        return None

                    def topk():
                        nmT = nmT_r()
                        for t in range(4):
                            qt = 4 * j + t
                            ps = k.pM()
                            for cbk in range(ncb):
                                k.mm(ps, ps[:, 0:64], pg[cbk], pg[cbk][:, t * 128:(t + 1) * 128], Msel, Msel[:, cbk, :],
                                     cbk == 0, cbk == ncb - 1)
                            sc = sc_r()
                            k.tt("dve", sc, sc[:], ps, ps[:, 0:64], Fm, Fm[:, qt, :], ALU.add)
                            m8 = m8_r()
                            k.op("dve", [sc], [m8], lambda sc=sc, m8=m8: nc.vector.max(out=m8[:, 0:8], in_=sc[:]))
                            sc2 = sc_r()
                            k.op("dve", [sc, m8], [sc2], lambda sc=sc, m8=m8, sc2=sc2: nc.vector.match_replace(
                                out=sc2[:], in_to_replace=m8[:, 0:8], in_values=sc[:], imm_value=-1e9))
                            k.op("dve", [sc2], [m8], lambda sc2=sc2, m8=m8: nc.vector.max(out=m8[:, 8:16], in_=sc2[:]))
                            nm = nm_r()
                            k.ts("dve", nm, nm[:], sc, sc[:], m8[:, 15:16], ALU.is_lt, -30000.0, ALU.mult, extra=[m8])
                            ps2 = k.pM()
                            k.mm(ps2, ps2[0:64, 0:128], nm, nm[:], g["ident"], g["ident"][:], True, True)
                            k.cp("act", nmT, nmT[:, t * 128:(t + 1) * 128], ps2, ps2[0:64, 0:128])
                        k.st([nmT], [negm.res((gI, j))], negm.ap()[gI][:, tsl], nmT[:])
                        return None
                    return topk
                pend.push(fin)
        pend.flush()
    k.end()
    return ocmp, negm


def nsa_slc_win(k, g, hT, vSW, gateT, ocmp, negm, yT, cin):
    nc = k.nc
    k.begin()
    k.set_pools(4, 2, 2)
    Sel = k.sb([128, 24, 128], BF16)
    gt = k.sb([128, S], BF16)
    k.memset("pool", Sel, Sel[:], 0.0)
    k.memset("pool", gt, gt[:], 0.0)
    gst32 = k.sb([24, S], F32)
    k.ld([], [gst32], gst32[:, 0:3072], cin["c_Sel"].ap().rearrange("r a m -> r (a m)"))
    k.cp("dve", Sel, Sel[0:24, :, :], gst32, gst32[:, 0:3072].rearrange("r (a m) -> r a m", m=128))
    k.ld(gateT.all(), [gst32], gst32[:], gateT.ap())
    k.cp("dve", gt, gt[0:24, :], gst32, gst32[:])
    Es = k.sb([128, 16, TC], BF16)
    Ew = k.sb([128, 8, TC], BF16)
    KE = k.sb([128, 32, 128], BF16)
    k.ld([g["exbf"].res()], [KE], KE[64:128, :, :], g["exbf"].ap())
    KW = k.sb([128, S], BF16)
    k.memset("pool", KW, KW[64:128, :], 0.0)
    vts = [k.sb([128, 32, 128], BF16) for _ in range(4)]
    QN_r = k.rot(2, [128, S], BF16)
    p_r = k.rot(7, [128, TC], BF16)
    f_r = k.rot(8, [128, TC], F32)
    gs_r = k.rot(6, [128, TC], F32)
    od_r = k.rot(3, [128, TC], F32)
    odb_r = k.rot(3, [128, TC], BF16)
    oc_r = k.rot(2, [128, TC], F32)
    yst_r = k.rot(2, [128, S], BF16)
    vv = vSW.ap().rearrange("(b p) c -> p b c", p=128)
    pend = Pend(3)
    for gI in range(2):
        k.ld(hT.all(), [KE], KE[0:64, :, :], hT.ap()[768 + gI * 64:768 + gI * 64 + 64, :].rearrange("d (b t) -> d b t", t=128))
        k.ld(hT.all(), [KW], KW[0:64, :], hT.ap()[896 + gI * 64:896 + gI * 64 + 64, :])
        for bi, c0 in ((0, gI * 64), (1, 128 + gI * 64)):
            for par in range(2):
                vt = vts[bi * 2 + par]
                k.ld(vSW.all(), [vt], vt[:, :, par * 64:par * 64 + 64], vv[:, :, c0:c0 + 64])
                k.memset("pool", vt, vt[:, :, (1 - par) * 64:(1 - par) * 64 + 64], 1.0)
        for hg in range(4):
            head = 4 * gI + hg
            par = head % 2
            rows = slice(par * 64, par * 64 + 64)
            rsel = g["rsel0"] if par == 0 else g["rsel1"]
            if par == 0:
                yst = yst_r()
            QN = QN_r()
            k.ld(hT.all(), [QN], QN[0:64, :], hT.ap()[head * 64:(head + 1) * 64, :])
            k.ld(negm.all(), [QN], QN[64:128, :], negm.ap()[gI])
            k.ld(g["EF"].all(), [Es], Es[:], g["EF"].ap()[head].rearrange("d p q -> p d q"))
            k.ld(g["EW"].all(), [Ew], Ew[:], g["EW"].ap()[head].rearrange("d p q -> p d q"))
            for j in range(NTC):
                tsl = slice(j * TC, (j + 1) * TC)
                gsb = []
                for b3 in range(3):
                    r = (b3 * 2 + gI) * 4 + hg
                    gp = k.pM()
                    k.mm(gp, gp[:], Sel, Sel[:, r, :], gt, gt[:, tsl], True, True)
                    gs_ = gs_r()
                    k.cp("act", gs_, gs_[rows, :], gp, gp[rows, :])
                    gsb.append(gs_)
                res_sw = {}
                for br in (1, 2):
                    O = k.pA()
                    vt = vts[(br - 1) * 2 + par]
                    ilist = list(range(4 * j + 4)) if br == 1 else list(range(max(0, 4 * j - 4), 4 * j + 4))
                    for ii, i in enumerate(ilist):
                        dl = 4 * j - i
                        near = (br == 2) or dl < 13
                        ps = k.pS()
                        if br == 1:
                            k.mm(ps, ps[:], KE, KE[:, i, :], QN, QN[:, tsl], True, not near)
                        else:
                            k.mm(ps, ps[:], KW, KW[:, i * 128:(i + 1) * 128], QN, QN[:, tsl], True, not near)
                        P = p_r()
                        if near:
                            Et = Es if br == 1 else Ew
                            k.mm(ps, ps[:], g["ident"], g["ident"][:], Et, Et[:, dl + 3, :], False, True)
                            k.act(P, P[:], ps, ps[:], AF.Exp, scale=0.125)
                        else:
                            k.act(P, P[:], ps, ps[:], AF.Exp, bias=g["b31"][:, head:head + 1], scale=0.125, extra=[g["b31"]])
                        first = ii == 0
                        last = ii == len(ilist) - 1

                        def fin(P=P, i=i, O=O, first=first, last=last, vt=vt, br=br, res_sw=res_sw, j=j,
                                head=head, rows=rows, yst=yst, tsl=tsl, rsel=rsel, gsb=gsb):
                            k.mm(O, O[:], vt, vt[:, i, :], P, P[:], first, last)
                            if not last:
                                return None

                            def evac1():
                                od = od_r()
                                k.cp("act", od, od[:], O, O[:])

                                def evac2():
                                    dn = k.pM()
                                    k.mm(dn, dn[:], rsel, rsel[:], od, od[:], True, True)
                                    rd = f_r()
                                    k.recip(rd, rd[rows, :], dn, dn[rows, :])
                                    fac = f_r()
                                    k.tt("pool", fac, fac[rows, :], rd, rd[rows, :], gsb[br], gsb[br][rows, :], ALU.mult)
                                    on = f_r()
                                    k.tt("dve", on, on[rows, :], od, od[rows, :], fac, fac[rows, :], ALU.mult)
                                    res_sw[br] = on
                                    if br == 1:
                                        return None
                                    oc = oc_r()
                                    k.ld(ocmp.all(), [oc], oc[rows, :], ocmp.ap()[head][rows, tsl])
                                    tcm = f_r()
                                    k.tt("pool", tcm, tcm[rows, :], oc, oc[rows, :], gsb[0], gsb[0][rows, :], ALU.mult)
                                    a2 = f_r()
                                    k.tt("pool", a2, a2[rows, :], tcm, tcm[rows, :], res_sw[1], res_sw[1][rows, :], ALU.add)
                                    k.tt("dve", yst, yst[rows, tsl], a2, a2[rows, :], on, on[rows, :], ALU.add)
                                    return None
                                return evac2
                            return evac1
                        pend.push(fin)
            pend.flush()
            if par == 1:
                k.st([yst], [yT.res(head // 2)], yT.ap()[(head // 2) * 128:(head // 2 + 1) * 128, :], yst[:])
    k.end()


def odd_layer(k, g, xin, xout, W, cin):
    hT, vSW, gateT, qdT, kdT, krr, vD = odd_proj(k, g, xin, W, cin)
    yT = k.dram([1024, S], BF16, "od_yT")
    attn_D(k, g, qdT, kdT, krr, vD, yT)
    kcT, vcd = nsa_compress(k, g, hT, W)
    ocmp, negm = nsa_cmp_select(k, g, hT, kcT, vcd, cin)
    nsa_slc_win(k, g, hT, vSW, gateT, ocmp, negm, yT, cin)
    x1 = k.dram([DM, S], F32, "x1")
    x1.bf = k.dram([DM, S], BF16)
    out_phase(k, g, yT, 8, W["w_out"], xin, W["ln_g0"], W["ln_b0"], x1)
    ffn_phase(k, g, x1, W["w_up"], W["conv_w"], W["conv_b"], W["w_down"], W["ln_g1"], W["ln_b1"], xout)


EV_W = {"w_in": [1024, 3840], "w_out": [768, 1024], "lam": [4, 64], "subln": [128]}
OD_W = {"w_in": [1024, 1976], "w_out": [1024, 1024], "cmp_pe": [2, 32, 64], "cmp_w1": [2, 2048, 256], "cmp_w2": [2, 256, 64],
        "q_norm": [384], "kv_norm": [256], "w_uq": [384, 768], "w_uk": [256, 512], "w_uv": [256, 512]}
FF_W = {"w_up": [1024, 5632], "conv_w": [3, 2816], "conv_b": [2816], "w_down": [2816, 1024], "ln_g0": [1024], "ln_b0": [1024],
        "ln_g1": [1024], "ln_b1": [1024]}
FUSED = True


def layer_shapes(l):
    sh = dict(EV_W if l % 2 == 0 else OD_W)
    sh.update(FF_W)
    return sh


def build_program(layers):
    nc = bass.Bass("TRN2", target_bir_lowering=False)
    k = K(nc)
    cin = {n: nc.dram_tensor(n, s, F32, kind="ExternalInput") for n, s in CONST_SHAPES.items()}
    rel_bias = nc.dram_tensor("rel_bias", [32, 16], F32, kind="ExternalInput")
    xin = DT(nc.dram_tensor("xin", [DM, S], F32, kind="ExternalInput"))
    xo = DT(nc.dram_tensor("xo", [DM, S], F32, kind="ExternalOutput"))
    Ws = {}
    for l in layers:
        Ws[l] = {n: nc.dram_tensor("L%d_%s" % (l, n), s, F32, kind="ExternalInput").ap() for n, s in layer_shapes(l).items()}
    g = setup_globals(k, rel_bias, cin)
    cur = xin
    for li, l in enumerate(layers):
        nxt = xo if li == len(layers) - 1 else k.dram([DM, S], F32)
        if nxt is not xo:
            nxt.bf = k.dram([DM, S], BF16)
        if l % 2 == 0:
            even_layer(k, g, cur, nxt, Ws[l], l)
        else:
            odd_layer(k, g, cur, nxt, Ws[l], cin)
        cur = nxt
    k.c.barrier()
    return nc


def layer_inputs(inp, l):
    i = l // 2
    m = {}
    if l % 2 == 0:
        m.update({"w_in": inp["ev_w_in"][i], "w_out": inp["ev_w_out"][i], "lam": inp["ev_lambda"][i], "subln": inp["ev_subln"][i]})
    else:
        m.update({"w_in": inp["od_w_in"][i], "w_out": inp["od_w_out"][i], "cmp_pe": inp["od_cmp_pe"][i], "cmp_w1": inp["od_cmp_w1"][i],
                  "cmp_w2": inp["od_cmp_w2"][i], "q_norm": inp["od_q_norm"][i], "kv_norm": inp["od_kv_norm"][i],
                  "w_uq": inp["od_w_uq"][i], "w_uk": inp["od_w_uk"][i], "w_uv": inp["od_w_uv"][i]})
    m.update({"w_up": inp["ffn_w_up"][l], "conv_w": inp["ffn_conv_w"][l], "conv_b": inp["ffn_conv_b"][l], "w_down": inp["ffn_w_down"][l],
              "ln_g0": inp["ln_g"][l, 0], "ln_b0": inp["ln_b"][l, 0], "ln_g1": inp["ln_g"][l, 1], "ln_b1": inp["ln_b"][l, 1]})
    return {"L%d_%s" % (l, n): np.ascontiguousarray(np.asarray(v, dtype=np.float32)) for n, v in m.items()}


def kernel(**inputs):
    inp = {n: np.asarray(v) for n, v in inputs.items()}
    x = inp["x"].astype(np.float32, copy=False)
    nb = x.shape[0]
    consts = host_consts()
    xT = [np.ascontiguousarray(x[b].T) for b in range(nb)]
    groups = [[0, 1, 2, 3]] if FUSED else [[0], [1], [2], [3]]
    for layers in groups:
        nc = build_program(layers)
        shared = dict(consts)
        shared["rel_bias"] = np.ascontiguousarray(inp["rel_bias"].astype(np.float32))
        for l in layers:
            shared.update(layer_inputs(inp, l))
        in_maps = []
        for b in range(nb):
            m = dict(shared)
            m["xin"] = xT[b]
            in_maps.append(m)
        res = run_bass_kernel_spmd(nc, in_maps, core_ids=list(range(nb)))
        xT = [np.asarray(res.results[b]["xo"]) for b in range(nb)]
    out = np.stack([xT[b].T for b in range(nb)], axis=0).astype(np.float32)
    return np.ascontiguousarray(out)
```

```python
import math
from contextlib import ExitStack

import numpy as np
import concourse.bass as bass
import concourse.mybir as mybir
from concourse.bass_utils import run_bass_kernel_spmd

F32 = mybir.dt.float32
BF16 = mybir.dt.bfloat16
I32 = mybir.dt.int32
AF = mybir.ActivationFunctionType
ALU = mybir.AluOpType
AX = mybir.AxisListType


class Res:
    __slots__ = ("lw", "rd", "name")

    def __init__(self, name=""):
        self.lw = None
        self.rd = {}
        self.name = name


class Ctx:
    NRING = 12

    def __init__(self, nc):
        self.nc = nc
        self.eng = {"pe": nc.tensor, "act": nc.scalar, "dve": nc.vector, "pool": nc.gpsimd, "sp": nc.sync}
        self.sem = {}
        self.cnt = {}
        for e in ("pe", "act", "dve", "pool"):
            self.sem[e] = nc.alloc_semaphore("s_" + e)
            self.cnt[e] = 0
        self.rings = {}
        for q in ("sp", "pool", "act"):
            keys = []
            for i in range(self.NRING):
                k = "d_%s_%d" % (q, i)
                self.sem[k] = nc.alloc_semaphore(k)
                self.cnt[k] = 0
                keys.append(k)
            self.rings[q] = [keys, 0]
        self.seen = {e: {} for e in self.eng}
        self.n_wait = 0
        self.n_ins = 0

    def _need(self, e, needs, ev):
        k, v, _ = ev
        if self.seen[e].get(k, 0) >= v:
            return
        if needs.get(k, 0) < v:
            needs[k] = v

    def _deps(self, e, reads, writes):
        needs = {}
        for r in reads:
            if r.lw is not None:
                if not (r.lw[2] == e and e == "pe" and r.lw[0] == "pe"):
                    self._need(e, needs, r.lw)
        for w in writes:
            if w.lw is not None and not (w.lw[0] == e):
                self._need(e, needs, w.lw)
            for k, (v, re_) in w.rd.items():
                if k != e:
                    self._need(e, needs, (k, v, re_))
        for k, v in needs.items():
            if not (getattr(self, "skip_pe_waits", False) and e == "pe"):
                self.eng[e].wait_ge(self.sem[k], v)
            self.seen[e][k] = v
            self.n_wait += 1

    def _commit(self, ev, reads, writes):
        k, v, e = ev
        for r in reads:
            r.rd[k] = (v, e)
        for w in writes:
            w.lw = ev
            w.rd = {}

    def op(self, e, reads, writes, fn):
        self._deps(e, reads, writes)
        ins = fn()
        self.cnt[e] += 1
        ins.then_inc(self.sem[e], 1)
        self.n_ins += 1
        self._commit((e, self.cnt[e], e), reads, writes)
        return ins

    def dma(self, q, reads, writes, out, in_, **kw):
        keys, idx = self.rings[q]
        k = keys[idx % self.NRING]
        self.rings[q][1] = idx + 1
        if self.cnt[k] > 0 and self.seen[q].get(k, 0) < self.cnt[k]:
            self.eng[q].wait_ge(self.sem[k], self.cnt[k])
            self.seen[q][k] = self.cnt[k]
        self._deps(q, reads, writes)
        ins = self.eng[q].dma_start(out=out, in_=in_, **kw)
        self.cnt[k] += 16
        ins.then_inc(self.sem[k], 16)
        self.n_ins += 1
        self._commit((k, self.cnt[k], "dma"), reads, writes)
        return ins

    def barrier(self):
        for e in self.eng:
            for k, v in self.cnt.items():
                if v > 0 and k != e and self.seen[e].get(k, 0) < v:
                    self.eng[e].wait_ge(self.sem[k], v)
                    self.seen[e][k] = v


S = 4096
DM = 1024
NTC = 8
TC = 512
PADL = 4112
LF = PADL + 4096
PADB = 127
LB = 384
ALPHA = (2 * 4) ** 0.25
DFF = 2816


class TT:
    __slots__ = ("h", "r")

    def __init__(self, h):
        self.h = h
        self.r = Res()

    def __getitem__(self, idx):
        return self.h[idx]


class DT:
    def __init__(self, h):
        self.h = h
        self.rs = {}

    def res(self, key=0):
        if key not in self.rs:
            self.rs[key] = Res()
        return self.rs[key]

    def all(self):
        return list(self.rs.values())

    def ap(self):
        return self.h.ap()


def _r(x):
    return x.r if isinstance(x, TT) else x


class XB:
    def __init__(self, tt):
        self.h = tt.h
        self.parts = [TT(tt.h) for _ in range(4)]

    def __getitem__(self, idx):
        return self.h[idx]

    def t(self, n):
        return self.parts[n // 2]

    def all(self):
        return list(self.parts)


class K:
    def __init__(self, nc):
        self.nc = nc
        self.c = Ctx(nc)
        self.es = ExitStack()
        self.ph = None
        self.uid = 0
        self.ps_all = [self._ps() for _ in range(8)]
        self.set_pools(3, 4, 1)

    def _ps(self):
        self.uid += 1
        return TT(self.es.enter_context(self.nc.psum_tensor("ps%d" % self.uid, [128, 512], F32)))

    def set_pools(self, ns, na, nm):
        assert ns + na + nm == 8
        self.ps_s = self.ps_all[0:ns]
        self.ps_a = self.ps_all[ns:ns + na]
        self.ps_m = self.ps_all[ns + na:]
        self.i_s = self.i_a = self.i_m = 0

    def pS(self):
        self.i_s += 1
        return self.ps_s[self.i_s % len(self.ps_s)]

    def pA(self):
        self.i_a += 1
        return self.ps_a[self.i_a % len(self.ps_a)]

    def pM(self):
        self.i_m += 1
        return self.ps_m[self.i_m % len(self.ps_m)]

    def begin(self):
        self.ph = ExitStack()

    def end(self):
        self.c.barrier()
        self.ph.close()
        self.ph = None
        self.set_pools(3, 4, 1)

    def sb(self, shape, dtype, glob=False, mid=None):
        self.uid += 1
        st = mid if mid is not None else (self.es if glob else self.ph)
        return TT(st.enter_context(self.nc.sbuf_tensor("t%d" % self.uid, list(shape), dtype)))

    def rot(self, n, shape, dtype):
        bufs = [self.sb(shape, dtype) for _ in range(n)]
        st = [0]

        def nxt():
            st[0] += 1
            return bufs[st[0] % n]
        return nxt

    def dram(self, shape, dtype, name=None):
        self.uid += 1
        if name is not None and name in getattr(self, "dbg", ()):
            return DT(self.nc.dram_tensor("dbg_" + name, list(shape), dtype, kind="ExternalOutput"))
        return DT(self.nc.dram_tensor("scr%d" % self.uid, list(shape), dtype))

    def op(self, e, reads, writes, fn):
        return self.c.op(e, [_r(x) for x in reads], [_r(x) for x in writes], fn)

    def ld(self, reads, writes, out, in_, q="sp", slow=False):
        kw = {"allow_slow_non_contiguous": True} if slow else {}
        return self.c.dma(q, [_r(x) for x in reads], [_r(x) for x in writes], out, in_, **kw)

    def st(self, reads, writes, out, in_, q="pool"):
        return self.c.dma(q, [_r(x) for x in reads], [_r(x) for x in writes], out, in_)

    def mm(self, out_t, out_ap, lhs_t, lhs_ap, rhs_t, rhs_ap, start, stop):
        nc = self.nc
        reads = (list(lhs_t) if isinstance(lhs_t, (list, tuple)) else [lhs_t]) + \
                (list(rhs_t) if isinstance(rhs_t, (list, tuple)) else [rhs_t])
        return self.op("pe", reads, [out_t],
                       lambda: nc.tensor.matmul(out_ap, lhsT=lhs_ap, rhs=rhs_ap, start=start, stop=stop))

    def act(self, out_t, out_ap, in_t, in_ap, func, bias=None, scale=1.0, extra=()):
        nc = self.nc
        kw = {}
        if bias is not None:
            kw["bias"] = bias
        return self.op("act", [in_t] + list(extra), [out_t],
                       lambda: nc.scalar.activation(out=out_ap, in_=in_ap, func=func, scale=scale, **kw))

    def tt(self, e, out_t, out_ap, a_t, a_ap, b_t, b_ap, op):
        eng = self.nc.vector if e == "dve" else self.nc.gpsimd
        return self.op(e, [a_t, b_t], [out_t], lambda: eng.tensor_tensor(out=out_ap, in0=a_ap, in1=b_ap, op=op))

    def ts(self, e, out_t, out_ap, a_t, a_ap, s1, op0, s2=None, op1=None, extra=()):
        eng = self.nc.vector if e == "dve" else self.nc.gpsimd
        if op1 is None:
            return self.op(e, [a_t] + list(extra), [out_t],
                           lambda: eng.tensor_scalar(out=out_ap, in0=a_ap, scalar1=s1, scalar2=None, op0=op0))
        return self.op(e, [a_t] + list(extra), [out_t],
                       lambda: eng.tensor_scalar(out=out_ap, in0=a_ap, scalar1=s1, scalar2=s2, op0=op0, op1=op1))

    def stt(self, e, out_t, out_ap, a_t, a_ap, scalar, b_t, b_ap, op0, op1, extra=()):
        eng = self.nc.vector if e == "dve" else self.nc.gpsimd
        return self.op(e, [a_t, b_t] + list(extra), [out_t],
                       lambda: eng.scalar_tensor_tensor(out=out_ap, in0=a_ap, scalar=scalar, in1=b_ap, op0=op0, op1=op1))

    def cp(self, e, out_t, out_ap, in_t, in_ap):
        nc = self.nc
        if e == "act":
            return self.op("act", [in_t], [out_t], lambda: nc.scalar.copy(out=out_ap, in_=in_ap))
        eng = nc.vector if e == "dve" else nc.gpsimd
        return self.op(e, [in_t], [out_t], lambda: eng.tensor_copy(out=out_ap, in_=in_ap))

    def memset(self, e, t, ap, val):
        eng = self.nc.vector if e == "dve" else self.nc.gpsimd
        return self.op(e, [], [t], lambda: eng.memset(ap, val))

    def recip(self, out_t, out_ap, in_t, in_ap):
        nc = self.nc
        return self.op("dve", [in_t], [out_t], lambda: nc.vector.reciprocal(out=out_ap, in_=in_ap))


def t5_bucket_np(dist):
    dist = np.asarray(dist, dtype=np.int64)
    n = np.maximum(dist, 0)
    nf = np.maximum(n, 1).astype(np.float32)
    large = 16 + (np.log(nf / np.float32(16)) / np.float32(math.log(2048 / 16)) * np.float32(16)).astype(np.int32)
    return np.where(n < 16, n, np.minimum(large, 31))


def host_consts():
    cs = {}
    oh = np.zeros((32, 4096), np.float32)
    oh[t5_bucket_np(np.arange(4096)), np.arange(4096)] = 1.0
    cs["c_ohF"] = oh
    ohb = np.zeros((3, 32, 129), np.float32)
    for p, d in enumerate((1, 4, 16)):
        idx = np.arange(129)
        ohb[p, t5_bucket_np(idx * d), idx] = 1.0
    cs["c_ohB"] = ohb
    cs["c_ident"] = np.eye(128, dtype=np.float32)
    cs["c_J"] = np.eye(128, dtype=np.float32)[::-1].copy()
    M = np.zeros((256, 64), np.float32)
    for j in range(64):
        for cc, w in ((4 * j - 1, .5), (4 * j, 1.), (4 * j + 1, 1.), (4 * j + 2, 1.), (4 * j + 3, .5)):
            if 0 <= cc < 255:
                M[cc, j] += w
    cs["c_Msel"] = M
    q = np.arange(4096)[:, None]
    jb = np.arange(64)[None, :]
    qb = q // 64
    fm = np.where(jb > qb, -1e4, 0.0) + np.where((jb == 0) | (jb == qb) | (jb == qb - 1), 1e4, 0.0)
    cs["c_Fm"] = fm.astype(np.float32)
    ex = np.zeros((64, 32, 128), np.float32)
    for i in range(32):
        for k in range(128):
            ex[2 * i + k // 64, i, k] = 1.0
    cs["c_Ex"] = ex
    sel = np.zeros((24, 24, 128), np.float32)
    for r in range(24):
        sel[r, r, :] = 1.0
    cs["c_Sel"] = sel
    half = 16
    inv = (np.float32(10000.0) ** (-np.arange(half, dtype=np.float32) / np.float32(half))).astype(np.float32)
    ang = np.arange(4096, dtype=np.float32)[None, :] * inv[:, None]
    cs["c_cos"] = np.concatenate([np.cos(ang), np.cos(ang)], 0).astype(np.float32)
    cs["c_sin"] = np.concatenate([np.sin(ang), np.sin(ang)], 0).astype(np.float32)
    return cs


CONST_SHAPES = {"c_ohF": [32, 4096], "c_ohB": [3, 32, 129], "c_ident": [128, 128], "c_J": [128, 128],
                "c_Msel": [256, 64], "c_Fm": [4096, 64], "c_Ex": [64, 32, 128], "c_Sel": [24, 24, 128],
                "c_cos": [32, 4096], "c_sin": [32, 4096]}


def setup_globals(k, rel_bias, cin):
    nc = k.nc
    g = {}
    for nm in ("ident", "J", "ones", "zeros"):
        g[nm] = k.sb([128, 128], BF16, glob=True)
    g["ones32"] = k.sb([128, 128], F32, glob=True)
    g["rsel0"] = k.sb([128, 128], F32, glob=True)
    g["rsel1"] = k.sb([128, 128], F32, glob=True)
    g["eps"] = k.sb([128, 1], F32, glob=True)
    g["b31"] = k.sb([128, 16], F32, glob=True)
    k.begin()
    st32r = k.rot(2, [128, 128], F32)
    for nm in ("ident", "J"):
        st32 = st32r()
        k.ld([], [st32], st32[:], cin["c_" + nm].ap())
        k.cp("dve", g[nm], g[nm][:], st32, st32[:])
    k.memset("pool", g["ones"], g["ones"][:], 1.0)
    k.memset("pool", g["ones32"], g["ones32"][:], 1.0)
    k.memset("pool", g["zeros"], g["zeros"][:], 0.0)
    k.memset("pool", g["rsel0"], g["rsel0"][:], 0.0)
    k.memset("pool", g["rsel1"], g["rsel1"][:], 0.0)
    k.memset("pool", g["rsel0"], g["rsel0"][64:65, :], 1.0)
    k.memset("pool", g["rsel1"], g["rsel1"][0:1, :], 1.0)
    k.memset("pool", g["eps"], g["eps"][:], 1e-5)
    k.ld([], [g["b31"]], g["b31"][:], bass.AP(rel_bias, 31 * 16, [[0, 128], [1, 16]]))
    tbl = k.sb([32, 16], F32)
    k.ld([], [tbl], tbl[:], rel_bias.ap())
    oh = k.sb([32, 4096], F32)
    k.ld([], [oh], oh[:], cin["c_ohF"].ap())
    stg = k.sb([16, LF], BF16)
    k.memset("pool", stg, stg[:], 0.0)
    stl = k.sb([16, LF], BF16)
    k.memset("pool", stl, stl[:], -30000.0)
    for n in range(8):
        ps = k.pM()
        k.mm(ps, ps[0:16, :], tbl, tbl[:], oh, oh[:, n * 512:(n + 1) * 512], True, True)
        k.act(stg, stg[:, PADL + n * 512:PADL + (n + 1) * 512], ps, ps[0:16, :], AF.Exp)
        k.act(stl, stl[:, PADL + n * 512:PADL + (n + 1) * 512], ps, ps[0:16, :], AF.Copy, scale=8.0)
    vecF = k.dram([16, LF], BF16)
    k.st([stg], [vecF.res()], vecF.ap(), stg[:])
    vecFl = k.dram([16, LF], BF16)
    k.st([stl], [vecFl.res()], vecFl.ap(), stl[:])
    stw = k.sb([16, LF], BF16)
    k.memset("pool", stw, stw[:], -30000.0)
    k.cp("dve", stw, stw[:, PADL:PADL + 512], stl, stl[:, PADL:PADL + 512])
    vecW = k.dram([16, LF], BF16)
    k.st([stw], [vecW.res()], vecW.ap(), stw[:])
    stm = k.sb([16, LF], BF16)
    k.memset("pool", stm, stm[:], -30000.0)
    k.memset("pool", stm, stm[:, PADL:], 0.0)
    vecM = k.dram([16, LF], BF16)
    k.st([stm], [vecM.res()], vecM.ap(), stm[:])
    g["vecF"], g["vecW"], g["vecM"] = vecF, vecW, vecM
    vecB = []
    for p in range(3):
        ohb = k.sb([32, 129], F32)
        k.ld([], [ohb], ohb[:], cin["c_ohB"].ap()[p])
        sb_ = k.sb([16, LB], BF16)
        k.memset("pool", sb_, sb_[:], 0.0)
        ps = k.pM()
        k.mm(ps, ps[0:16, 0:129], tbl, tbl[:], ohb, ohb[:], True, True)
        k.act(sb_, sb_[:, PADB:PADB + 129], ps, ps[0:16, 0:129], AF.Exp)
        vb = k.dram([16, LB], BF16)
        k.st([sb_], [vb.res()], vb.ap(), sb_[:])
        vecB.append(vb)
    g["vecB"] = vecB
    EF = k.dram([8, 16, 128, TC], BF16)
    EW = k.dram([8, 8, 128, TC], BF16)
    EC = k.dram([8, 16, 128, TC], BF16)
    EM = k.dram([4, 128, TC], BF16)
    hrot = k.rot(4, [128, TC], BF16)
    est = k.rot(4, [128, TC], BF16)
    jobs = []
    for h in range(8):
        for dl in range(-3, 13):
            jobs.append((vecFl, h, toep_off(dl), 1, EF, EF.ap()[h, dl + 3]))
        for dl in range(-3, 5):
            jobs.append((vecW, h, toep_off(dl), 1, EW, EW.ap()[h, dl + 3]))
        for j in range(8):
            for cbk in range(2):
                jobs.append((vecF, h, PADL - 31 + 512 * j - 2048 * cbk - 2032, 16, EC, EC.ap()[h, 2 * j + cbk]))
    for dl in range(-3, 1):
        jobs.append((vecM, 0, toep_off(dl), 1, EM, EM.ap()[dl + 3]))
    for ji, (vec, row, off, pstep, dst, dap) in enumerate(jobs):
        H = hrot()
        k.ld([vec.res()], [H], H[:], bass.AP(vec.h, row * LF + off, [[pstep, 128], [1, TC]]))
        ps = k.pA()
        k.mm(ps, ps[:], g["J"], g["J"][:], H, H[:], True, True)
        e_ = est()
        k.cp("act" if ji % 2 else "dve", e_, e_[:], ps, ps[:])
        k.st([e_], [dst.res(ji)], dap, e_[:])
    g["EF"], g["EW"], g["EC"], g["EM"] = EF, EW, EC, EM
    exs = k.sb([64, 32, 128], F32)
    exb = k.sb([64, 32, 128], BF16)
    k.ld([], [exs], exs[:], cin["c_Ex"].ap())
    k.cp("pool", exb, exb[:], exs, exs[:])
    exbf = k.dram([64, 32, 128], BF16)
    k.st([exb], [exbf.res()], exbf.ap(), exb[:])
    g["exbf"] = exbf
    k.end()
    return g


def load_E(k, g, dst, dst_ap, vec, row, L, off, pstep, W, hrot):
    H = hrot()
    k.ld([vec.res()], [H], H[:, 0:W], bass.AP(vec.h, row * L + off, [[pstep, 128], [1, W]]))
    ps = k.pM()
    k.mm(ps, ps[:, 0:W], g["J"], g["J"][:], H, H[:, 0:W], True, True)
    k.cp("act", dst, dst_ap, ps, ps[:, 0:W])


def toep_off(delta):
    return PADL + 128 * delta - 127


def load_xb(k, xin, xb_tt):
    xb = XB(xb_tt)
    if getattr(xin, "bf", None) is not None:
        xv = xin.bf.ap().rearrange("(kc p) t -> p kc t", p=128)
        for n in range(4):
            k.ld(xin.bf.all(), [xb.parts[n]], xb[:, :, n * 1024:(n + 1) * 1024], xv[:, :, n * 1024:(n + 1) * 1024])
        return xb
    stg = k.rot(2, [128, 1024], F32)
    xv = xin.ap().rearrange("(kc p) t -> p kc t", p=128)
    i = 0
    for q4 in range(4):
        for kc in range(8):
            s = stg()
            k.ld([xin.res(kc)], [s], s[:], xv[:, kc, q4 * 1024:(q4 + 1) * 1024])
            k.cp("dve" if i % 2 == 0 else "act", xb.parts[q4], xb[:, kc, q4 * 1024:(q4 + 1) * 1024], s, s[:])
            i += 1
    return xb


def load_w_bf16(k, w_ap, nk, ncols, dst, dst_ap, stg_rot, eng="pool"):
    s = stg_rot()
    k.ld([], [s], s[:, 0:nk, 0:ncols], w_ap.rearrange("(kc p) m -> p kc m", p=128))
    k.cp(eng, dst, dst_ap, s, s[:, 0:nk, 0:ncols])


def ln_block(k, g, z, nchunk, gam, bet, dst_fn):
    nc = k.nc
    zb = k.ln_zb()
    sq = k.ln_sq()
    s1 = k.pA()
    s2 = k.pA()
    for m in range(nchunk):
        k.cp("act", zb, zb[:, m, :], z, z[:, m, :])
        k.op("act", [z], [sq], lambda m=m: nc.scalar.activation(out=sq[:, m, :], in_=z[:, m, :], func=AF.Square))
    for m in range(nchunk):
        k.mm(s1, s1[:], g["ones"], g["ones"][:], zb, zb[:, m, :], m == 0, m == nchunk - 1)
    for m in range(nchunk):
        k.mm(s2, s2[:], g["ones"], g["ones"][:], sq, sq[:, m, :], m == 0, m == nchunk - 1)
    nf = float(nchunk * 128)
    mean = k.ln_s()
    k.op("act", [s1], [mean], lambda: nc.scalar.mul(out=mean[:], in_=s1[:], mul=1.0 / nf))
    msq = k.ln_s()
    k.tt("dve", msq, msq[:], mean, mean[:], mean, mean[:], ALU.mult)
    var = k.ln_s()
    k.stt("dve", var, var[:], s2, s2[:], 1.0 / nf, msq, msq[:], ALU.mult, ALU.subtract)
    sd = k.ln_s()
    k.act(sd, sd[:], var, var[:], AF.Sqrt, bias=g["eps"][:], extra=[g["eps"]])
    rstd = k.ln_s()
    k.recip(rstd, rstd[:], sd, sd[:])
    if hasattr(k, "ln_dump"):
        for nm_, t_ in (("mean", mean), ("msq", msq), ("var", var), ("sd", sd), ("rstd", rstd)):
            k.ln_dump(nm_, t_)
    for m in range(nchunk):
        t = k.ln_t()
        k.tt("dve", t, t[:], z, z[:, m, :], mean, mean[:], ALU.subtract)
        t2 = k.ln_t()
        k.tt("pool", t2, t2[:], t, t[:], rstd, rstd[:], ALU.mult)
        o, oap = dst_fn(m)
        k.op("act", [t2, gam, bet], [o],
             lambda m=m, t2=t2, oap=oap: nc.scalar.activation(out=oap, in_=t2[:], func=AF.Identity,
                                                              scale=gam[:, m:m + 1], bias=bet[:, m:m + 1]))


def proj_resid_ln(k, g, yT_fn, nk, w_ap, xres, gam_ap, bet_ap, xout, w_pre=None):
    nc = k.nc
    if w_pre is not None:
        w = w_pre
    else:
        w = k.sb([128, nk, DM], BF16)
        wst = k.rot(1, [128, nk, 128], F32)
        for cc in range(8):
            load_w_bf16(k, w_ap[:, cc * 128:(cc + 1) * 128], nk, 128, w, w[:, :, cc * 128:(cc + 1) * 128], wst,
                        "pool" if cc % 2 else "dve")
    gam = k.sb([128, 8], F32)
    bet = k.sb([128, 8], F32)
    k.ld([], [gam], gam[:], gam_ap.rearrange("(m p) -> p m", p=128), slow=True)
    k.ld([], [bet], bet[:], bet_ap.rearrange("(m p) -> p m", p=128), slow=True)
    xr = k.rot(3, [128, 8, TC], F32)
    k.ln_sq = k.rot(1, [128, 8, TC], BF16)
    k.ln_zb = k.rot(1, [128, 8, TC], BF16)
    k.ln_t = k.rot(4, [128, TC], F32)
    k.ln_s = k.rot(5, [128, TC], F32)
    ost = k.rot(1, [128, 8, TC], F32)
    obt = k.rot(1, [128, 8, TC], BF16)
    xv = xres.ap().rearrange("(m p) t -> p m t", p=128)
    ov = xout.ap().rearrange("(m p) t -> p m t", p=128)
    obv = xout.bf.ap().rearrange("(m p) t -> p m t", p=128) if getattr(xout, "bf", None) is not None else None
    pend = Pend(1)
    for n in range(NTC):
        yt, yap = yT_fn(n)
        xt = xr()
        k.ld(xres.all(), [xt], xt[:], xv[:, :, n * TC:(n + 1) * TC])
        z = xt
        for m in range(8):
            ps = k.pS()
            for kc in range(nk):
                k.mm(ps, ps[:], w, w[:, kc, m * 128:(m + 1) * 128], yt, yap(kc), kc == 0, kc == nk - 1)
            k.stt("dve", z, z[:, m, :], xt, xt[:, m, :], ALPHA, ps, ps[:], ALU.mult, ALU.add)

        def fin(z=z, n=n):
            o = ost()
            ln_block(k, g, z, 8, gam, bet, lambda m, o=o: (o, o[:, m, :]))
            k.st([o], [xout.res(kc) for kc in range(8)], ov[:, :, n * TC:(n + 1) * TC], o[:])
            if obv is not None:
                ob = obt()
                k.cp("pool", ob, ob[:], o, o[:])
                k.st([ob], [xout.bf.res(n)], obv[:, :, n * TC:(n + 1) * TC], ob[:])
            return None
        pend.push(fin)
    pend.flush()


def ffn_phase(k, g, x1, w_up, conv_w, conv_b, w_down, ln_g, ln_b, xout):
    nc = k.nc
    hT = k.dram([DFF, S], BF16, "hT")
    mid = ExitStack()
    wdn = k.sb([128, 22, DM], BF16, mid=mid)
    k.begin()
    wdst = k.rot(1, [128, 22, 128], F32)
    xb = load_xb(k, x1, k.sb([128, 8, S], BF16))
    cw = k.sb([128, 3, 22], F32)
    k.ld([], [cw], cw[:], conv_w.rearrange("j (c p) -> p j c", p=128), slow=True)
    cb = k.sb([128, 22], F32)
    k.ld([], [cb], cb[:], conv_b.rearrange("(c p) -> p c", p=128), slow=True)
    wst = k.rot(3, [128, 8, 128], F32)
    wa_r = k.rot(2, [128, 8, 128], BF16)
    wg_r = k.rot(2, [128, 8, 128], BF16)
    gb_r = k.rot(2, [128, S + 2], F32)
    for _ in range(2):
        gb = gb_r()
        k.memset("pool", gb, gb[:, 0:2], 0.0)
    t_r = k.rot(5, [128, TC], F32)
    hst_r = k.rot(2, [128, S], BF16)

    def wload(cc):
        wa = wa_r()
        wg = wg_r()
        load_w_bf16(k, w_up[:, cc * 128:(cc + 1) * 128], 8, 128, wa, wa[:], wst, "pool")
        load_w_bf16(k, w_up[:, DFF + cc * 128:DFF + (cc + 1) * 128], 8, 128, wg, wg[:], wst, "pool")
        return wa, wg
    wcur = wload(0)
    for cc in range(22):
        wa, wg = wcur
        if cc + 1 < 22:
            wcur = wload(cc + 1)
        hst = hst_r()
        gb = gb_r()
        if cc % 2 == 1 and cc // 2 < 8:
            c8 = cc // 2
            load_w_bf16(k, w_down[:, c8 * 128:(c8 + 1) * 128], 22, 128, wdn, wdn[:, :, c8 * 128:(c8 + 1) * 128], wdst, "act")
        for n in range(NTC):
            pa = k.pS()
            pg = k.pA()
            for kc in range(8):
                k.mm(pg, pg[:], wg, wg[:, kc, :], xb.t(n), xb[:, kc, n * TC:(n + 1) * TC], kc == 0, kc == 7)
            for kc in range(8):
                k.mm(pa, pa[:], wa, wa[:, kc, :], xb.t(n), xb[:, kc, n * TC:(n + 1) * TC], kc == 0, kc == 7)
            o = 2 + n * TC
            k.cp("act", gb, gb[:, o:o + TC], pg, pg[:])
            t1 = t_r()
            k.op("act", [pg, cw, cb], [t1],
                 lambda t1=t1, pg=pg, cc=cc: nc.scalar.activation(out=t1[:], in_=pg[:], func=AF.Identity,
                                                                 scale=cw[:, 2, cc:cc + 1], bias=cb[:, cc:cc + 1]))
            t2 = t_r()
            k.stt("dve", t2, t2[:], gb, gb[:, o - 1:o - 1 + TC], cw[:, 1, cc:cc + 1], t1, t1[:], ALU.mult, ALU.add, extra=[cw])
            t3 = t_r()
            k.stt("dve", t3, t3[:], gb, gb[:, o - 2:o - 2 + TC], cw[:, 0, cc:cc + 1], t2, t2[:], ALU.mult, ALU.add, extra=[cw])
            t4 = t_r()
            k.act(t4, t4[:], t3, t3[:], AF.Gelu_apprx_tanh)
            k.tt("dve", hst, hst[:, n * TC:(n + 1) * TC], t4, t4[:], pa, pa[:], ALU.mult)
        k.st([hst], [hT.res(cc)], hT.ap()[cc * 128:(cc + 1) * 128, :], hst[:])
    k.end()
    k.begin()
    hr = k.rot(2, [128, 22, TC], BF16)
    hv = hT.ap().rearrange("(c p) t -> p c t", p=128)

    def yT_fn(n):
        h = hr()
        k.ld(hT.all(), [h], h[:], hv[:, :, n * TC:(n + 1) * TC])
        return h, (lambda kc, h=h: h[:, kc, :])
    proj_resid_ln(k, g, yT_fn, 22, w_down, x1, ln_g, ln_b, xout, w_pre=wdn)
    k.end()
    mid.close()


def live_cols(dl, win=False):
    if dl < 0:
        return -dl * 128, TC
    if win and dl == 4:
        return 0, 128
    return 0, TC


def ssl(t0, n, d):
    return slice(t0, t0 + (n - 1) * d + 1, d)


class Pend:
    def __init__(self, depth=1):
        self.q = []
        self.depth = depth

    def _run(self, fn):
        r = fn()
        if callable(r):
            self.q.append(r)

    def push(self, fn):
        self.q.append(fn)
        while len(self.q) > self.depth:
            self._run(self.q.pop(0))

    def flush(self):
        while self.q:
            self._run(self.q.pop(0))


def proj_fm_load(k, w_in, col0, ncols, wst, wbf):
    w = wbf()
    load_w_bf16(k, w_in[:, col0:col0 + ncols], 8, ncols, w, w[:, :, 0:ncols], wst, "pool")
    return w


def proj_fm_compute(k, xb, w, ncols, out_dt, row0, stg_r, key):
    stg = stg_r()
    for n in range(NTC):
        ps = k.pS()
        for kc in range(8):
            k.mm(ps, ps[0:ncols, :], w, w[:, kc, 0:ncols], xb.t(n), xb[:, kc, n * TC:(n + 1) * TC], kc == 0, kc == 7)
        k.cp("act" if n % 2 == 0 else "dve", stg, stg[0:ncols, n * TC:(n + 1) * TC], ps, ps[0:ncols, :])
    k.st([stg], [out_dt.res(key)], out_dt.ap()[row0:row0 + ncols, :], stg[0:ncols, :])


def proj_fm_all(k, xb, w_in, cols, out_dt, wst, wbf, stg_r):
    w = proj_fm_load(k, w_in, cols[0], 128, wst, wbf)
    for ci, c0 in enumerate(cols):
        wn = proj_fm_load(k, w_in, cols[ci + 1], 128, wst, wbf) if ci + 1 < len(cols) else None
        proj_fm_compute(k, xb, w, 128, out_dt, ci * 128, stg_r, ci)
        w = wn


def even_proj(k, g, xin, w_in):
    hT = k.dram([2560, S], BF16, "ev_hT")
    vA = k.dram([S, 512], BF16, "ev_vA")
    vB = k.dram([3, 32, 128, 256], BF16, "ev_vB")
    k.begin()
    xb = load_xb(k, xin, k.sb([128, 8, S], BF16))
    wst = k.rot(2, [128, 8, 512], F32)
    wbf = k.rot(2, [128, 8, 128], BF16)
    stg_r = k.rot(2, [128, S], BF16)
    cols = list(range(0, 1024, 128))
    for p in range(3):
        base = 1536 + p * 768
        cols += [base, base + 128, base + 256, base + 384]
    proj_fm_all(k, xb, w_in, cols, hT, wst, wbf, stg_r)
    wv = k.sb([128, 8, 512], BF16)
    load_w_bf16(k, w_in[:, 1024:1536], 8, 512, wv, wv[:], wst, "pool")
    vst = k.rot(3, [128, 512], BF16)
    for b in range(32):
        ps = k.pS()
        for kc in range(8):
            k.mm(ps, ps[:], xb.t(b // 4), xb[:, kc, b * 128:(b + 1) * 128], wv, wv[:, kc, :], kc == 0, kc == 7)
        s = vst()
        k.cp("act" if b % 2 == 0 else "dve", s, s[:], ps, ps[:])
        k.st([s], [vA.res(b)], vA.ap()[b * 128:(b + 1) * 128, :], s[:])
    for p, d in enumerate((1, 4, 16)):
        base = 1536 + p * 768 + 512
        load_w_bf16(k, w_in[:, base:base + 256], 8, 256, wv, wv[:, :, 0:256], wst, "pool")
        nb = 32 // d
        for r in range(d):
            for b in range(nb):
                t0 = r + d * 128 * b
                ps = k.pS()
                for kc in range(8):
                    k.mm(ps, ps[:, 0:256], xb.all(), xb[:, kc, ssl(t0, 128, d)], wv, wv[:, kc, 0:256], kc == 0, kc == 7)
                s = vst()
                k.cp("act" if b % 2 == 0 else "dve", s, s[:, 0:256], ps, ps[:, 0:256])
                k.st([s], [vB.res((p, r * nb + b))], vB.ap()[p, r * nb + b], s[:, 0:256])
    k.end()
    return hT, vA, vB


def attn_A(k, g, hT, vA, yT, lam_p, subln, layer_idx):
    nc = k.nc
    lam_init = 0.8 - 0.6 * math.exp(-0.3 * layer_idx)
    k.begin()
    lpb = k.sb([128, 256], F32)
    k.ld([], [lpb], lpb[:], bass.AP(lam_p.tensor, lam_p.offset, [[0, 128], [1, 256]]))
    pr = k.sb([128, 128], F32)
    k.tt("dve", pr, pr[:, 0:64], lpb, lpb[:, 0:64], lpb, lpb[:, 64:128], ALU.mult)
    k.tt("dve", pr, pr[:, 64:128], lpb, lpb[:, 128:192], lpb, lpb[:, 192:256], ALU.mult)
    sm = k.sb([128, 2], F32)
    k.op("dve", [pr], [sm], lambda: nc.vector.reduce_sum(out=sm[:, 0:1], in_=pr[:, 0:64], axis=AX.X))
    k.op("dve", [pr], [sm], lambda: nc.vector.reduce_sum(out=sm[:, 1:2], in_=pr[:, 64:128], axis=AX.X))
    ex = k.sb([128, 2], F32)
    k.act(ex, ex[:], sm, sm[:], AF.Exp)
    neglam = k.sb([128, 1], F32)
    k.stt("dve", neglam, neglam[:], ex, ex[:, 1:2], -lam_init, ex, ex[:, 0:1], ALU.add, ALU.subtract)
    gsc = k.sb([128, 1], F32)
    k.ld([], [gsc], gsc[:], subln.rearrange("(p o) -> p o", o=1), slow=True)
    k.ts("dve", gsc, gsc[:], gsc, gsc[:], 1.0 - lam_init, ALU.mult)
    eps = g["eps"]
    Eh = k.sb([128, 16, TC], BF16)
    hrot = k.rot(2, [128, TC], BF16)
    vt_r = k.rot(2, [128, 32, 128], BF16)
    qk_r = k.rot(4, [128, S], BF16)
    for _ in range(4):
        t_ = qk_r()
        k.memset("pool", t_, t_[64:128, :], 0.0)
    p_r = k.rot(6, [128, TC], BF16)
    f_r = k.rot(12, [128, TC], F32)
    om_r = k.rot(6, [128, TC], F32)
    yst_r = k.rot(2, [128, S], BF16)
    vAv = vA.ap().rearrange("(b p) c -> p b c", p=128)
    pend = Pend(2)
    tile_i = 0
    for h in range(4):
        k.ld(g["EF"].all(), [Eh], Eh[:], g["EF"].ap()[h].rearrange("d p q -> p d q"))
        vt = vt_r()
        k.ld(vA.all(), [vt], vt[:], vAv[:, :, h * 128:(h + 1) * 128])
        QK = []
        for m in range(2):
            QT = qk_r()
            KT = qk_r()
            rq = m * 256 + h * 64
            k.ld(hT.all(), [QT], QT[0:64, :], hT.ap()[rq:rq + 64, :])
            k.ld(hT.all(), [KT], KT[0:64, :], hT.ap()[512 + rq:512 + rq + 64, :])
            QK.append((QT, KT))
        yst = yst_r()
        for j in range(NTC):
            oms = []
            for m in range(2):
                QT, KT = QK[m]
                O = k.pA()
                Dn = k.pA()
                nblk = 4 * j + 4
                om = om_r()
                oms.append(om)
                for i in range(nblk):
                    dl = 4 * j - i
                    near = dl < 13
                    c0, c1 = live_cols(dl)
                    ps = k.pS()
                    k.mm(ps, ps[:, c0:c1], KT, KT[:, i * 128:(i + 1) * 128], QT, QT[:, j * TC + c0:j * TC + c1], True, not near)
                    P = p_r()
                    if near:
                        k.mm(ps, ps[:, c0:c1], g["ident"], g["ident"][:], Eh, Eh[:, dl + 3, c0:c1], False, True)
                        k.act(P, P[:, c0:c1], ps, ps[:, c0:c1], AF.Exp, scale=0.125)
                    else:
                        k.act(P, P[:], ps, ps[:], AF.Exp, bias=g["b31"][:, h:h + 1], scale=0.125, extra=[g["b31"]])

                    def fin(P=P, i=i, O=O, Dn=Dn, nblk=nblk, om=om, m=m, j=j, oms=oms, yst=yst, vt=vt, c0=c0, c1=c1):
                        k.mm(O, O[:, c0:c1], vt, vt[:, i, :], P, P[:, c0:c1], i == 0, i == nblk - 1)
                        k.mm(Dn, Dn[:, c0:c1], g["ones"], g["ones"][:], P, P[:, c0:c1], i == 0, i == nblk - 1)
                        if i < nblk - 1:
                            return None

                        def evac1():
                            rd = f_r()
                            k.recip(rd, rd[:], Dn, Dn[:])
                            k.tt("dve", om, om[:], O, O[:], rd, rd[:], ALU.mult)
                            if m == 0:
                                return None
                            o = f_r()
                            k.stt("dve", o, o[:], oms[1], oms[1][:], neglam[:, 0:1], oms[0], oms[0][:], ALU.mult, ALU.add,
                                  extra=[neglam])
                            sq = f_r()
                            k.act(sq, sq[:], o, o[:], AF.Square)

                            def evac2():
                                ss = k.pM()
                                k.mm(ss, ss[:], g["ones32"], g["ones32"][:], sq, sq[:], True, True)
                                sd = f_r()
                                k.act(sd, sd[:], ss, ss[:], AF.Sqrt, bias=eps[:], scale=1.0 / 128.0, extra=[eps])
                                rs = f_r()
                                k.recip(rs, rs[:], sd, sd[:])
                                k.stt("dve", yst, yst[:, j * TC:(j + 1) * TC], o, o[:], gsc[:, 0:1], rs, rs[:], ALU.mult,
                                      ALU.mult, extra=[gsc])
                                return None
                            return evac2
                        return evac1
                    pend.push(fin)
        pend.flush()
        k.st([yst], [yT.res(h)], yT.ap()[h * 128:(h + 1) * 128, :], yst[:])
    k.end()


def attn_B(k, g, hT, vB, yT):
    nc = k.nc
    k.begin()
    accO = k.sb([128, S], F32)
    accD = k.sb([128, S], F32)
    Vp_r = k.rot(1, [128, 32, 256], BF16)
    qk_r = k.rot(4, [128, S], BF16)
    for _ in range(4):
        t_ = qk_r()
        k.memset("pool", t_, t_[64:128, :], 0.0)
    E_r = k.rot(4, [128, 4, 128], BF16)
    hrot = k.rot(2, [128, TC], BF16)
    p0_r = k.rot(4, [128, TC], BF16)
    p_r = k.rot(4, [128, TC], BF16)
    yst_r = k.rot(1, [128, S], BF16)
    rd_r = k.rot(2, [128, TC], F32)
    pend = Pend()
    for hp in range(2):
        for p, d in enumerate((1, 4, 16)):
            nb = 32 // d
            G = min(4, nb)
            W = G * 128
            Vp = Vp_r()
            k.ld(vB.all(), [Vp], Vp[:], vB.ap()[p].rearrange("b t c -> t b c"))
            for hh in range(2):
                h = 2 * hp + hh
                R0 = hh * 64
                QT = qk_r()
                KT = qk_r()
                rq = 1024 + p * 512 + h * 64
                k.ld(hT.all(), [QT], QT[0:64, :], hT.ap()[rq:rq + 64, :])
                k.ld(hT.all(), [KT], KT[0:64, :], hT.ap()[rq + 256:rq + 320, :])
                Es = E_r()
                Ep = E_r()
                for Et, dl in ((Es, 0), (Ep, 1)):
                    load_E(k, g, Et, Et[:, 0, :], g["vecB"][p], 4 + 4 * p + h, LB, 128 * dl, 1, 128, hrot)
                    for gi in range(1, 4):
                        k.cp("pool", Et, Et[:, gi, :], Et, Et[:, 0, :])
                for r in range(d):
                    for b0 in range(0, nb, G):
                        def blk(b):
                            t0 = r + d * 128 * b
                            return ssl(t0, 128, d)
                        Ss = k.pS()
                        Sp = k.pS()
                        for gi in range(G):
                            b = b0 + gi
                            k.mm(Ss, Ss[:, gi * 128:(gi + 1) * 128], KT, KT[:, blk(b)], QT, QT[:, blk(b)], True, True)
                        for gi in range(G):
                            b = b0 + gi
                            if b >= 1:
                                k.mm(Sp, Sp[:, gi * 128:(gi + 1) * 128], KT, KT[:, blk(b - 1)], QT, QT[:, blk(b)], True, True)
                        c0 = 128 if b0 == 0 else 0
                        Ps0 = p0_r()
                        Ps = p_r()
                        k.act(Ps0, Ps0[:, 0:W], Ss, Ss[:, 0:W], AF.Exp, scale=0.125)
                        k.tt("dve", Ps, Ps[:, 0:W], Ps0, Ps0[:, 0:W], Es, Es[:, 0:G, :], ALU.mult)
                        Pp = None
                        if W > c0:
                            Pp0 = p0_r()
                            Pp = p_r()
                            k.act(Pp0, Pp0[:, c0:W], Sp, Sp[:, c0:W], AF.Exp, scale=0.125)
                            k.tt("pool", Pp, Pp[:, c0:W], Pp0, Pp0[:, c0:W], Ep, Ep[:, c0 // 128:G, :], ALU.mult)

                        def fin(Ps=Ps, Pp=Pp, b0=b0, r=r, d=d, nb=nb, G=G, W=W, Vp=Vp, hp=hp, p=p, R0=R0):
                            O = k.pA()
                            Dn = k.pA()
                            for gi in range(G):
                                b = b0 + gi
                                cs = slice(gi * 128, (gi + 1) * 128)
                                vs = Vp[:, r * nb + b, hp * 128:(hp + 1) * 128]
                                k.mm(O, O[:, cs], Vp, vs, Ps, Ps[:, cs], True, b == 0)
                                if b >= 1:
                                    vp_ = Vp[:, r * nb + b - 1, hp * 128:(hp + 1) * 128]
                                    k.mm(O, O[:, cs], Vp, vp_, Pp, Pp[:, cs], False, True)
                            for gi in range(G):
                                b = b0 + gi
                                cs = slice(gi * 128, (gi + 1) * 128)
                                k.mm(Dn, Dn[:, cs], g["ones"], g["ones"][:], Ps, Ps[:, cs], True, b == 0)
                                if b >= 1:
                                    k.mm(Dn, Dn[:, cs], g["ones"], g["ones"][:], Pp, Pp[:, cs], False, True)
                            t0 = r + d * 128 * b0
                            tsl = ssl(t0, W, d)
                            rows = slice(R0, R0 + 64)
                            if p == 0:
                                k.cp("act", accO, accO[rows, tsl], O, O[rows, 0:W])
                                k.cp("dve", accD, accD[rows, tsl], Dn, Dn[rows, 0:W])
                            else:
                                k.tt("dve", accO, accO[rows, tsl], accO, accO[rows, tsl], O, O[rows, 0:W], ALU.add)
                                k.tt("dve", accD, accD[rows, tsl], accD, accD[rows, tsl], Dn, Dn[rows, 0:W], ALU.add)
                        pend.push(fin)
                pend.flush()
        yst = yst_r()
        for n in range(NTC):
            rd = rd_r()
            k.recip(rd, rd[:], accD, accD[:, n * TC:(n + 1) * TC])
            k.tt("dve", yst, yst[:, n * TC:(n + 1) * TC], accO, accO[:, n * TC:(n + 1) * TC], rd, rd[:], ALU.mult)
        k.st([yst], [yT.res(4 + hp)], yT.ap()[512 + hp * 128:512 + (hp + 1) * 128, :], yst[:])
    k.end()


def out_phase(k, g, yT, nk, w_out, xin, gam, bet, x1):
    k.begin()
    yr = k.rot(2, [128, nk, TC], BF16)
    yv = yT.ap().rearrange("(c p) t -> p c t", p=128)

    def yT_fn(n):
        y = yr()
        k.ld(yT.all(), [y], y[:], yv[:, :, n * TC:(n + 1) * TC])
        return y, (lambda kc, y=y: y[:, kc, :])
    proj_resid_ln(k, g, yT_fn, nk, w_out, xin, gam, bet, x1)
    k.end()


def even_layer(k, g, xin, xout, W, l):
    hT, vA, vB = even_proj(k, g, xin, W["w_in"])
    yT = k.dram([768, S], BF16, "ev_yT")
    attn_A(k, g, hT, vA, yT, W["lam"], W["subln"], l)
    attn_B(k, g, hT, vB, yT)
    x1 = k.dram([DM, S], F32, "x1")
    x1.bf = k.dram([DM, S], BF16)
    out_phase(k, g, yT, 6, W["w_out"], xin, W["ln_g0"], W["ln_b0"], x1)
    ffn_phase(k, g, x1, W["w_up"], W["conv_w"], W["conv_b"], W["w_down"], W["ln_g1"], W["ln_b1"], xout)


def rms_fm(k, g, z, nchunk, gam, out_t, f_r, eps, sq_r):
    nc = k.nc
    sq = sq_r()
    ss = k.pM()
    for m in range(nchunk):
        k.op("act", [z], [sq], lambda m=m: nc.scalar.activation(out=sq[:, m, :], in_=z[:, m, :], func=AF.Square))
    for m in range(nchunk):
        k.mm(ss, ss[:], g["ones"], g["ones"][:], sq, sq[:, m, :], m == 0, m == nchunk - 1)
    sd = f_r()
    k.act(sd, sd[:, 0, :], ss, ss[:], AF.Sqrt, bias=eps[:], scale=1.0 / (128.0 * nchunk), extra=[eps])
    k.recip(sd, sd[:, 1, :], sd, sd[:, 0, :])
    for m in range(nchunk):
        k.stt("dve", out_t, out_t[:, m, :], z, z[:, m, :], gam[:, m:m + 1], sd, sd[:, 1, :], ALU.mult, ALU.mult, extra=[gam])


def odd_proj(k, g, xin, W, cin):
    nc = k.nc
    w_in = W["w_in"]
    hT = k.dram([1024, S], BF16, "od_hT")
    vSW = k.dram([S, 256], BF16, "od_vSW")
    gateT = k.dram([24, S], F32, "od_gate")
    qdT = k.dram([8, 96, S], BF16, "od_qd")
    kdT = k.dram([8, 64, S], BF16, "od_kd")
    krr = k.dram([32, S], BF16, "od_krr")
    vD = k.dram([S, 512], BF16, "od_vD")
    k.begin()
    xb = load_xb(k, xin, k.sb([128, 8, S], BF16))
    wst = k.rot(2, [128, 8, 384], F32)
    wbf = k.rot(2, [128, 8, 128], BF16)
    stg_r = k.rot(1, [128, S], BF16)
    proj_fm_all(k, xb, w_in, [0, 128, 256, 384, 512, 640, 768, 1024], hT, wst, wbf, stg_r)
    wv2 = k.sb([128, 8, 256], BF16)
    load_w_bf16(k, w_in[:, 896:1024], 8, 128, wv2, wv2[:, :, 0:128], wst, "pool")
    load_w_bf16(k, w_in[:, 1152:1280], 8, 128, wv2, wv2[:, :, 128:256], wst, "pool")
    vst = k.rot(3, [128, 512], BF16)
    for b in range(32):
        ps = k.pS()
        for kc in range(8):
            k.mm(ps, ps[:, 0:256], xb.t(b // 4), xb[:, kc, b * 128:(b + 1) * 128], wv2, wv2[:, kc, :], kc == 0, kc == 7)
        s = vst()
        k.cp("act" if b % 2 == 0 else "dve", s, s[:, 0:256], ps, ps[:, 0:256])
        k.st([s], [vSW.res(b)], vSW.ap()[b * 128:(b + 1) * 128, :], s[:, 0:256])
    wg = wbf()
    load_w_bf16(k, w_in[:, 1280:1304], 8, 24, wg, wg[:, :, 0:24], wst, "pool")
    gst_r = k.rot(2, [24, TC], F32)
    for n in range(NTC):
        ps = k.pS()
        for kc in range(8):
            k.mm(ps, ps[0:24, :], wg, wg[:, kc, 0:24], xb.t(n), xb[:, kc, n * TC:(n + 1) * TC], kc == 0, kc == 7)
        gst = gst_r()
        k.act(gst, gst[:], ps, ps[0:24, :], AF.Sigmoid)
        k.st([gst], [gateT.res(n)], gateT.ap()[:, n * TC:(n + 1) * TC], gst[:])
    k.end()
    k.begin()
    xb = load_xb(k, xin, k.sb([128, 8, S], BF16))
    wst = k.rot(1, [128, 8, 384], F32)
    vst = k.rot(3, [128, 512], BF16)
    wcq = k.sb([128, 8, 384], BF16)
    load_w_bf16(k, w_in[:, 1304:1688], 8, 384, wcq, wcq[:], wst, "pool")
    wckv = k.sb([128, 8, 256], BF16)
    load_w_bf16(k, w_in[:, 1688:1944], 8, 256, wckv, wckv[:], wst, "pool")
    wkr = k.sb([128, 8, 96], BF16)
    wkrr = k.sb([128, 8, 96], BF16)
    k.memset("pool", wkr, wkr[:], 0.0)
    k.memset("pool", wkrr, wkrr[:], 0.0)
    s = wst()
    k.ld([], [s], s[:, :, 0:32], w_in[:, 1944:1976].rearrange("(kc p) m -> p kc m", p=128))
    k.cp("pool", wkr, wkr[:, :, 64:96], s, s[:, :, 0:32])
    k.cp("pool", wkrr, wkrr[:, :, 80:96], s, s[:, :, 0:16])
    k.ts("dve", wkrr, wkrr[:, :, 64:80], s, s[:, :, 16:32], -1.0, ALU.mult)
    wq = k.sb([128, 3, 768], BF16)
    wqr = k.sb([128, 3, 8, 96], BF16)
    k.memset("pool", wqr, wqr[:], 0.0)
    wst2 = k.rot(1, [128, 3, 768], F32)
    s = wst2()
    k.ld([], [s], s[:], W["w_uq"].rearrange("(kc p) m -> p kc m", p=128))
    k.cp("pool", wq, wq[:], s, s[:])
    for h in range(8):
        k.cp("pool", wqr, wqr[:, :, h, 80:96], s, s[:, :, h * 96 + 64:h * 96 + 80])
        k.ts("dve", wqr, wqr[:, :, h, 64:80], s, s[:, :, h * 96 + 80:h * 96 + 96], -1.0, ALU.mult)
    wk = k.sb([128, 2, 512], BF16)
    wv = k.sb([128, 2, 512], BF16)
    for wt, nm in ((wk, "w_uk"), (wv, "w_uv")):
        s = wst2()
        k.ld([], [s], s[:, 0:2, 0:512], W[nm].rearrange("(kc p) m -> p kc m", p=128))
        k.cp("pool", wt, wt[:], s, s[:, 0:2, 0:512])
    gq = k.sb([128, 3], F32)
    k.ld([], [gq], gq[:], W["q_norm"].rearrange("(m p) -> p m", p=128), slow=True)
    gkv = k.sb([128, 2], F32)
    k.ld([], [gkv], gkv[:], W["kv_norm"].rearrange("(m p) -> p m", p=128), slow=True)
    cs_r = k.rot(3, [96, 2, TC], F32)
    z_r = k.rot(2, [128, 3, TC], F32)
    f_r = k.rot(2, [128, 2, TC], F32)
    sq_r = k.rot(2, [128, 3, TC], BF16)
    cqn_r = k.rot(3, [128, 3, TC], BF16)
    cn_r = k.rot(3, [128, 2, TC], BF16)
    t_r = k.rot(4, [96, TC], F32)
    qst_r = k.rot(3, [96, TC], BF16)
    kst_r = k.rot(3, [64, TC], BF16)
    eps = g["eps"]
    k.set_pools(3, 3, 2)
    pend = Pend(1)

    def rope_from(psA, psB, dst, dst_ap, cst):
        t = t_r()
        u = t_r()
        k.tt("dve", t, t[64:96, :], psA, psA[64:96, :], cst, cst[64:96, 0, :], ALU.mult)
        k.tt("dve", u, u[64:96, :], psB, psB[64:96, :], cst, cst[64:96, 1, :], ALU.mult)
        k.tt("pool", dst, dst_ap, t, t[64:96, :], u, u[64:96, :], ALU.add)

    def heads_and_v(n, tsl, cqn, cn, cst):
        for h in range(8):
            psA = k.pA()
            psB = k.pA()
            for kc in range(3):
                k.mm(psA, psA[0:96, :], wq, wq[:, kc, h * 96:(h + 1) * 96], cqn, cqn[:, kc, :], kc == 0, kc == 2)
            for kc in range(3):
                k.mm(psB, psB[0:96, :], wqr, wqr[:, kc, h, :], cqn, cqn[:, kc, :], kc == 0, kc == 2)
            qs = qst_r()
            k.cp("act", qs, qs[0:64, :], psA, psA[0:64, :])
            rope_from(psA, psB, qs, qs[64:96, :], cst)
            k.st([qs], [qdT.res((h, n))], qdT.ap()[h][:, tsl], qs[:])
            ps = k.pS()
            for kc in range(2):
                k.mm(ps, ps[0:64, :], wk, wk[:, kc, h * 64:(h + 1) * 64], cn, cn[:, kc, :], kc == 0, kc == 1)
            ks = kst_r()
            k.cp("act", ks, ks[:], ps, ps[0:64, :])
            k.st([ks], [kdT.res((h, n))], kdT.ap()[h][:, tsl], ks[:])
        for bb in range(4):
            ps = k.pS()
            for kc in range(2):
                k.mm(ps, ps[:], cn, cn[:, kc, bb * 128:(bb + 1) * 128], wv, wv[:, kc, :], kc == 0, kc == 1)
            s = vst()
            k.cp("dve", s, s[:], ps, ps[:])
            b = n * 4 + bb
            k.st([s], [vD.res(b)], vD.ap()[b * 128:(b + 1) * 128, :], s[:])

    for n in range(NTC):
        tsl = slice(n * TC, (n + 1) * TC)
        cst = cs_r()
        k.ld([], [cst], cst[64:96, 0, :], cin["c_cos"].ap()[:, tsl])
        k.ld([], [cst], cst[64:96, 1, :], cin["c_sin"].ap()[:, tsl])
        z = z_r()
        for m in range(3):
            ps = k.pS()
            for kc in range(8):
                k.mm(ps, ps[:], wcq, wcq[:, kc, m * 128:(m + 1) * 128], xb.t(n), xb[:, kc, tsl], kc == 0, kc == 7)
            k.cp("act", z, z[:, m, :], ps, ps[:])
        cqn = cqn_r()
        rms_fm(k, g, z, 3, gq, cqn, f_r, eps, sq_r)
        z = z_r()
        for m in range(2):
            ps = k.pS()
            for kc in range(8):
                k.mm(ps, ps[:], wckv, wckv[:, kc, m * 128:(m + 1) * 128], xb.t(n), xb[:, kc, tsl], kc == 0, kc == 7)
            k.cp("act", z, z[:, m, :], ps, ps[:])
        cn = cn_r()
        rms_fm(k, g, z, 2, gkv, cn, f_r, eps, sq_r)
        psA = k.pA()
        psB = k.pA()
        for kc in range(8):
            k.mm(psA, psA[0:96, :], wkr, wkr[:, kc, :], xb.t(n), xb[:, kc, tsl], kc == 0, kc == 7)
        for kc in range(8):
            k.mm(psB, psB[0:96, :], wkrr, wkrr[:, kc, :], xb.t(n), xb[:, kc, tsl], kc == 0, kc == 7)
        qs = qst_r()
        rope_from(psA, psB, qs, qs[64:96, :], cst)
        k.st([qs], [krr.res(n)], krr.ap()[:, tsl], qs[64:96, :])
        pend.push(lambda n=n, tsl=tsl, cqn=cqn, cn=cn, cst=cst: heads_and_v(n, tsl, cqn, cn, cst))
    pend.flush()
    k.end()
    return hT, vSW, gateT, qdT, kdT, krr, vD


def attn_D(k, g, qdT, kdT, krr, vD, yT):
    k.begin()
    k.set_pools(4, 2, 2)
    Em = k.sb([128, 4, TC], BF16)
    k.ld(g["EM"].all(), [Em], Em[:], g["EM"].ap().rearrange("d p q -> p d q"))
    vt_r = k.rot(2, [128, 32, 128], BF16)
    qk_r = k.rot(4, [96, S], BF16)
    p_r = k.rot(7, [128, TC], BF16)
    f_r = k.rot(4, [128, TC], F32)
    od_r = k.rot(4, [128, TC], F32)
    odb_r = k.rot(4, [128, TC], BF16)
    yst_r = k.rot(2, [128, S], BF16)
    vDv = vD.ap().rearrange("(b p) c -> p b c", p=128)
    scale = float(96 ** -0.5)
    pend = Pend(3)
    for hp in range(4):
        yst = yst_r()
        for hh in range(2):
            h = 2 * hp + hh
            rows = slice(hh * 64, hh * 64 + 64)
            rsel = g["rsel0"] if hh == 0 else g["rsel1"]
            vt = vt_r()
            k.ld(vD.all(), [vt], vt[:, :, hh * 64:hh * 64 + 64], vDv[:, :, h * 64:(h + 1) * 64])
            k.memset("pool", vt, vt[:, :, (1 - hh) * 64:(1 - hh) * 64 + 64], 1.0)
            QT = qk_r()
            KT = qk_r()
            k.ld(qdT.all(), [QT], QT[:], qdT.ap()[h])
            k.ld(kdT.all(), [KT], KT[0:64, :], kdT.ap()[h])
            k.ld(krr.all(), [KT], KT[64:96, :], krr.ap())
            for j in range(NTC):
                O = k.pA()
                nblk = 4 * j + 4
                for i in range(nblk):
                    dl = 4 * j - i
                    near = dl <= 0
                    c0, c1 = live_cols(dl)
                    ps = k.pS()
                    k.mm(ps, ps[:, c0:c1], KT, KT[:, i * 128:(i + 1) * 128], QT, QT[:, j * TC + c0:j * TC + c1], True, not near)
                    if near:
                        k.mm(ps, ps[:, c0:c1], g["ident"], g["ident"][:], Em, Em[:, dl + 3, c0:c1], False, True)
                    P = p_r()
                    k.act(P, P[:, c0:c1], ps, ps[:, c0:c1], AF.Exp, scale=scale)

                    def fin(P=P, i=i, O=O, nblk=nblk, j=j, yst=yst, vt=vt, rows=rows, rsel=rsel, c0=c0, c1=c1):
                        k.mm(O, O[:, c0:c1], vt, vt[:, i, :], P, P[:, c0:c1], i == 0, i == nblk - 1)
                        if i < nblk - 1:
                            return None

                        def evac1():
                            od = od_r()
                            k.cp("act", od, od[:], O, O[:])

                            def evac2():
                                dn = k.pM()
                                k.mm(dn, dn[:], rsel, rsel[:], od, od[:], True, True)
                                rd = f_r()
                                k.recip(rd, rd[rows, :], dn, dn[rows, :])
                                k.tt("dve", yst, yst[rows, j * TC:(j + 1) * TC], od, od[rows, :], rd, rd[rows, :], ALU.mult)
                                return None
                            return evac2
                        return evac1
                    pend.push(fin)
            pend.flush()
        k.st([yst], [yT.res(4 + hp)], yT.ap()[512 + hp * 128:512 + (hp + 1) * 128, :], yst[:])
    k.end()


def nsa_compress(k, g, hT, W):
    nc = k.nc
    kcT = k.dram([2, 64, 256], BF16, "od_kc")
    vcd = k.dram([2, 256, 64], BF16, "od_vc")
    k.begin()
    w1s_r = k.rot(1, [64, 32, 256], F32)
    w1_r = k.rot(1, [64, 32, 256], BF16)
    w2_r = k.rot(1, [128, 2, 64], BF16)
    w2s_r = k.rot(1, [128, 2, 64], F32)
    pes = k.sb([64, 32], F32)
    peT = k.sb([64, 32], BF16)
    cb = k.sb([128, 2], F32)
    tt_r = k.rot(2, [64, S], BF16)
    hid_r = k.rot(2, [128, 2, 256], BF16)
    st_r = k.rot(2, [128, 256], BF16)
    for kv in range(2):
        w1s = w1s_r()
        k.ld([], [w1s], w1s[:], W["cmp_w1"][kv].rearrange("(pos d) h -> d pos h", d=64))
        w1 = w1_r()
        k.cp("pool", w1, w1[:], w1s, w1s[:])
        w2s = w2s_r()
        k.ld([], [w2s], w2s[:], W["cmp_w2"][kv].rearrange("(hh p) d -> p hh d", p=128))
        w2 = w2_r()
        k.cp("pool", w2, w2[:], w2s, w2s[:])
        k.ld([], [pes], pes[:], W["cmp_pe"][kv].rearrange("pos d -> d pos"), slow=True)
        k.cp("dve", peT, peT[:], pes, pes[:])
        for hh in range(2):
            ps = k.pM()
            for pos in range(32):
                k.mm(ps, ps[:, 0:1], w1, w1[:, pos, hh * 128:(hh + 1) * 128], peT, peT[:, pos:pos + 1], pos == 0, pos == 31)
            k.cp("dve", cb, cb[:, hh:hh + 1], ps, ps[:, 0:1])
        for gI in range(2):
            T = tt_r()
            r0 = 512 + kv * 128 + gI * 64
            k.ld(hT.all(), [T], T[:], hT.ap()[r0:r0 + 64, :])
            hid = hid_r()
            k.memset("pool", hid, hid[:], 0.0)
            for hh in range(2):
                ps = k.pS()
                for pos in range(32):
                    k.mm(ps, ps[:, 0:255], w1, w1[:, pos, hh * 128:(hh + 1) * 128], T, T[:, ssl(pos, 255, 16)], pos == 0, pos == 31)
                k.act(hid, hid[:, hh, 0:255], ps, ps[:, 0:255], AF.Gelu_apprx_tanh, bias=cb[:, hh:hh + 1], extra=[cb])
            s = st_r()
            if kv == 0:
                ps = k.pM()
                for hh in range(2):
                    k.mm(ps, ps[0:64, 0:256], w2, w2[:, hh, :], hid, hid[:, hh, :], hh == 0, hh == 1)
                k.cp("dve", s, s[0:64, :], ps, ps[0:64, 0:256])
                k.memset("pool", s, s[0:64, 255:256], 0.0)
                k.st([s], [kcT.res(gI)], kcT.ap()[gI], s[0:64, :])
            else:
                for cbk in range(2):
                    ps = k.pM()
                    for hh in range(2):
                        k.mm(ps, ps[:, 0:64], hid, hid[:, hh, cbk * 128:(cbk + 1) * 128], w2, w2[:, hh, :], hh == 0, hh == 1)
                    s = st_r()
                    k.cp("dve", s, s[:, 0:64], ps, ps[:, 0:64])
                    k.st([s], [vcd.res((gI, cbk))], vcd.ap()[gI][cbk * 128:(cbk + 1) * 128, :], s[:, 0:64])
    k.end()
    return kcT, vcd


def nsa_cmp_select(k, g, hT, kcT, vcd, cin):
    nc = k.nc
    ocmp = k.dram([8, 128, S], F32, "od_ocmp")
    negm = k.dram([2, 64, S], BF16, "od_negm")
    k.begin()
    Msel = k.sb([128, 2, 64], F32)
    k.ld([], [Msel], Msel[:], cin["c_Msel"].ap().rearrange("(cb p) j -> p cb j", p=128))
    Fm = k.sb([128, 32, 64], F32)
    k.ld([], [Fm], Fm[:], cin["c_Fm"].ap().rearrange("(t p) j -> p t j", p=128))
    kc_r = k.rot(1, [128, 256], BF16)
    vc_r = k.rot(1, [128, 2, 128], BF16)
    q_r = k.rot(4, [128, S], BF16)
    for _ in range(4):
        t_ = q_r()
        k.memset("pool", t_, t_[64:128, :], 0.0)
    t_ = kc_r()
    k.memset("pool", t_, t_[64:128, :], 0.0)
    hrot = k.rot(2, [128, TC], BF16)
    k.set_pools(3, 3, 2)
    E_r = k.rot(4, [128, TC], BF16)
    p0_r = k.rot(4, [128, TC], BF16)
    p_r = k.rot(8, [128, TC], BF16)
    f_r = k.rot(8, [128, TC], F32)
    oc_r = k.rot(2, [128, TC], F32)
    sc_r = k.rot(4, [128, 64], F32)
    m8_r = k.rot(2, [128, 16], F32)
    nm_r = k.rot(2, [128, 64], BF16)
    nmT_r = k.rot(2, [64, TC], BF16)
    pend = Pend(1)
    pgs_r = k.rot(6, [128, TC], F32)
    for gI in range(2):
        kc = kc_r()
        k.ld(kcT.all(), [kc], kc[0:64, :], kcT.ap()[gI])
        vc = vc_r()
        vv = vcd.ap()[gI].rearrange("(cb p) d -> p cb d", p=128)
        k.ld(vcd.all(), [vc], vc[:, :, 0:64], vv)
        k.ld(vcd.all(), [vc], vc[:, :, 64:128], vv)
        QTs = []
        for hg in range(4):
            QT = q_r()
            head = 4 * gI + hg
            k.ld(hT.all(), [QT], QT[0:64, :], hT.ap()[head * 64:(head + 1) * 64, :])
            QTs.append(QT)
        for j in range(NTC):
            ncb = 1 if j < 4 else 2
            tsl = slice(j * TC, (j + 1) * TC)
            pg = [pgs_r() for _ in range(ncb)]
            for hg in range(4):
                head = 4 * gI + hg
                Ps = []
                for cbk in range(ncb):
                    E = E_r()
                    k.ld(g["EC"].all(), [E], E[:], g["EC"].ap()[head, 2 * j + cbk])
                    ps = k.pS()
                    k.mm(ps, ps[:], kc, kc[:, cbk * 128:(cbk + 1) * 128], QTs[hg], QTs[hg][:, tsl], True, True)
                    P0 = p0_r()
                    k.act(P0, P0[:], ps, ps[:], AF.Exp, scale=0.125)
                    P = p_r()
                    k.tt("dve", P, P[:], P0, P0[:], E, E[:], ALU.mult)
                    Ps.append(P)

                def fin(Ps=Ps, ncb=ncb, hg=hg, head=head, pg=pg, tsl=tsl, j=j):
                    O = k.pA()
                    Dn = k.pA()
                    for cbk in range(ncb):
                        k.mm(O, O[:], vc, vc[:, cbk, :], Ps[cbk], Ps[cbk][:], cbk == 0, cbk == ncb - 1)
                        k.mm(Dn, Dn[:], g["ones"], g["ones"][:], Ps[cbk], Ps[cbk][:], cbk == 0, cbk == ncb - 1)
                    dm = f_r()
                    k.ts("dve", dm, dm[:], Dn, Dn[:], 1e-30, ALU.max)
                    ln_ = f_r()
                    k.act(ln_, ln_[:], dm, dm[:], AF.Ln)
                    rd = f_r()
                    k.act(rd, rd[:], ln_, ln_[:], AF.Exp, scale=-1.0)
                    oc = oc_r()
                    k.tt("dve", oc, oc[:], O, O[:], rd, rd[:], ALU.mult)
                    k.st([oc], [ocmp.res((head, j))], ocmp.ap()[head][:, tsl], oc[:])
                    for cbk in range(ncb):
                        if hg == 0:
                            k.tt("pool", pg[cbk], pg[cbk][:], Ps[cbk], Ps[cbk][:], rd, rd[:], ALU.mult)
                        else:
                            tmp = f_r()
                            k.tt("pool", tmp, tmp[:], Ps[cbk], Ps[cbk][:], rd, rd[:], ALU.mult)
                            k.tt("dve", pg[cbk], pg[cbk][:], pg[cbk], pg[cbk][:], tmp, tmp[:], ALU.add)
                    if hg < 3:
                        return None

                    def topk():
                        nmT = nmT_r()
                        for t in range(4):
                            qt = 4 * j + t
                            ps = k.pM()
                            for cbk in range(ncb):
                                k.mm(ps, ps[:, 0:64], pg[cbk], pg[cbk][:, t * 128:(t + 1) * 128], Msel, Msel[:, cbk, :],
                                     cbk == 0, cbk == ncb - 1)
                            sc = sc_r()
                            k.tt("dve", sc, sc[:], ps, ps[:, 0:64], Fm, Fm[:, qt, :], ALU.add)
                            m8 = m8_r()
                            k.op("dve", [sc], [m8], lambda sc=sc, m8=m8: nc.vector.max(out=m8[:, 0:8], in_=sc[:]))
                            sc2 = sc_r()
                            k.op("dve", [sc, m8], [sc2], lambda sc=sc, m8=m8, sc2=sc2: nc.vector.match_replace(
                                out=sc2[:], in_to_replace=m8[:, 0:8], in_values=sc[:], imm_value=-1e9))
                            k.op("dve", [sc2], [m8], lambda sc2=sc2, m8=m8: nc.vector.max(out=m8[:, 8:16], in_=sc2[:]))
                            nm = nm_r()
                            k.ts("dve", nm, nm[:], sc, sc[:], m8[:, 15:16], ALU.is_lt, -30000.0, ALU.mult, extra=[m8])
                            ps2 = k.pM()
                            k.mm(ps2, ps2[0:64, 0:128], nm, nm[:], g["ident"], g["ident"][:], True, True)
                            k.cp("act", nmT, nmT[:, t * 128:(t + 1) * 128], ps2, ps2[0:64, 0:128])
                        k.st([nmT], [negm.res((gI, j))], negm.ap()[gI][:, tsl], nmT[:])
                        return None
                    return topk
                pend.push(fin)
        pend.flush()
    k.end()
    return ocmp, negm


def nsa_slc_win(k, g, hT, vSW, gateT, ocmp, negm, yT, cin):
    nc = k.nc
    k.begin()
    k.set_pools(4, 2, 2)
    Sel = k.sb([128, 24, 128], BF16)
    gt = k.sb([128, S], BF16)
    k.memset("pool", Sel, Sel[:], 0.0)
    k.memset("pool", gt, gt[:], 0.0)
    gst32 = k.sb([24, S], F32)
    k.ld([], [gst32], gst32[:, 0:3072], cin["c_Sel"].ap().rearrange("r a m -> r (a m)"))
    k.cp("dve", Sel, Sel[0:24, :, :], gst32, gst32[:, 0:3072].rearrange("r (a m) -> r a m", m=128))
    k.ld(gateT.all(), [gst32], gst32[:], gateT.ap())
    k.cp("dve", gt, gt[0:24, :], gst32, gst32[:])
    Es = k.sb([128, 16, TC], BF16)
    Ew = k.sb([128, 8, TC], BF16)
    KE = k.sb([128, 32, 128], BF16)
    k.ld([g["exbf"].res()], [KE], KE[64:128, :, :], g["exbf"].ap())
    KW = k.sb([128, S], BF16)
    k.memset("pool", KW, KW[64:128, :], 0.0)
    vts = [k.sb([128, 32, 128], BF16) for _ in range(4)]
    QN_r = k.rot(2, [128, S], BF16)
    p_r = k.rot(7, [128, TC], BF16)
    f_r = k.rot(8, [128, TC], F32)
    gs_r = k.rot(6, [128, TC], F32)
    od_r = k.rot(3, [128, TC], F32)
    odb_r = k.rot(3, [128, TC], BF16)
    oc_r = k.rot(2, [128, TC], F32)
    yst_r = k.rot(2, [128, S], BF16)
    vv = vSW.ap().rearrange("(b p) c -> p b c", p=128)
    pend = Pend(3)
    for gI in range(2):
        k.ld(hT.all(), [KE], KE[0:64, :, :], hT.ap()[768 + gI * 64:768 + gI * 64 + 64, :].rearrange("d (b t) -> d b t", t=128))
        k.ld(hT.all(), [KW], KW[0:64, :], hT.ap()[896 + gI * 64:896 + gI * 64 + 64, :])
        for bi, c0 in ((0, gI * 64), (1, 128 + gI * 64)):
            for par in range(2):
                vt = vts[bi * 2 + par]
                k.ld(vSW.all(), [vt], vt[:, :, par * 64:par * 64 + 64], vv[:, :, c0:c0 + 64])
                k.memset("pool", vt, vt[:, :, (1 - par) * 64:(1 - par) * 64 + 64], 1.0)
        for hg in range(4):
            head = 4 * gI + hg
            par = head % 2
            rows = slice(par * 64, par * 64 + 64)
            rsel = g["rsel0"] if par == 0 else g["rsel1"]
            if par == 0:
                yst = yst_r()
            QN = QN_r()
            k.ld(hT.all(), [QN], QN[0:64, :], hT.ap()[head * 64:(head + 1) * 64, :])
            k.ld(negm.all(), [QN], QN[64:128, :], negm.ap()[gI])
            k.ld(g["EF"].all(), [Es], Es[:], g["EF"].ap()[head].rearrange("d p q -> p d q"))
            k.ld(g["EW"].all(), [Ew], Ew[:], g["EW"].ap()[head].rearrange("d p q -> p d q"))
            for j in range(NTC):
                tsl = slice(j * TC, (j + 1) * TC)
                gsb = []
                for b3 in range(3):
                    r = (b3 * 2 + gI) * 4 + hg
                    gp = k.pM()
                    k.mm(gp, gp[:], Sel, Sel[:, r, :], gt, gt[:, tsl], True, True)
                    gs_ = gs_r()
                    k.cp("act", gs_, gs_[rows, :], gp, gp[rows, :])
                    gsb.append(gs_)
                res_sw = {}
                for br in (1, 2):
                    O = k.pA()
                    vt = vts[(br - 1) * 2 + par]
                    ilist = list(range(4 * j + 4)) if br == 1 else list(range(max(0, 4 * j - 4), 4 * j + 4))
                    if br == 2 and len(ilist) == 8:
                        ilist = [ilist[1], ilist[0]] + ilist[2:]
                    for ii, i in enumerate(ilist):
                        dl = 4 * j - i
                        near = (br == 2) or dl < 13
                        c0, c1 = live_cols(dl, br == 2)
                        qsl = slice(j * TC + c0, j * TC + c1)
                        ps = k.pS()
                        if br == 1:
                            k.mm(ps, ps[:, c0:c1], KE, KE[:, i, :], QN, QN[:, qsl], True, not near)
                        else:
                            k.mm(ps, ps[:, c0:c1], KW, KW[:, i * 128:(i + 1) * 128], QN, QN[:, qsl], True, not near)
                        P = p_r()
                        if near:
                            Et = Es if br == 1 else Ew
                            k.mm(ps, ps[:, c0:c1], g["ident"], g["ident"][:], Et, Et[:, dl + 3, c0:c1], False, True)
                            k.act(P, P[:, c0:c1], ps, ps[:, c0:c1], AF.Exp, scale=0.125)
                        else:
                            k.act(P, P[:], ps, ps[:], AF.Exp, bias=g["b31"][:, head:head + 1], scale=0.125, extra=[g["b31"]])
                        first = ii == 0
                        last = ii == len(ilist) - 1

                        def fin(P=P, i=i, O=O, first=first, last=last, vt=vt, br=br, res_sw=res_sw, j=j,
                                head=head, rows=rows, yst=yst, tsl=tsl, rsel=rsel, gsb=gsb, c0=c0, c1=c1):
                            k.mm(O, O[:, c0:c1], vt, vt[:, i, :], P, P[:, c0:c1], first, last)
                            if not last:
                                return None

                            def evac1():
                                od = od_r()
                                k.cp("act", od, od[:], O, O[:])

                                def evac2():
                                    dn = k.pM()
                                    k.mm(dn, dn[:], rsel, rsel[:], od, od[:], True, True)
                                    rd = f_r()
                                    k.recip(rd, rd[rows, :], dn, dn[rows, :])
                                    fac = f_r()
                                    k.tt("pool", fac, fac[rows, :], rd, rd[rows, :], gsb[br], gsb[br][rows, :], ALU.mult)
                                    on = f_r()
                                    k.tt("dve", on, on[rows, :], od, od[rows, :], fac, fac[rows, :], ALU.mult)
                                    res_sw[br] = on
                                    if br == 1:
                                        return None
                                    oc = oc_r()
                                    k.ld(ocmp.all(), [oc], oc[rows, :], ocmp.ap()[head][rows, tsl])
                                    tcm = f_r()
                                    k.tt("pool", tcm, tcm[rows, :], oc, oc[rows, :], gsb[0], gsb[0][rows, :], ALU.mult)
                                    a2 = f_r()
                                    k.tt("pool", a2, a2[rows, :], tcm, tcm[rows, :], res_sw[1], res_sw[1][rows, :], ALU.add)
                                    k.tt("dve", yst, yst[rows, tsl], a2, a2[rows, :], on, on[rows, :], ALU.add)
                                    return None
                                return evac2
                            return evac1
                        pend.push(fin)
            pend.flush()
            if par == 1:
                k.st([yst], [yT.res(head // 2)], yT.ap()[(head // 2) * 128:(head // 2 + 1) * 128, :], yst[:])
    k.end()


def odd_layer(k, g, xin, xout, W, cin):
    hT, vSW, gateT, qdT, kdT, krr, vD = odd_proj(k, g, xin, W, cin)
    yT = k.dram([1024, S], BF16, "od_yT")
    attn_D(k, g, qdT, kdT, krr, vD, yT)
    kcT, vcd = nsa_compress(k, g, hT, W)
    ocmp, negm = nsa_cmp_select(k, g, hT, kcT, vcd, cin)
    nsa_slc_win(k, g, hT, vSW, gateT, ocmp, negm, yT, cin)
    x1 = k.dram([DM, S], F32, "x1")
    x1.bf = k.dram([DM, S], BF16)
    out_phase(k, g, yT, 8, W["w_out"], xin, W["ln_g0"], W["ln_b0"], x1)
    ffn_phase(k, g, x1, W["w_up"], W["conv_w"], W["conv_b"], W["w_down"], W["ln_g1"], W["ln_b1"], xout)


EV_W = {"w_in": [1024, 3840], "w_out": [768, 1024], "lam": [4, 64], "subln": [128]}
OD_W = {"w_in": [1024, 1976], "w_out": [1024, 1024], "cmp_pe": [2, 32, 64], "cmp_w1": [2, 2048, 256], "cmp_w2": [2, 256, 64],
        "q_norm": [384], "kv_norm": [256], "w_uq": [384, 768], "w_uk": [256, 512], "w_uv": [256, 512]}
FF_W = {"w_up": [1024, 5632], "conv_w": [3, 2816], "conv_b": [2816], "w_down": [2816, 1024], "ln_g0": [1024], "ln_b0": [1024],
        "ln_g1": [1024], "ln_b1": [1024]}
FUSED = True


def layer_shapes(l):
    sh = dict(EV_W if l % 2 == 0 else OD_W)
    sh.update(FF_W)
    return sh


def build_program(layers):
    nc = bass.Bass("TRN2", target_bir_lowering=False)
    k = K(nc)
    cin = {n: nc.dram_tensor(n, s, F32, kind="ExternalInput") for n, s in CONST_SHAPES.items()}
    rel_bias = nc.dram_tensor("rel_bias", [32, 16], F32, kind="ExternalInput")
    xin = DT(nc.dram_tensor("xin", [DM, S], F32, kind="ExternalInput"))
    xo = DT(nc.dram_tensor("xo", [DM, S], F32, kind="ExternalOutput"))
    Ws = {}
    for l in layers:
        Ws[l] = {n: nc.dram_tensor("L%d_%s" % (l, n), s, F32, kind="ExternalInput").ap() for n, s in layer_shapes(l).items()}
    g = setup_globals(k, rel_bias, cin)
    cur = xin
    for li, l in enumerate(layers):
        nxt = xo if li == len(layers) - 1 else k.dram([DM, S], F32)
        if nxt is not xo:
            nxt.bf = k.dram([DM, S], BF16)
        if l % 2 == 0:
            even_layer(k, g, cur, nxt, Ws[l], l)
        else:
            odd_layer(k, g, cur, nxt, Ws[l], cin)
        cur = nxt
    k.c.barrier()
    return nc


def layer_inputs(inp, l):
    i = l // 2
    m = {}
    if l % 2 == 0:
        m.update({"w_in": inp["ev_w_in"][i], "w_out": inp["ev_w_out"][i], "lam": inp["ev_lambda"][i], "subln": inp["ev_subln"][i]})
    else:
        m.update({"w_in": inp["od_w_in"][i], "w_out": inp["od_w_out"][i], "cmp_pe": inp["od_cmp_pe"][i], "cmp_w1": inp["od_cmp_w1"][i],
                  "cmp_w2": inp["od_cmp_w2"][i], "q_norm": inp["od_q_norm"][i], "kv_norm": inp["od_kv_norm"][i],
                  "w_uq": inp["od_w_uq"][i], "w_uk": inp["od_w_uk"][i], "w_uv": inp["od_w_uv"][i]})
    m.update({"w_up": inp["ffn_w_up"][l], "conv_w": inp["ffn_conv_w"][l], "conv_b": inp["ffn_conv_b"][l], "w_down": inp["ffn_w_down"][l],
              "ln_g0": inp["ln_g"][l, 0], "ln_b0": inp["ln_b"][l, 0], "ln_g1": inp["ln_g"][l, 1], "ln_b1": inp["ln_b"][l, 1]})
    return {"L%d_%s" % (l, n): np.ascontiguousarray(np.asarray(v, dtype=np.float32)) for n, v in m.items()}


def kernel(**inputs):
    inp = {n: np.asarray(v) for n, v in inputs.items()}
    x = inp["x"].astype(np.float32, copy=False)
    nb = x.shape[0]
    consts = host_consts()
    xT = [np.ascontiguousarray(x[b].T) for b in range(nb)]
    groups = [[0, 1, 2, 3]] if FUSED else [[0], [1], [2], [3]]
    for layers in groups:
        nc = build_program(layers)
        shared = dict(consts)
        shared["rel_bias"] = np.ascontiguousarray(inp["rel_bias"].astype(np.float32))
        for l in layers:
            shared.update(layer_inputs(inp, l))
        in_maps = []
        for b in range(nb):
            m = dict(shared)
            m["xin"] = xT[b]
            in_maps.append(m)
        res = run_bass_kernel_spmd(nc, in_maps, core_ids=list(range(nb)))
        xT = [np.asarray(res.results[b]["xo"]) for b in range(nb)]
    out = np.stack([xT[b].T for b in range(nb)], axis=0).astype(np.float32)
    return np.ascontiguousarray(out)
```

```python
import math
from contextlib import ExitStack

import numpy as np
import concourse.bass as bass
import concourse.mybir as mybir
from concourse.bass_utils import run_bass_kernel_spmd

F32 = mybir.dt.float32
BF16 = mybir.dt.bfloat16
I32 = mybir.dt.int32
AF = mybir.ActivationFunctionType
ALU = mybir.AluOpType
AX = mybir.AxisListType


class Res:
    __slots__ = ("lw", "rd", "name")

    def __init__(self, name=""):
        self.lw = None
        self.rd = {}
        self.name = name


class Ctx:
    NRING = 12

    def __init__(self, nc):
        self.nc = nc
        self.eng = {"pe": nc.tensor, "act": nc.scalar, "dve": nc.vector, "pool": nc.gpsimd, "sp": nc.sync}
        self.sem = {}
        self.cnt = {}
        for e in ("pe", "act", "dve", "pool"):
            self.sem[e] = nc.alloc_semaphore("s_" + e)
            self.cnt[e] = 0
        self.rings = {}
        for q in ("sp", "pool", "act"):
            keys = []
            for i in range(self.NRING):
                k = "d_%s_%d" % (q, i)
                self.sem[k] = nc.alloc_semaphore(k)
                self.cnt[k] = 0
                keys.append(k)
            self.rings[q] = [keys, 0]
        self.seen = {e: {} for e in self.eng}
        self.n_wait = 0
        self.n_ins = 0

    def _need(self, e, needs, ev):
        k, v, _ = ev
        if self.seen[e].get(k, 0) >= v:
            return
        if needs.get(k, 0) < v:
            needs[k] = v

    def _deps(self, e, reads, writes):
        needs = {}
        for r in reads:
            if r.lw is not None:
                if not (r.lw[2] == e and e == "pe" and r.lw[0] == "pe"):
                    self._need(e, needs, r.lw)
        for w in writes:
            if w.lw is not None and (w.lw[0] != e or e == "pool"):
                self._need(e, needs, w.lw)
            for k, (v, re_) in w.rd.items():
                if k != e:
                    self._need(e, needs, (k, v, re_))
        for k, v in needs.items():
            if not (getattr(self, "skip_pe_waits", False) and e == "pe"):
                self.eng[e].wait_ge(self.sem[k], v)
            self.seen[e][k] = v
            self.n_wait += 1

    def _commit(self, ev, reads, writes):
        k, v, e = ev
        for r in reads:
            r.rd[k] = (v, e)
        for w in writes:
            w.lw = ev
            w.rd = {}

    def op(self, e, reads, writes, fn):
        self._deps(e, reads, writes)
        ins = fn()
        self.cnt[e] += 1
        ins.then_inc(self.sem[e], 1)
        self.n_ins += 1
        self._commit((e, self.cnt[e], e), reads, writes)
        return ins

    def dma(self, q, reads, writes, out, in_, **kw):
        keys, idx = self.rings[q]
        k = keys[idx % self.NRING]
        self.rings[q][1] = idx + 1
        if self.cnt[k] > 0 and self.seen[q].get(k, 0) < self.cnt[k]:
            self.eng[q].wait_ge(self.sem[k], self.cnt[k])
            self.seen[q][k] = self.cnt[k]
        self._deps(q, reads, writes)
        ins = self.eng[q].dma_start(out=out, in_=in_, **kw)
        self.cnt[k] += 16
        ins.then_inc(self.sem[k], 16)
        self.n_ins += 1
        self._commit((k, self.cnt[k], "dma"), reads, writes)
        return ins

    def barrier(self):
        for e in self.eng:
            for k, v in self.cnt.items():
                if v > 0 and k != e and self.seen[e].get(k, 0) < v:
                    self.eng[e].wait_ge(self.sem[k], v)
                    self.seen[e][k] = v


S = 4096
DM = 1024
NTC = 8
TC = 512
PADL = 4112
LF = PADL + 4096
PADB = 127
LB = 384
ALPHA = (2 * 4) ** 0.25
DFF = 2816


class TT:
    __slots__ = ("h", "r")

    def __init__(self, h):
        self.h = h
        self.r = Res()

    def __getitem__(self, idx):
        return self.h[idx]


class DT:
    def __init__(self, h):
        self.h = h
        self.rs = {}

    def res(self, key=0):
        if key not in self.rs:
            self.rs[key] = Res()
        return self.rs[key]

    def all(self):
        return list(self.rs.values())

    def ap(self):
        return self.h.ap()


def _r(x):
    return x.r if isinstance(x, TT) else x


class XB:
    def __init__(self, tt):
        self.h = tt.h
        self.parts = [TT(tt.h) for _ in range(4)]

    def __getitem__(self, idx):
        return self.h[idx]

    def t(self, n):
        return self.parts[n // 2]

    def all(self):
        return list(self.parts)


class K:
    def __init__(self, nc):
        self.nc = nc
        self.c = Ctx(nc)
        self.es = ExitStack()
        self.ph = None
        self.uid = 0
        self.ps_all = [self._ps() for _ in range(8)]
        self.set_pools(3, 4, 1)

    def _ps(self):
        self.uid += 1
        return TT(self.es.enter_context(self.nc.psum_tensor("ps%d" % self.uid, [128, 512], F32)))

    def set_pools(self, ns, na, nm):
        assert ns + na + nm == 8
        self.ps_s = self.ps_all[0:ns]
        self.ps_a = self.ps_all[ns:ns + na]
        self.ps_m = self.ps_all[ns + na:]
        self.i_s = self.i_a = self.i_m = 0

    def pS(self):
        self.i_s += 1
        return self.ps_s[self.i_s % len(self.ps_s)]

    def pA(self):
        self.i_a += 1
        return self.ps_a[self.i_a % len(self.ps_a)]

    def pM(self):
        self.i_m += 1
        return self.ps_m[self.i_m % len(self.ps_m)]

    def begin(self):
        self.ph = ExitStack()

    def end(self):
        self.c.barrier()
        self.ph.close()
        self.ph = None
        self.set_pools(3, 4, 1)

    def sb(self, shape, dtype, glob=False, mid=None):
        self.uid += 1
        st = mid if mid is not None else (self.es if glob else self.ph)
        return TT(st.enter_context(self.nc.sbuf_tensor("t%d" % self.uid, list(shape), dtype)))

    def rot(self, n, shape, dtype):
        bufs = [self.sb(shape, dtype) for _ in range(n)]
        st = [0]

        def nxt():
            st[0] += 1
            return bufs[st[0] % n]
        return nxt

    def dram(self, shape, dtype, name=None):
        self.uid += 1
        if name is not None and name in getattr(self, "dbg", ()):
            return DT(self.nc.dram_tensor("dbg_" + name, list(shape), dtype, kind="ExternalOutput"))
        return DT(self.nc.dram_tensor("scr%d" % self.uid, list(shape), dtype))

    def op(self, e, reads, writes, fn):
        return self.c.op(e, [_r(x) for x in reads], [_r(x) for x in writes], fn)

    def ld(self, reads, writes, out, in_, q="sp", slow=False):
        kw = {"allow_slow_non_contiguous": True} if slow else {}
        return self.c.dma(q, [_r(x) for x in reads], [_r(x) for x in writes], out, in_, **kw)

    def st(self, reads, writes, out, in_, q="pool"):
        return self.c.dma(q, [_r(x) for x in reads], [_r(x) for x in writes], out, in_)

    def mm(self, out_t, out_ap, lhs_t, lhs_ap, rhs_t, rhs_ap, start, stop):
        nc = self.nc
        reads = (list(lhs_t) if isinstance(lhs_t, (list, tuple)) else [lhs_t]) + \
                (list(rhs_t) if isinstance(rhs_t, (list, tuple)) else [rhs_t])
        return self.op("pe", reads, [out_t],
                       lambda: nc.tensor.matmul(out_ap, lhsT=lhs_ap, rhs=rhs_ap, start=start, stop=stop))

    def act(self, out_t, out_ap, in_t, in_ap, func, bias=None, scale=1.0, extra=()):
        nc = self.nc
        kw = {}
        if bias is not None:
            kw["bias"] = bias
        return self.op("act", [in_t] + list(extra), [out_t],
                       lambda: nc.scalar.activation(out=out_ap, in_=in_ap, func=func, scale=scale, **kw))

    def tt(self, e, out_t, out_ap, a_t, a_ap, b_t, b_ap, op):
        eng = self.nc.vector if e == "dve" else self.nc.gpsimd
        return self.op(e, [a_t, b_t], [out_t], lambda: eng.tensor_tensor(out=out_ap, in0=a_ap, in1=b_ap, op=op))

    def ts(self, e, out_t, out_ap, a_t, a_ap, s1, op0, s2=None, op1=None, extra=()):
        eng = self.nc.vector if e == "dve" else self.nc.gpsimd
        if op1 is None:
            return self.op(e, [a_t] + list(extra), [out_t],
                           lambda: eng.tensor_scalar(out=out_ap, in0=a_ap, scalar1=s1, scalar2=None, op0=op0))
        return self.op(e, [a_t] + list(extra), [out_t],
                       lambda: eng.tensor_scalar(out=out_ap, in0=a_ap, scalar1=s1, scalar2=s2, op0=op0, op1=op1))

    def stt(self, e, out_t, out_ap, a_t, a_ap, scalar, b_t, b_ap, op0, op1, extra=()):
        eng = self.nc.vector if e == "dve" else self.nc.gpsimd
        return self.op(e, [a_t, b_t] + list(extra), [out_t],
                       lambda: eng.scalar_tensor_tensor(out=out_ap, in0=a_ap, scalar=scalar, in1=b_ap, op0=op0, op1=op1))

    def cp(self, e, out_t, out_ap, in_t, in_ap):
        nc = self.nc
        if e == "act":
            return self.op("act", [in_t], [out_t], lambda: nc.scalar.copy(out=out_ap, in_=in_ap))
        eng = nc.vector if e == "dve" else nc.gpsimd
        return self.op(e, [in_t], [out_t], lambda: eng.tensor_copy(out=out_ap, in_=in_ap))

    def memset(self, e, t, ap, val):
        eng = self.nc.vector if e == "dve" else self.nc.gpsimd
        return self.op(e, [], [t], lambda: eng.memset(ap, val))

    def recip(self, out_t, out_ap, in_t, in_ap):
        nc = self.nc
        return self.op("dve", [in_t], [out_t], lambda: nc.vector.reciprocal(out=out_ap, in_=in_ap))


def t5_bucket_np(dist):
    dist = np.asarray(dist, dtype=np.int64)
    n = np.maximum(dist, 0)
    nf = np.maximum(n, 1).astype(np.float32)
    large = 16 + (np.log(nf / np.float32(16)) / np.float32(math.log(2048 / 16)) * np.float32(16)).astype(np.int32)
    return np.where(n < 16, n, np.minimum(large, 31))


def host_consts():
    cs = {}
    oh = np.zeros((32, 4096), np.float32)
    oh[t5_bucket_np(np.arange(4096)), np.arange(4096)] = 1.0
    cs["c_ohF"] = oh
    ohb = np.zeros((3, 32, 129), np.float32)
    for p, d in enumerate((1, 4, 16)):
        idx = np.arange(129)
        ohb[p, t5_bucket_np(idx * d), idx] = 1.0
    cs["c_ohB"] = ohb
    cs["c_ident"] = np.eye(128, dtype=np.float32)
    cs["c_J"] = np.eye(128, dtype=np.float32)[::-1].copy()
    M = np.zeros((256, 64), np.float32)
    for j in range(64):
        for cc, w in ((4 * j - 1, .5), (4 * j, 1.), (4 * j + 1, 1.), (4 * j + 2, 1.), (4 * j + 3, .5)):
            if 0 <= cc < 255:
                M[cc, j] += w
    cs["c_Msel"] = M
    q = np.arange(4096)[:, None]
    jb = np.arange(64)[None, :]
    qb = q // 64
    fm = np.where(jb > qb, -1e4, 0.0) + np.where((jb == 0) | (jb == qb) | (jb == qb - 1), 1e4, 0.0)
    cs["c_Fm"] = fm.astype(np.float32)
    ex = np.zeros((64, 32, 128), np.float32)
    for i in range(32):
        for k in range(128):
            ex[2 * i + k // 64, i, k] = 1.0
    cs["c_Ex"] = ex
    sel = np.zeros((24, 24, 128), np.float32)
    for r in range(24):
        sel[r, r, :] = 1.0
    cs["c_Sel"] = sel
    half = 16
    inv = (np.float32(10000.0) ** (-np.arange(half, dtype=np.float32) / np.float32(half))).astype(np.float32)
    ang = np.arange(4096, dtype=np.float32)[None, :] * inv[:, None]
    cs["c_cos"] = np.concatenate([np.cos(ang), np.cos(ang)], 0).astype(np.float32)
    cs["c_sin"] = np.concatenate([np.sin(ang), np.sin(ang)], 0).astype(np.float32)
    return cs


CONST_SHAPES = {"c_ohF": [32, 4096], "c_ohB": [3, 32, 129], "c_ident": [128, 128], "c_J": [128, 128],
                "c_Msel": [256, 64], "c_Fm": [4096, 64], "c_Ex": [64, 32, 128], "c_Sel": [24, 24, 128],
                "c_cos": [32, 4096], "c_sin": [32, 4096]}


def setup_globals(k, rel_bias, cin):
    nc = k.nc
    g = {}
    for nm in ("ident", "J", "ones", "zeros"):
        g[nm] = k.sb([128, 128], BF16, glob=True)
    g["ones32"] = k.sb([128, 128], F32, glob=True)
    g["rsel0"] = k.sb([128, 128], F32, glob=True)
    g["rsel1"] = k.sb([128, 128], F32, glob=True)
    g["eps"] = k.sb([128, 1], F32, glob=True)
    g["b31"] = k.sb([128, 16], F32, glob=True)
    k.begin()
    st32r = k.rot(2, [128, 128], F32)
    for nm in ("ident", "J"):
        st32 = st32r()
        k.ld([], [st32], st32[:], cin["c_" + nm].ap())
        k.cp("dve", g[nm], g[nm][:], st32, st32[:])
    k.memset("pool", g["ones"], g["ones"][:], 1.0)
    k.memset("pool", g["ones32"], g["ones32"][:], 1.0)
    k.memset("pool", g["zeros"], g["zeros"][:], 0.0)
    k.memset("pool", g["rsel0"], g["rsel0"][:], 0.0)
    k.memset("pool", g["rsel1"], g["rsel1"][:], 0.0)
    k.memset("pool", g["rsel0"], g["rsel0"][64:65, :], 1.0)
    k.memset("pool", g["rsel1"], g["rsel1"][0:1, :], 1.0)
    k.memset("pool", g["eps"], g["eps"][:], 1e-5)
    k.ld([], [g["b31"]], g["b31"][:], bass.AP(rel_bias, 31 * 16, [[0, 128], [1, 16]]))
    tbl = k.sb([32, 16], F32)
    k.ld([], [tbl], tbl[:], rel_bias.ap())
    oh = k.sb([32, 4096], F32)
    k.ld([], [oh], oh[:], cin["c_ohF"].ap())
    stg = k.sb([16, LF], BF16)
    k.memset("pool", stg, stg[:], 0.0)
    stl = k.sb([16, LF], BF16)
    k.memset("pool", stl, stl[:], -30000.0)
    for n in range(8):
        ps = k.pM()
        k.mm(ps, ps[0:16, :], tbl, tbl[:], oh, oh[:, n * 512:(n + 1) * 512], True, True)
        k.act(stg, stg[:, PADL + n * 512:PADL + (n + 1) * 512], ps, ps[0:16, :], AF.Exp)
        k.act(stl, stl[:, PADL + n * 512:PADL + (n + 1) * 512], ps, ps[0:16, :], AF.Copy, scale=8.0)
    vecF = k.dram([16, LF], BF16)
    k.st([stg], [vecF.res()], vecF.ap(), stg[:])
    vecFl = k.dram([16, LF], BF16)
    k.st([stl], [vecFl.res()], vecFl.ap(), stl[:])
    stw = k.sb([16, LF], BF16)
    k.memset("pool", stw, stw[:], -30000.0)
    k.cp("dve", stw, stw[:, PADL:PADL + 512], stl, stl[:, PADL:PADL + 512])
    vecW = k.dram([16, LF], BF16)
    k.st([stw], [vecW.res()], vecW.ap(), stw[:])
    stm = k.sb([16, LF], BF16)
    k.memset("pool", stm, stm[:], -30000.0)
    k.memset("pool", stm, stm[:, PADL:], 0.0)
    vecM = k.dram([16, LF], BF16)
    k.st([stm], [vecM.res()], vecM.ap(), stm[:])
    g["vecF"], g["vecW"], g["vecM"] = vecF, vecW, vecM
    vecB = []
    for p in range(3):
        ohb = k.sb([32, 129], F32)
        k.ld([], [ohb], ohb[:], cin["c_ohB"].ap()[p])
        sb_ = k.sb([16, LB], BF16)
        k.memset("pool", sb_, sb_[:], 0.0)
        ps = k.pM()
        k.mm(ps, ps[0:16, 0:129], tbl, tbl[:], ohb, ohb[:], True, True)
        k.act(sb_, sb_[:, PADB:PADB + 129], ps, ps[0:16, 0:129], AF.Exp)
        vb = k.dram([16, LB], BF16)
        k.st([sb_], [vb.res()], vb.ap(), sb_[:])
        vecB.append(vb)
    g["vecB"] = vecB
    EF = k.dram([8, 16, 128, TC], BF16)
    EW = k.dram([8, 8, 128, TC], BF16)
    EC = k.dram([8, 16, 128, TC], BF16)
    EM = k.dram([4, 128, TC], BF16)
    hrot = k.rot(4, [128, TC], BF16)
    est = k.rot(4, [128, TC], BF16)
    jobs = []
    for h in range(8):
        for dl in range(-3, 13):
            jobs.append((vecFl, h, toep_off(dl), 1, EF, EF.ap()[h, dl + 3]))
        for dl in range(-3, 5):
            jobs.append((vecW, h, toep_off(dl), 1, EW, EW.ap()[h, dl + 3]))
        for j in range(8):
            for cbk in range(2):
                jobs.append((vecF, h, PADL - 31 + 512 * j - 2048 * cbk - 2032, 16, EC, EC.ap()[h, 2 * j + cbk]))
    for dl in range(-3, 1):
        jobs.append((vecM, 0, toep_off(dl), 1, EM, EM.ap()[dl + 3]))
    for ji, (vec, row, off, pstep, dst, dap) in enumerate(jobs):
        H = hrot()
        k.ld([vec.res()], [H], H[:], bass.AP(vec.h, row * LF + off, [[pstep, 128], [1, TC]]))
        ps = k.pA()
        k.mm(ps, ps[:], g["J"], g["J"][:], H, H[:], True, True)
        e_ = est()
        k.cp("act" if ji % 2 else "dve", e_, e_[:], ps, ps[:])
        k.st([e_], [dst.res(ji)], dap, e_[:])
    g["EF"], g["EW"], g["EC"], g["EM"] = EF, EW, EC, EM
    exs = k.sb([64, 32, 128], F32)
    exb = k.sb([64, 32, 128], BF16)
    k.ld([], [exs], exs[:], cin["c_Ex"].ap())
    k.cp("pool", exb, exb[:], exs, exs[:])
    exbf = k.dram([64, 32, 128], BF16)
    k.st([exb], [exbf.res()], exbf.ap(), exb[:])
    g["exbf"] = exbf
    k.end()
    return g


def load_E(k, g, dst, dst_ap, vec, row, L, off, pstep, W, hrot):
    H = hrot()
    k.ld([vec.res()], [H], H[:, 0:W], bass.AP(vec.h, row * L + off, [[pstep, 128], [1, W]]))
    ps = k.pM()
    k.mm(ps, ps[:, 0:W], g["J"], g["J"][:], H, H[:, 0:W], True, True)
    k.cp("act", dst, dst_ap, ps, ps[:, 0:W])


def toep_off(delta):
    return PADL + 128 * delta - 127


def load_xb(k, xin, xb_tt):
    xb = XB(xb_tt)
    if getattr(xin, "bf", None) is not None:
        xv = xin.bf.ap().rearrange("(kc p) t -> p kc t", p=128)
        for n in range(4):
            k.ld(xin.bf.all(), [xb.parts[n]], xb[:, :, n * 1024:(n + 1) * 1024], xv[:, :, n * 1024:(n + 1) * 1024])
        return xb
    stg = k.rot(2, [128, 1024], F32)
    xv = xin.ap().rearrange("(kc p) t -> p kc t", p=128)
    i = 0
    for q4 in range(4):
        for kc in range(8):
            s = stg()
            k.ld([xin.res(kc)], [s], s[:], xv[:, kc, q4 * 1024:(q4 + 1) * 1024])
            k.cp("dve" if i % 2 == 0 else "act", xb.parts[q4], xb[:, kc, q4 * 1024:(q4 + 1) * 1024], s, s[:])
            i += 1
    return xb


def load_w_bf16(k, w_ap, nk, ncols, dst, dst_ap, stg_rot, eng="pool"):
    s = stg_rot()
    k.ld([], [s], s[:, 0:nk, 0:ncols], w_ap.rearrange("(kc p) m -> p kc m", p=128))
    k.cp(eng, dst, dst_ap, s, s[:, 0:nk, 0:ncols])


def ln_block(k, g, z, nchunk, gam, bet, dst_fn):
    nc = k.nc
    zb = k.ln_zb()
    sq = k.ln_sq()
    s1 = k.pA()
    s2 = k.pA()
    for m in range(nchunk):
        k.cp("act", zb, zb[:, m, :], z, z[:, m, :])
        k.op("act", [z], [sq], lambda m=m: nc.scalar.activation(out=sq[:, m, :], in_=z[:, m, :], func=AF.Square))
    for m in range(nchunk):
        k.mm(s1, s1[:], g["ones"], g["ones"][:], zb, zb[:, m, :], m == 0, m == nchunk - 1)
    for m in range(nchunk):
        k.mm(s2, s2[:], g["ones"], g["ones"][:], sq, sq[:, m, :], m == 0, m == nchunk - 1)
    nf = float(nchunk * 128)
    mean = k.ln_s()
    k.op("act", [s1], [mean], lambda: nc.scalar.mul(out=mean[:], in_=s1[:], mul=1.0 / nf))
    msq = k.ln_s()
    k.tt("dve", msq, msq[:], mean, mean[:], mean, mean[:], ALU.mult)
    var = k.ln_s()
    k.stt("dve", var, var[:], s2, s2[:], 1.0 / nf, msq, msq[:], ALU.mult, ALU.subtract)
    sd = k.ln_s()
    k.act(sd, sd[:], var, var[:], AF.Sqrt, bias=g["eps"][:], extra=[g["eps"]])
    rstd = k.ln_s()
    k.recip(rstd, rstd[:], sd, sd[:])
    if hasattr(k, "ln_dump"):
        for nm_, t_ in (("mean", mean), ("msq", msq), ("var", var), ("sd", sd), ("rstd", rstd)):
            k.ln_dump(nm_, t_)
    for m in range(nchunk):
        t = k.ln_t()
        k.tt("dve", t, t[:], z, z[:, m, :], mean, mean[:], ALU.subtract)
        t2 = k.ln_t()
        k.tt("pool", t2, t2[:], t, t[:], rstd, rstd[:], ALU.mult)
        o, oap = dst_fn(m)
        k.op("act", [t2, gam, bet], [o],
             lambda m=m, t2=t2, oap=oap: nc.scalar.activation(out=oap, in_=t2[:], func=AF.Identity,
                                                              scale=gam[:, m:m + 1], bias=bet[:, m:m + 1]))


def proj_resid_ln(k, g, yT_fn, nk, w_ap, xres, gam_ap, bet_ap, xout, w_pre=None):
    nc = k.nc
    if w_pre is not None:
        w = w_pre
    else:
        w = k.sb([128, nk, DM], BF16)
        wst = k.rot(1, [128, nk, 128], F32)
        for cc in range(8):
            load_w_bf16(k, w_ap[:, cc * 128:(cc + 1) * 128], nk, 128, w, w[:, :, cc * 128:(cc + 1) * 128], wst,
                        "pool" if cc % 2 else "dve")
    gam = k.sb([128, 8], F32)
    bet = k.sb([128, 8], F32)
    k.ld([], [gam], gam[:], gam_ap.rearrange("(m p) -> p m", p=128), slow=True)
    k.ld([], [bet], bet[:], bet_ap.rearrange("(m p) -> p m", p=128), slow=True)
    xr = k.rot(3, [128, 8, TC], F32)
    k.ln_sq = k.rot(1, [128, 8, TC], BF16)
    k.ln_zb = k.rot(1, [128, 8, TC], BF16)
    k.ln_t = k.rot(4, [128, TC], F32)
    k.ln_s = k.rot(5, [128, TC], F32)
    ost = k.rot(1, [128, 8, TC], F32)
    obt = k.rot(1, [128, 8, TC], BF16)
    xv = xres.ap().rearrange("(m p) t -> p m t", p=128)
    ov = xout.ap().rearrange("(m p) t -> p m t", p=128)
    obv = xout.bf.ap().rearrange("(m p) t -> p m t", p=128) if getattr(xout, "bf", None) is not None else None
    pend = Pend(1)
    for n in range(NTC):
        yt, yap = yT_fn(n)
        xt = xr()
        k.ld(xres.all(), [xt], xt[:], xv[:, :, n * TC:(n + 1) * TC])
        z = xt
        for m in range(8):
            ps = k.pS()
            for kc in range(nk):
                k.mm(ps, ps[:], w, w[:, kc, m * 128:(m + 1) * 128], yt, yap(kc), kc == 0, kc == nk - 1)
            k.stt("dve", z, z[:, m, :], xt, xt[:, m, :], ALPHA, ps, ps[:], ALU.mult, ALU.add)

        def fin(z=z, n=n):
            o = ost()
            ln_block(k, g, z, 8, gam, bet, lambda m, o=o: (o, o[:, m, :]))
            k.st([o], [xout.res(kc) for kc in range(8)], ov[:, :, n * TC:(n + 1) * TC], o[:])
            if obv is not None:
                ob = obt()
                k.cp("pool", ob, ob[:], o, o[:])
                k.st([ob], [xout.bf.res(n)], obv[:, :, n * TC:(n + 1) * TC], ob[:])
            return None
        pend.push(fin)
    pend.flush()


def ffn_phase(k, g, x1, w_up, conv_w, conv_b, w_down, ln_g, ln_b, xout):
    nc = k.nc
    hT = k.dram([DFF, S], BF16, "hT")
    mid = ExitStack()
    wdn = k.sb([128, 22, DM], BF16, mid=mid)
    k.begin()
    wdst = k.rot(1, [128, 22, 128], F32)
    xb = load_xb(k, x1, k.sb([128, 8, S], BF16))
    cw = k.sb([128, 3, 22], F32)
    k.ld([], [cw], cw[:], conv_w.rearrange("j (c p) -> p j c", p=128), slow=True)
    cb = k.sb([128, 22], F32)
    k.ld([], [cb], cb[:], conv_b.rearrange("(c p) -> p c", p=128), slow=True)
    wst = k.rot(3, [128, 8, 128], F32)
    wa_r = k.rot(2, [128, 8, 128], BF16)
    wg_r = k.rot(2, [128, 8, 128], BF16)
    gb_r = k.rot(2, [128, S + 2], F32)
    for _ in range(2):
        gb = gb_r()
        k.memset("pool", gb, gb[:, 0:2], 0.0)
    t_r = k.rot(5, [128, TC], F32)
    hst_r = k.rot(2, [128, S], BF16)

    def wload(cc):
        wa = wa_r()
        wg = wg_r()
        load_w_bf16(k, w_up[:, cc * 128:(cc + 1) * 128], 8, 128, wa, wa[:], wst, "pool")
        load_w_bf16(k, w_up[:, DFF + cc * 128:DFF + (cc + 1) * 128], 8, 128, wg, wg[:], wst, "pool")
        return wa, wg
    wcur = wload(0)
    for cc in range(22):
        wa, wg = wcur
        if cc + 1 < 22:
            wcur = wload(cc + 1)
        hst = hst_r()
        gb = gb_r()
        if cc % 2 == 1 and cc // 2 < 8:
            c8 = cc // 2
            load_w_bf16(k, w_down[:, c8 * 128:(c8 + 1) * 128], 22, 128, wdn, wdn[:, :, c8 * 128:(c8 + 1) * 128], wdst, "act")
        for n in range(NTC):
            pa = k.pS()
            pg = k.pA()
            for kc in range(8):
                k.mm(pg, pg[:], wg, wg[:, kc, :], xb.t(n), xb[:, kc, n * TC:(n + 1) * TC], kc == 0, kc == 7)
            for kc in range(8):
                k.mm(pa, pa[:], wa, wa[:, kc, :], xb.t(n), xb[:, kc, n * TC:(n + 1) * TC], kc == 0, kc == 7)
            o = 2 + n * TC
            k.cp("act", gb, gb[:, o:o + TC], pg, pg[:])
            t1 = t_r()
            k.op("act", [pg, cw, cb], [t1],
                 lambda t1=t1, pg=pg, cc=cc: nc.scalar.activation(out=t1[:], in_=pg[:], func=AF.Identity,
                                                                 scale=cw[:, 2, cc:cc + 1], bias=cb[:, cc:cc + 1]))
            t2 = t_r()
            k.stt("dve", t2, t2[:], gb, gb[:, o - 1:o - 1 + TC], cw[:, 1, cc:cc + 1], t1, t1[:], ALU.mult, ALU.add, extra=[cw])
            t3 = t_r()
            k.stt("dve", t3, t3[:], gb, gb[:, o - 2:o - 2 + TC], cw[:, 0, cc:cc + 1], t2, t2[:], ALU.mult, ALU.add, extra=[cw])
            t4 = t_r()
            k.act(t4, t4[:], t3, t3[:], AF.Gelu_apprx_tanh)
            k.tt("dve", hst, hst[:, n * TC:(n + 1) * TC], t4, t4[:], pa, pa[:], ALU.mult)
        k.st([hst], [hT.res(cc)], hT.ap()[cc * 128:(cc + 1) * 128, :], hst[:])
    k.end()
    k.begin()
    hr = k.rot(2, [128, 22, TC], BF16)
    hv = hT.ap().rearrange("(c p) t -> p c t", p=128)

    def yT_fn(n):
        h = hr()
        k.ld(hT.all(), [h], h[:], hv[:, :, n * TC:(n + 1) * TC])
        return h, (lambda kc, h=h: h[:, kc, :])
    proj_resid_ln(k, g, yT_fn, 22, w_down, x1, ln_g, ln_b, xout, w_pre=wdn)
    k.end()
    mid.close()


def live_cols(dl, win=False):
    if dl < 0:
        return -dl * 128, TC
    if win and dl == 4:
        return 0, 128
    return 0, TC


def ssl(t0, n, d):
    return slice(t0, t0 + (n - 1) * d + 1, d)


class Pend:
    def __init__(self, depth=1):
        self.q = []
        self.depth = depth

    def _run(self, fn):
        r = fn()
        if callable(r):
            self.q.append(r)

    def push(self, fn):
        self.q.append(fn)
        while len(self.q) > self.depth:
            self._run(self.q.pop(0))

    def flush(self):
        while self.q:
            self._run(self.q.pop(0))


def proj_fm_load(k, w_in, col0, ncols, wst, wbf):
    w = wbf()
    load_w_bf16(k, w_in[:, col0:col0 + ncols], 8, ncols, w, w[:, :, 0:ncols], wst, "pool")
    return w


def proj_fm_compute(k, xb, w, ncols, out_dt, row0, stg_r, key):
    stg = stg_r()
    for n in range(NTC):
        ps = k.pS()
        for kc in range(8):
            k.mm(ps, ps[0:ncols, :], w, w[:, kc, 0:ncols], xb.t(n), xb[:, kc, n * TC:(n + 1) * TC], kc == 0, kc == 7)
        k.cp("act" if n % 2 == 0 else "dve", stg, stg[0:ncols, n * TC:(n + 1) * TC], ps, ps[0:ncols, :])
    k.st([stg], [out_dt.res(key)], out_dt.ap()[row0:row0 + ncols, :], stg[0:ncols, :])


def proj_fm_all(k, xb, w_in, cols, out_dt, wst, wbf, stg_r):
    w = proj_fm_load(k, w_in, cols[0], 128, wst, wbf)
    for ci, c0 in enumerate(cols):
        wn = proj_fm_load(k, w_in, cols[ci + 1], 128, wst, wbf) if ci + 1 < len(cols) else None
        proj_fm_compute(k, xb, w, 128, out_dt, ci * 128, stg_r, ci)
        w = wn


def even_proj(k, g, xin, w_in):
    hT = k.dram([2560, S], BF16, "ev_hT")
    vA = k.dram([S, 512], BF16, "ev_vA")
    vB = k.dram([3, 32, 128, 256], BF16, "ev_vB")
    k.begin()
    xb = load_xb(k, xin, k.sb([128, 8, S], BF16))
    wst = k.rot(2, [128, 8, 512], F32)
    wbf = k.rot(2, [128, 8, 128], BF16)
    stg_r = k.rot(2, [128, S], BF16)
    cols = list(range(0, 1024, 128))
    for p in range(3):
        base = 1536 + p * 768
        cols += [base, base + 128, base + 256, base + 384]
    proj_fm_all(k, xb, w_in, cols, hT, wst, wbf, stg_r)
    wv = k.sb([128, 8, 512], BF16)
    load_w_bf16(k, w_in[:, 1024:1536], 8, 512, wv, wv[:], wst, "pool")
    vst = k.rot(3, [128, 512], BF16)
    for b in range(32):
        ps = k.pS()
        for kc in range(8):
            k.mm(ps, ps[:], xb.t(b // 4), xb[:, kc, b * 128:(b + 1) * 128], wv, wv[:, kc, :], kc == 0, kc == 7)
        s = vst()
        k.cp("act" if b % 2 == 0 else "dve", s, s[:], ps, ps[:])
        k.st([s], [vA.res(b)], vA.ap()[b * 128:(b + 1) * 128, :], s[:])
    for p, d in enumerate((1, 4, 16)):
        base = 1536 + p * 768 + 512
        load_w_bf16(k, w_in[:, base:base + 256], 8, 256, wv, wv[:, :, 0:256], wst, "pool")
        nb = 32 // d
        for r in range(d):
            for b in range(nb):
                t0 = r + d * 128 * b
                ps = k.pS()
                for kc in range(8):
                    k.mm(ps, ps[:, 0:256], xb.all(), xb[:, kc, ssl(t0, 128, d)], wv, wv[:, kc, 0:256], kc == 0, kc == 7)
                s = vst()
                k.cp("act" if b % 2 == 0 else "dve", s, s[:, 0:256], ps, ps[:, 0:256])
                k.st([s], [vB.res((p, r * nb + b))], vB.ap()[p, r * nb + b], s[:, 0:256])
    k.end()
    return hT, vA, vB


def attn_A(k, g, hT, vA, yT, lam_p, subln, layer_idx):
    nc = k.nc
    lam_init = 0.8 - 0.6 * math.exp(-0.3 * layer_idx)
    k.begin()
    lpb = k.sb([128, 256], F32)
    k.ld([], [lpb], lpb[:], bass.AP(lam_p.tensor, lam_p.offset, [[0, 128], [1, 256]]))
    pr = k.sb([128, 128], F32)
    k.tt("dve", pr, pr[:, 0:64], lpb, lpb[:, 0:64], lpb, lpb[:, 64:128], ALU.mult)
    k.tt("dve", pr, pr[:, 64:128], lpb, lpb[:, 128:192], lpb, lpb[:, 192:256], ALU.mult)
    sm = k.sb([128, 2], F32)
    k.op("dve", [pr], [sm], lambda: nc.vector.reduce_sum(out=sm[:, 0:1], in_=pr[:, 0:64], axis=AX.X))
    k.op("dve", [pr], [sm], lambda: nc.vector.reduce_sum(out=sm[:, 1:2], in_=pr[:, 64:128], axis=AX.X))
    ex = k.sb([128, 2], F32)
    k.act(ex, ex[:], sm, sm[:], AF.Exp)
    neglam = k.sb([128, 1], F32)
    k.stt("dve", neglam, neglam[:], ex, ex[:, 1:2], -lam_init, ex, ex[:, 0:1], ALU.add, ALU.subtract)
    gsc = k.sb([128, 1], F32)
    k.ld([], [gsc], gsc[:], subln.rearrange("(p o) -> p o", o=1), slow=True)
    k.ts("dve", gsc, gsc[:], gsc, gsc[:], 1.0 - lam_init, ALU.mult)
    eps = g["eps"]
    Eh = k.sb([128, 16, TC], BF16)
    hrot = k.rot(2, [128, TC], BF16)
    vt_r = k.rot(2, [128, 32, 128], BF16)
    qk_r = k.rot(4, [128, S], BF16)
    for _ in range(4):
        t_ = qk_r()
        k.memset("pool", t_, t_[64:128, :], 0.0)
    p_r = k.rot(6, [128, TC], BF16)
    f_r = k.rot(12, [128, TC], F32)
    om_r = k.rot(6, [128, TC], F32)
    yst_r = k.rot(2, [128, S], BF16)
    vAv = vA.ap().rearrange("(b p) c -> p b c", p=128)
    pend = Pend(2)
    tile_i = 0
    for h in range(4):
        k.ld(g["EF"].all(), [Eh], Eh[:], g["EF"].ap()[h].rearrange("d p q -> p d q"))
        vt = vt_r()
        k.ld(vA.all(), [vt], vt[:], vAv[:, :, h * 128:(h + 1) * 128])
        QK = []
        for m in range(2):
            QT = qk_r()
            KT = qk_r()
            rq = m * 256 + h * 64
            k.ld(hT.all(), [QT], QT[0:64, :], hT.ap()[rq:rq + 64, :])
            k.ld(hT.all(), [KT], KT[0:64, :], hT.ap()[512 + rq:512 + rq + 64, :])
            QK.append((QT, KT))
        yst = yst_r()
        for j in range(NTC):
            oms = []
            for m in range(2):
                QT, KT = QK[m]
                O = k.pA()
                Dn = k.pA()
                nblk = 4 * j + 4
                om = om_r()
                oms.append(om)
                for i in range(nblk):
                    dl = 4 * j - i
                    near = dl < 13
                    c0, c1 = live_cols(dl)
                    ps = k.pS()
                    k.mm(ps, ps[:, c0:c1], KT, KT[:, i * 128:(i + 1) * 128], QT, QT[:, j * TC + c0:j * TC + c1], True, not near)
                    P = p_r()
                    if near:
                        k.mm(ps, ps[:, c0:c1], g["ident"], g["ident"][:], Eh, Eh[:, dl + 3, c0:c1], False, True)
                        k.act(P, P[:, c0:c1], ps, ps[:, c0:c1], AF.Exp, scale=0.125)
                    else:
                        k.act(P, P[:], ps, ps[:], AF.Exp, bias=g["b31"][:, h:h + 1], scale=0.125, extra=[g["b31"]])

                    def fin(P=P, i=i, O=O, Dn=Dn, nblk=nblk, om=om, m=m, j=j, oms=oms, yst=yst, vt=vt, c0=c0, c1=c1):
                        k.mm(O, O[:, c0:c1], vt, vt[:, i, :], P, P[:, c0:c1], i == 0, i == nblk - 1)
                        k.mm(Dn, Dn[:, c0:c1], g["ones"], g["ones"][:], P, P[:, c0:c1], i == 0, i == nblk - 1)
                        if i < nblk - 1:
                            return None

                        def evac1():
                            rd = f_r()
                            k.recip(rd, rd[:], Dn, Dn[:])
                            k.tt("dve", om, om[:], O, O[:], rd, rd[:], ALU.mult)
                            if m == 0:
                                return None
                            o = f_r()
                            k.stt("dve", o, o[:], oms[1], oms[1][:], neglam[:, 0:1], oms[0], oms[0][:], ALU.mult, ALU.add,
                                  extra=[neglam])
                            sq = f_r()
                            k.act(sq, sq[:], o, o[:], AF.Square)

                            def evac2():
                                ss = k.pM()
                                k.mm(ss, ss[:], g["ones32"], g["ones32"][:], sq, sq[:], True, True)
                                sd = f_r()
                                k.act(sd, sd[:], ss, ss[:], AF.Sqrt, bias=eps[:], scale=1.0 / 128.0, extra=[eps])
                                rs = f_r()
                                k.recip(rs, rs[:], sd, sd[:])
                                k.stt("dve", yst, yst[:, j * TC:(j + 1) * TC], o, o[:], gsc[:, 0:1], rs, rs[:], ALU.mult,
                                      ALU.mult, extra=[gsc])
                                return None
                            return evac2
                        return evac1
                    pend.push(fin)
        pend.flush()
        k.st([yst], [yT.res(h)], yT.ap()[h * 128:(h + 1) * 128, :], yst[:])
    k.end()


def attn_B(k, g, hT, vB, yT):
    nc = k.nc
    k.begin()
    accO = k.sb([128, S], F32)
    accD = k.sb([128, S], F32)
    Vp_r = k.rot(1, [128, 32, 256], BF16)
    qk_r = k.rot(4, [128, S], BF16)
    for _ in range(4):
        t_ = qk_r()
        k.memset("pool", t_, t_[64:128, :], 0.0)
    E_r = k.rot(4, [128, 4, 128], BF16)
    hrot = k.rot(2, [128, TC], BF16)
    p0_r = k.rot(4, [128, TC], BF16)
    p_r = k.rot(4, [128, TC], BF16)
    yst_r = k.rot(1, [128, S], BF16)
    rd_r = k.rot(2, [128, TC], F32)
    pend = Pend()
    for hp in range(2):
        for p, d in enumerate((1, 4, 16)):
            nb = 32 // d
            G = min(4, nb)
            W = G * 128
            Vp = Vp_r()
            k.ld(vB.all(), [Vp], Vp[:], vB.ap()[p].rearrange("b t c -> t b c"))
            for hh in range(2):
                h = 2 * hp + hh
                R0 = hh * 64
                QT = qk_r()
                KT = qk_r()
                rq = 1024 + p * 512 + h * 64
                k.ld(hT.all(), [QT], QT[0:64, :], hT.ap()[rq:rq + 64, :])
                k.ld(hT.all(), [KT], KT[0:64, :], hT.ap()[rq + 256:rq + 320, :])
                Es = E_r()
                Ep = E_r()
                for Et, dl in ((Es, 0), (Ep, 1)):
                    load_E(k, g, Et, Et[:, 0, :], g["vecB"][p], 4 + 4 * p + h, LB, 128 * dl, 1, 128, hrot)
                    for gi in range(1, 4):
                        k.cp("pool", Et, Et[:, gi, :], Et, Et[:, 0, :])
                for r in range(d):
                    for b0 in range(0, nb, G):
                        def blk(b):
                            t0 = r + d * 128 * b
                            return ssl(t0, 128, d)
                        Ss = k.pS()
                        Sp = k.pS()
                        for gi in range(G):
                            b = b0 + gi
                            k.mm(Ss, Ss[:, gi * 128:(gi + 1) * 128], KT, KT[:, blk(b)], QT, QT[:, blk(b)], True, True)
                        for gi in range(G):
                            b = b0 + gi
                            if b >= 1:
                                k.mm(Sp, Sp[:, gi * 128:(gi + 1) * 128], KT, KT[:, blk(b - 1)], QT, QT[:, blk(b)], True, True)
                        c0 = 128 if b0 == 0 else 0
                        Ps0 = p0_r()
                        Ps = p_r()
                        k.act(Ps0, Ps0[:, 0:W], Ss, Ss[:, 0:W], AF.Exp, scale=0.125)
                        k.tt("dve", Ps, Ps[:, 0:W], Ps0, Ps0[:, 0:W], Es, Es[:, 0:G, :], ALU.mult)
                        Pp = None
                        if W > c0:
                            Pp0 = p0_r()
                            Pp = p_r()
                            k.act(Pp0, Pp0[:, c0:W], Sp, Sp[:, c0:W], AF.Exp, scale=0.125)
                            k.tt("pool", Pp, Pp[:, c0:W], Pp0, Pp0[:, c0:W], Ep, Ep[:, c0 // 128:G, :], ALU.mult)

                        def fin(Ps=Ps, Pp=Pp, b0=b0, r=r, d=d, nb=nb, G=G, W=W, Vp=Vp, hp=hp, p=p, R0=R0):
                            O = k.pA()
                            Dn = k.pA()
                            for gi in range(G):
                                b = b0 + gi
                                cs = slice(gi * 128, (gi + 1) * 128)
                                vs = Vp[:, r * nb + b, hp * 128:(hp + 1) * 128]
                                k.mm(O, O[:, cs], Vp, vs, Ps, Ps[:, cs], True, b == 0)
                                if b >= 1:
                                    vp_ = Vp[:, r * nb + b - 1, hp * 128:(hp + 1) * 128]
                                    k.mm(O, O[:, cs], Vp, vp_, Pp, Pp[:, cs], False, True)
                            for gi in range(G):
                                b = b0 + gi
                                cs = slice(gi * 128, (gi + 1) * 128)
                                k.mm(Dn, Dn[:, cs], g["ones"], g["ones"][:], Ps, Ps[:, cs], True, b == 0)
                                if b >= 1:
                                    k.mm(Dn, Dn[:, cs], g["ones"], g["ones"][:], Pp, Pp[:, cs], False, True)
                            t0 = r + d * 128 * b0
                            tsl = ssl(t0, W, d)
                            rows = slice(R0, R0 + 64)
                            if p == 0:
                                k.cp("act", accO, accO[rows, tsl], O, O[rows, 0:W])
                                k.cp("dve", accD, accD[rows, tsl], Dn, Dn[rows, 0:W])
                            else:
                                k.tt("dve", accO, accO[rows, tsl], accO, accO[rows, tsl], O, O[rows, 0:W], ALU.add)
                                k.tt("dve", accD, accD[rows, tsl], accD, accD[rows, tsl], Dn, Dn[rows, 0:W], ALU.add)
                        pend.push(fin)
                pend.flush()
        yst = yst_r()
        for n in range(NTC):
            rd = rd_r()
            k.recip(rd, rd[:], accD, accD[:, n * TC:(n + 1) * TC])
            k.tt("dve", yst, yst[:, n * TC:(n + 1) * TC], accO, accO[:, n * TC:(n + 1) * TC], rd, rd[:], ALU.mult)
        k.st([yst], [yT.res(4 + hp)], yT.ap()[512 + hp * 128:512 + (hp + 1) * 128, :], yst[:])
    k.end()


def out_phase(k, g, yT, nk, w_out, xin, gam, bet, x1):
    k.begin()
    yr = k.rot(2, [128, nk, TC], BF16)
    yv = yT.ap().rearrange("(c p) t -> p c t", p=128)

    def yT_fn(n):
        y = yr()
        k.ld(yT.all(), [y], y[:], yv[:, :, n * TC:(n + 1) * TC])
        return y, (lambda kc, y=y: y[:, kc, :])
    proj_resid_ln(k, g, yT_fn, nk, w_out, xin, gam, bet, x1)
    k.end()


def even_layer(k, g, xin, xout, W, l):
    hT, vA, vB = even_proj(k, g, xin, W["w_in"])
    yT = k.dram([768, S], BF16, "ev_yT")
    attn_A(k, g, hT, vA, yT, W["lam"], W["subln"], l)
    attn_B(k, g, hT, vB, yT)
    x1 = k.dram([DM, S], F32, "x1")
    x1.bf = k.dram([DM, S], BF16)
    out_phase(k, g, yT, 6, W["w_out"], xin, W["ln_g0"], W["ln_b0"], x1)
    ffn_phase(k, g, x1, W["w_up"], W["conv_w"], W["conv_b"], W["w_down"], W["ln_g1"], W["ln_b1"], xout)


def rms_fm(k, g, z, nchunk, gam, out_t, f_r, eps, sq_r):
    nc = k.nc
    sq = sq_r()
    ss = k.pM()
    for m in range(nchunk):
        k.op("act", [z], [sq], lambda m=m: nc.scalar.activation(out=sq[:, m, :], in_=z[:, m, :], func=AF.Square))
    for m in range(nchunk):
        k.mm(ss, ss[:], g["ones"], g["ones"][:], sq, sq[:, m, :], m == 0, m == nchunk - 1)
    sd = f_r()
    k.act(sd, sd[:, 0, :], ss, ss[:], AF.Sqrt, bias=eps[:], scale=1.0 / (128.0 * nchunk), extra=[eps])
    k.recip(sd, sd[:, 1, :], sd, sd[:, 0, :])
    for m in range(nchunk):
        k.stt("dve", out_t, out_t[:, m, :], z, z[:, m, :], gam[:, m:m + 1], sd, sd[:, 1, :], ALU.mult, ALU.mult, extra=[gam])


def odd_proj(k, g, xin, W, cin):
    nc = k.nc
    w_in = W["w_in"]
    hT = k.dram([1024, S], BF16, "od_hT")
    vSW = k.dram([S, 256], BF16, "od_vSW")
    gateT = k.dram([24, S], F32, "od_gate")
    qdT = k.dram([8, 96, S], BF16, "od_qd")
    kdT = k.dram([8, 64, S], BF16, "od_kd")
    krr = k.dram([32, S], BF16, "od_krr")
    vD = k.dram([S, 512], BF16, "od_vD")
    k.begin()
    xb = load_xb(k, xin, k.sb([128, 8, S], BF16))
    wst = k.rot(2, [128, 8, 384], F32)
    wbf = k.rot(2, [128, 8, 128], BF16)
    stg_r = k.rot(1, [128, S], BF16)
    proj_fm_all(k, xb, w_in, [0, 128, 256, 384, 512, 640, 768, 1024], hT, wst, wbf, stg_r)
    wv2 = k.sb([128, 8, 256], BF16)
    load_w_bf16(k, w_in[:, 896:1024], 8, 128, wv2, wv2[:, :, 0:128], wst, "pool")
    load_w_bf16(k, w_in[:, 1152:1280], 8, 128, wv2, wv2[:, :, 128:256], wst, "pool")
    vst = k.rot(3, [128, 512], BF16)
    for b in range(32):
        ps = k.pS()
        for kc in range(8):
            k.mm(ps, ps[:, 0:256], xb.t(b // 4), xb[:, kc, b * 128:(b + 1) * 128], wv2, wv2[:, kc, :], kc == 0, kc == 7)
        s = vst()
        k.cp("act" if b % 2 == 0 else "dve", s, s[:, 0:256], ps, ps[:, 0:256])
        k.st([s], [vSW.res(b)], vSW.ap()[b * 128:(b + 1) * 128, :], s[:, 0:256])
    wg = wbf()
    load_w_bf16(k, w_in[:, 1280:1304], 8, 24, wg, wg[:, :, 0:24], wst, "pool")
    gst_r = k.rot(2, [24, TC], F32)
    for n in range(NTC):
        ps = k.pS()
        for kc in range(8):
            k.mm(ps, ps[0:24, :], wg, wg[:, kc, 0:24], xb.t(n), xb[:, kc, n * TC:(n + 1) * TC], kc == 0, kc == 7)
        gst = gst_r()
        k.act(gst, gst[:], ps, ps[0:24, :], AF.Sigmoid)
        k.st([gst], [gateT.res(n)], gateT.ap()[:, n * TC:(n + 1) * TC], gst[:])
    k.end()
    k.begin()
    xb = load_xb(k, xin, k.sb([128, 8, S], BF16))
    wst = k.rot(1, [128, 8, 384], F32)
    vst = k.rot(3, [128, 512], BF16)
    wcq = k.sb([128, 8, 384], BF16)
    load_w_bf16(k, w_in[:, 1304:1688], 8, 384, wcq, wcq[:], wst, "pool")
    wckv = k.sb([128, 8, 256], BF16)
    load_w_bf16(k, w_in[:, 1688:1944], 8, 256, wckv, wckv[:], wst, "pool")
    wkr = k.sb([128, 8, 96], BF16)
    wkrr = k.sb([128, 8, 96], BF16)
    k.memset("pool", wkr, wkr[:], 0.0)
    k.memset("pool", wkrr, wkrr[:], 0.0)
    s = wst()
    k.ld([], [s], s[:, :, 0:32], w_in[:, 1944:1976].rearrange("(kc p) m -> p kc m", p=128))
    k.cp("pool", wkr, wkr[:, :, 64:96], s, s[:, :, 0:32])
    k.cp("pool", wkrr, wkrr[:, :, 80:96], s, s[:, :, 0:16])
    k.ts("dve", wkrr, wkrr[:, :, 64:80], s, s[:, :, 16:32], -1.0, ALU.mult)
    wq = k.sb([128, 3, 768], BF16)
    wqr = k.sb([128, 3, 8, 96], BF16)
    k.memset("pool", wqr, wqr[:], 0.0)
    wst2 = k.rot(1, [128, 3, 768], F32)
    s = wst2()
    k.ld([], [s], s[:], W["w_uq"].rearrange("(kc p) m -> p kc m", p=128))
    k.cp("pool", wq, wq[:], s, s[:])
    for h in range(8):
        k.cp("pool", wqr, wqr[:, :, h, 80:96], s, s[:, :, h * 96 + 64:h * 96 + 80])
        k.ts("dve", wqr, wqr[:, :, h, 64:80], s, s[:, :, h * 96 + 80:h * 96 + 96], -1.0, ALU.mult)
    wk = k.sb([128, 2, 512], BF16)
    wv = k.sb([128, 2, 512], BF16)
    for wt, nm in ((wk, "w_uk"), (wv, "w_uv")):
        s = wst2()
        k.ld([], [s], s[:, 0:2, 0:512], W[nm].rearrange("(kc p) m -> p kc m", p=128))
        k.cp("pool", wt, wt[:], s, s[:, 0:2, 0:512])
    gq = k.sb([128, 3], F32)
    k.ld([], [gq], gq[:], W["q_norm"].rearrange("(m p) -> p m", p=128), slow=True)
    gkv = k.sb([128, 2], F32)
    k.ld([], [gkv], gkv[:], W["kv_norm"].rearrange("(m p) -> p m", p=128), slow=True)
    cs_r = k.rot(3, [96, 2, TC], F32)
    z_r = k.rot(2, [128, 3, TC], F32)
    f_r = k.rot(2, [128, 2, TC], F32)
    sq_r = k.rot(2, [128, 3, TC], BF16)
    cqn_r = k.rot(3, [128, 3, TC], BF16)
    cn_r = k.rot(3, [128, 2, TC], BF16)
    t_r = k.rot(4, [96, TC], F32)
    qst_r = k.rot(3, [96, TC], BF16)
    kst_r = k.rot(3, [64, TC], BF16)
    eps = g["eps"]
    k.set_pools(3, 3, 2)
    pend = Pend(1)

    def rope_from(psA, psB, dst, dst_ap, cst):
        t = t_r()
        u = t_r()
        k.tt("dve", t, t[64:96, :], psA, psA[64:96, :], cst, cst[64:96, 0, :], ALU.mult)
        k.tt("dve", u, u[64:96, :], psB, psB[64:96, :], cst, cst[64:96, 1, :], ALU.mult)
        k.tt("pool", dst, dst_ap, t, t[64:96, :], u, u[64:96, :], ALU.add)

    def heads_and_v(n, tsl, cqn, cn, cst):
        for h in range(8):
            psA = k.pA()
            psB = k.pA()
            for kc in range(3):
                k.mm(psA, psA[0:96, :], wq, wq[:, kc, h * 96:(h + 1) * 96], cqn, cqn[:, kc, :], kc == 0, kc == 2)
            for kc in range(3):
                k.mm(psB, psB[0:96, :], wqr, wqr[:, kc, h, :], cqn, cqn[:, kc, :], kc == 0, kc == 2)
            qs = qst_r()
            k.cp("act", qs, qs[0:64, :], psA, psA[0:64, :])
            rope_from(psA, psB, qs, qs[64:96, :], cst)
            k.st([qs], [qdT.res((h, n))], qdT.ap()[h][:, tsl], qs[:])
            ps = k.pS()
            for kc in range(2):
                k.mm(ps, ps[0:64, :], wk, wk[:, kc, h * 64:(h + 1) * 64], cn, cn[:, kc, :], kc == 0, kc == 1)
            ks = kst_r()
            k.cp("act", ks, ks[:], ps, ps[0:64, :])
            k.st([ks], [kdT.res((h, n))], kdT.ap()[h][:, tsl], ks[:])
        for bb in range(4):
            ps = k.pS()
            for kc in range(2):
                k.mm(ps, ps[:], cn, cn[:, kc, bb * 128:(bb + 1) * 128], wv, wv[:, kc, :], kc == 0, kc == 1)
            s = vst()
            k.cp("dve", s, s[:], ps, ps[:])
            b = n * 4 + bb
            k.st([s], [vD.res(b)], vD.ap()[b * 128:(b + 1) * 128, :], s[:])

    for n in range(NTC):
        tsl = slice(n * TC, (n + 1) * TC)
        cst = cs_r()
        k.ld([], [cst], cst[64:96, 0, :], cin["c_cos"].ap()[:, tsl])
        k.ld([], [cst], cst[64:96, 1, :], cin["c_sin"].ap()[:, tsl])
        z = z_r()
        for m in range(3):
            ps = k.pS()
            for kc in range(8):
                k.mm(ps, ps[:], wcq, wcq[:, kc, m * 128:(m + 1) * 128], xb.t(n), xb[:, kc, tsl], kc == 0, kc == 7)
            k.cp("act", z, z[:, m, :], ps, ps[:])
        cqn = cqn_r()
        rms_fm(k, g, z, 3, gq, cqn, f_r, eps, sq_r)
        z = z_r()
        for m in range(2):
            ps = k.pS()
            for kc in range(8):
                k.mm(ps, ps[:], wckv, wckv[:, kc, m * 128:(m + 1) * 128], xb.t(n), xb[:, kc, tsl], kc == 0, kc == 7)
            k.cp("act", z, z[:, m, :], ps, ps[:])
        cn = cn_r()
        rms_fm(k, g, z, 2, gkv, cn, f_r, eps, sq_r)
        psA = k.pA()
        psB = k.pA()
        for kc in range(8):
            k.mm(psA, psA[0:96, :], wkr, wkr[:, kc, :], xb.t(n), xb[:, kc, tsl], kc == 0, kc == 7)
        for kc in range(8):
            k.mm(psB, psB[0:96, :], wkrr, wkrr[:, kc, :], xb.t(n), xb[:, kc, tsl], kc == 0, kc == 7)
        qs = qst_r()
        rope_from(psA, psB, qs, qs[64:96, :], cst)
        k.st([qs], [krr.res(n)], krr.ap()[:, tsl], qs[64:96, :])
        pend.push(lambda n=n, tsl=tsl, cqn=cqn, cn=cn, cst=cst: heads_and_v(n, tsl, cqn, cn, cst))
    pend.flush()
    k.end()
    return hT, vSW, gateT, qdT, kdT, krr, vD


def attn_D(k, g, qdT, kdT, krr, vD, yT):
    k.begin()
    k.set_pools(4, 2, 2)
    Em = k.sb([128, 4, TC], BF16)
    k.ld(g["EM"].all(), [Em], Em[:], g["EM"].ap().rearrange("d p q -> p d q"))
    vt_r = k.rot(2, [128, 32, 128], BF16)
    qk_r = k.rot(4, [96, S], BF16)
    p_r = k.rot(7, [128, TC], BF16)
    f_r = k.rot(4, [128, TC], F32)
    od_r = k.rot(4, [128, TC], F32)
    odb_r = k.rot(4, [128, TC], BF16)
    yst_r = k.rot(2, [128, S], BF16)
    vDv = vD.ap().rearrange("(b p) c -> p b c", p=128)
    scale = float(96 ** -0.5)
    pend = Pend(3)
    for hp in range(4):
        yst = yst_r()
        for hh in range(2):
            h = 2 * hp + hh
            rows = slice(hh * 64, hh * 64 + 64)
            rsel = g["rsel0"] if hh == 0 else g["rsel1"]
            vt = vt_r()
            k.ld(vD.all(), [vt], vt[:, :, hh * 64:hh * 64 + 64], vDv[:, :, h * 64:(h + 1) * 64])
            k.memset("pool", vt, vt[:, :, (1 - hh) * 64:(1 - hh) * 64 + 64], 1.0)
            QT = qk_r()
            KT = qk_r()
            k.ld(qdT.all(), [QT], QT[:], qdT.ap()[h])
            k.ld(kdT.all(), [KT], KT[0:64, :], kdT.ap()[h])
            k.ld(krr.all(), [KT], KT[64:96, :], krr.ap())
            for j in range(NTC):
                O = k.pA()
                nblk = 4 * j + 4
                for i in range(nblk):
                    dl = 4 * j - i
                    near = dl <= 0
                    c0, c1 = live_cols(dl)
                    ps = k.pS()
                    k.mm(ps, ps[:, c0:c1], KT, KT[:, i * 128:(i + 1) * 128], QT, QT[:, j * TC + c0:j * TC + c1], True, not near)
                    if near:
                        k.mm(ps, ps[:, c0:c1], g["ident"], g["ident"][:], Em, Em[:, dl + 3, c0:c1], False, True)
                    P = p_r()
                    k.act(P, P[:, c0:c1], ps, ps[:, c0:c1], AF.Exp, scale=scale)

                    def fin(P=P, i=i, O=O, nblk=nblk, j=j, yst=yst, vt=vt, rows=rows, rsel=rsel, c0=c0, c1=c1):
                        k.mm(O, O[:, c0:c1], vt, vt[:, i, :], P, P[:, c0:c1], i == 0, i == nblk - 1)
                        if i < nblk - 1:
                            return None

                        def evac1():
                            od = od_r()
                            k.cp("act", od, od[:], O, O[:])

                            def evac2():
                                dn = k.pM()
                                k.mm(dn, dn[:], rsel, rsel[:], od, od[:], True, True)
                                rd = f_r()
                                k.recip(rd, rd[rows, :], dn, dn[rows, :])
                                k.tt("dve", yst, yst[rows, j * TC:(j + 1) * TC], od, od[rows, :], rd, rd[rows, :], ALU.mult)
                                return None
                            return evac2
                        return evac1
                    pend.push(fin)
            pend.flush()
        k.st([yst], [yT.res(4 + hp)], yT.ap()[512 + hp * 128:512 + (hp + 1) * 128, :], yst[:])
    k.end()


def nsa_compress(k, g, hT, W):
    nc = k.nc
    kcT = k.dram([2, 64, 256], BF16, "od_kc")
    vcd = k.dram([2, 256, 64], BF16, "od_vc")
    k.begin()
    w1s_r = k.rot(1, [64, 32, 256], F32)
    w1_r = k.rot(1, [64, 32, 256], BF16)
    w2_r = k.rot(1, [128, 2, 64], BF16)
    w2s_r = k.rot(1, [128, 2, 64], F32)
    pes = k.sb([64, 32], F32)
    peT = k.sb([64, 32], BF16)
    cb = k.sb([128, 2], F32)
    tt_r = k.rot(2, [64, S], BF16)
    hid_r = k.rot(2, [128, 2, 256], BF16)
    st_r = k.rot(2, [128, 256], BF16)
    for kv in range(2):
        w1s = w1s_r()
        k.ld([], [w1s], w1s[:], W["cmp_w1"][kv].rearrange("(pos d) h -> d pos h", d=64))
        w1 = w1_r()
        k.cp("pool", w1, w1[:], w1s, w1s[:])
        w2s = w2s_r()
        k.ld([], [w2s], w2s[:], W["cmp_w2"][kv].rearrange("(hh p) d -> p hh d", p=128))
        w2 = w2_r()
        k.cp("pool", w2, w2[:], w2s, w2s[:])
        k.ld([], [pes], pes[:], W["cmp_pe"][kv].rearrange("pos d -> d pos"), slow=True)
        k.cp("dve", peT, peT[:], pes, pes[:])
        for hh in range(2):
            ps = k.pM()
            for pos in range(32):
                k.mm(ps, ps[:, 0:1], w1, w1[:, pos, hh * 128:(hh + 1) * 128], peT, peT[:, pos:pos + 1], pos == 0, pos == 31)
            k.cp("dve", cb, cb[:, hh:hh + 1], ps, ps[:, 0:1])
        for gI in range(2):
            T = tt_r()
            r0 = 512 + kv * 128 + gI * 64
            k.ld(hT.all(), [T], T[:], hT.ap()[r0:r0 + 64, :])
            hid = hid_r()
            k.memset("pool", hid, hid[:], 0.0)
            for hh in range(2):
                ps = k.pS()
                for pos in range(32):
                    k.mm(ps, ps[:, 0:255], w1, w1[:, pos, hh * 128:(hh + 1) * 128], T, T[:, ssl(pos, 255, 16)], pos == 0, pos == 31)
                k.act(hid, hid[:, hh, 0:255], ps, ps[:, 0:255], AF.Gelu_apprx_tanh, bias=cb[:, hh:hh + 1], extra=[cb])
            s = st_r()
            if kv == 0:
                ps = k.pM()
                for hh in range(2):
                    k.mm(ps, ps[0:64, 0:256], w2, w2[:, hh, :], hid, hid[:, hh, :], hh == 0, hh == 1)
                k.cp("dve", s, s[0:64, :], ps, ps[0:64, 0:256])
                k.memset("pool", s, s[0:64, 255:256], 0.0)
                k.st([s], [kcT.res(gI)], kcT.ap()[gI], s[0:64, :])
            else:
                for cbk in range(2):
                    ps = k.pM()
                    for hh in range(2):
                        k.mm(ps, ps[:, 0:64], hid, hid[:, hh, cbk * 128:(cbk + 1) * 128], w2, w2[:, hh, :], hh == 0, hh == 1)
                    s = st_r()
                    k.cp("dve", s, s[:, 0:64], ps, ps[:, 0:64])
                    k.st([s], [vcd.res((gI, cbk))], vcd.ap()[gI][cbk * 128:(cbk + 1) * 128, :], s[:, 0:64])
    k.end()
    return kcT, vcd


def nsa_cmp_select(k, g, hT, kcT, vcd, cin):
    nc = k.nc
    ocmp = k.dram([8, 128, S], F32, "od_ocmp")
    negm = k.dram([2, 64, S], BF16, "od_negm")
    k.begin()
    Msel = k.sb([128, 2, 64], F32)
    k.ld([], [Msel], Msel[:], cin["c_Msel"].ap().rearrange("(cb p) j -> p cb j", p=128))
    Fm = k.sb([128, 32, 64], F32)
    k.ld([], [Fm], Fm[:], cin["c_Fm"].ap().rearrange("(t p) j -> p t j", p=128))
    kc_r = k.rot(1, [128, 256], BF16)
    vc_r = k.rot(1, [128, 2, 128], BF16)
    q_r = k.rot(4, [128, S], BF16)
    for _ in range(4):
        t_ = q_r()
        k.memset("pool", t_, t_[64:128, :], 0.0)
    t_ = kc_r()
    k.memset("pool", t_, t_[64:128, :], 0.0)
    hrot = k.rot(2, [128, TC], BF16)
    k.set_pools(3, 3, 2)
    E_r = k.rot(4, [128, TC], BF16)
    p0_r = k.rot(4, [128, TC], BF16)
    p_r = k.rot(8, [128, TC], BF16)
    f_r = k.rot(8, [128, TC], F32)
    oc_r = k.rot(2, [128, TC], F32)
    sc_r = k.rot(4, [128, 64], F32)
    m8_r = k.rot(2, [128, 16], F32)
    nm_r = k.rot(2, [128, 64], BF16)
    nmT_r = k.rot(2, [64, TC], BF16)
    pend = Pend(1)
    pgs_r = k.rot(6, [128, TC], F32)
    for gI in range(2):
        kc = kc_r()
        k.ld(kcT.all(), [kc], kc[0:64, :], kcT.ap()[gI])
        vc = vc_r()
        vv = vcd.ap()[gI].rearrange("(cb p) d -> p cb d", p=128)
        k.ld(vcd.all(), [vc], vc[:, :, 0:64], vv)
        k.ld(vcd.all(), [vc], vc[:, :, 64:128], vv)
        QTs = []
        for hg in range(4):
            QT = q_r()
            head = 4 * gI + hg
            k.ld(hT.all(), [QT], QT[0:64, :], hT.ap()[head * 64:(head + 1) * 64, :])
            QTs.append(QT)
        for j in range(NTC):
            ncb = 1 if j < 4 else 2
            tsl = slice(j * TC, (j + 1) * TC)
            pg = [pgs_r() for _ in range(ncb)]
            for hg in range(4):
                head = 4 * gI + hg
                Ps = []
                for cbk in range(ncb):
                    E = E_r()
                    k.ld(g["EC"].all(), [E], E[:], g["EC"].ap()[head, 2 * j + cbk])
                    ps = k.pS()
                    k.mm(ps, ps[:], kc, kc[:, cbk * 128:(cbk + 1) * 128], QTs[hg], QTs[hg][:, tsl], True, True)
                    P0 = p0_r()
                    k.act(P0, P0[:], ps, ps[:], AF.Exp, scale=0.125)
                    P = p_r()
                    k.tt("dve", P, P[:], P0, P0[:], E, E[:], ALU.mult)
                    Ps.append(P)

                def fin(Ps=Ps, ncb=ncb, hg=hg, head=head, pg=pg, tsl=tsl, j=j):
                    O = k.pA()
                    Dn = k.pA()
                    for cbk in range(ncb):
                        k.mm(O, O[:], vc, vc[:, cbk, :], Ps[cbk], Ps[cbk][:], cbk == 0, cbk == ncb - 1)
                        k.mm(Dn, Dn[:], g["ones"], g["ones"][:], Ps[cbk], Ps[cbk][:], cbk == 0, cbk == ncb - 1)
                    dm = f_r()
                    k.ts("dve", dm, dm[:], Dn, Dn[:], 1e-30, ALU.max)
                    ln_ = f_r()
                    k.act(ln_, ln_[:], dm, dm[:], AF.Ln)
                    rd = f_r()
                    k.act(rd, rd[:], ln_, ln_[:], AF.Exp, scale=-1.0)
                    oc = oc_r()
                    k.tt("dve", oc, oc[:], O, O[:], rd, rd[:], ALU.mult)
                    k.st([oc], [ocmp.res((head, j))], ocmp.ap()[head][:, tsl], oc[:])
                    for cbk in range(ncb):
                        if hg == 0:
                            k.tt("pool", pg[cbk], pg[cbk][:], Ps[cbk], Ps[cbk][:], rd, rd[:], ALU.mult)
                        else:
                            tmp = f_r()
                            k.tt("pool", tmp, tmp[:], Ps[cbk], Ps[cbk][:], rd, rd[:], ALU.mult)
                            k.tt("dve", pg[cbk], pg[cbk][:], pg[cbk], pg[cbk][:], tmp, tmp[:], ALU.add)
                    if hg < 3:
                        return None

                    def topk():
                        nmT = nmT_r()
                        for t in range(4):
                            qt = 4 * j + t
                            ps = k.pM()
                            for cbk in range(ncb):
                                k.mm(ps, ps[:, 0:64], pg[cbk], pg[cbk][:, t * 128:(t + 1) * 128], Msel, Msel[:, cbk, :],
                                     cbk == 0, cbk == ncb - 1)
                            sc = sc_r()
                            k.tt("dve", sc, sc[:], ps, ps[:, 0:64], Fm, Fm[:, qt, :], ALU.add)
                            m8 = m8_r()
                            k.op("dve", [sc], [m8], lambda sc=sc, m8=m8: nc.vector.max(out=m8[:, 0:8], in_=sc[:]))
                            sc2 = sc_r()
                            k.op("dve", [sc, m8], [sc2], lambda sc=sc, m8=m8, sc2=sc2: nc.vector.match_replace(
                                out=sc2[:], in_to_replace=m8[:, 0:8], in_values=sc[:], imm_value=-1e9))
                            k.op("dve", [sc2], [m8], lambda sc2=sc2, m8=m8: nc.vector.max(out=m8[:, 8:16], in_=sc2[:]))
                            nm = nm_r()
                            k.ts("dve", nm, nm[:], sc, sc[:], m8[:, 15:16], ALU.is_lt, -30000.0, ALU.mult, extra=[m8])
                            ps2 = k.pM()
                            k.mm(ps2, ps2[0:64, 0:128], nm, nm[:], g["ident"], g["ident"][:], True, True)
                            k.cp("act", nmT, nmT[:, t * 128:(t + 1) * 128], ps2, ps2[0:64, 0:128])
                        k.st([nmT], [negm.res((gI, j))], negm.ap()[gI][:, tsl], nmT[:])
                        return None
                    return topk
                pend.push(fin)
        pend.flush()
    k.end()
    return ocmp, negm


def nsa_slc_win(k, g, hT, vSW, gateT, ocmp, negm, yT, cin):
    nc = k.nc
    k.begin()
    k.set_pools(4, 2, 2)
    Sel = k.sb([128, 24, 128], BF16)
    gt = k.sb([128, S], BF16)
    k.memset("pool", Sel, Sel[:], 0.0)
    k.memset("pool", gt, gt[:], 0.0)
    gst32 = k.sb([24, S], F32)
    k.ld([], [gst32], gst32[:, 0:3072], cin["c_Sel"].ap().rearrange("r a m -> r (a m)"))
    k.cp("dve", Sel, Sel[0:24, :, :], gst32, gst32[:, 0:3072].rearrange("r (a m) -> r a m", m=128))
    k.ld(gateT.all(), [gst32], gst32[:], gateT.ap())
    k.cp("dve", gt, gt[0:24, :], gst32, gst32[:])
    Es = k.sb([128, 16, TC], BF16)
    Ew = k.sb([128, 8, TC], BF16)
    KE = k.sb([128, 32, 128], BF16)
    k.ld([g["exbf"].res()], [KE], KE[64:128, :, :], g["exbf"].ap())
    KW = k.sb([128, S], BF16)
    k.memset("pool", KW, KW[64:128, :], 0.0)
    vts = [k.sb([128, 32, 128], BF16) for _ in range(4)]
    QN_r = k.rot(2, [128, S], BF16)
    p_r = k.rot(7, [128, TC], BF16)
    f_r = k.rot(8, [128, TC], F32)
    gs_r = k.rot(6, [128, TC], F32)
    od_r = k.rot(3, [128, TC], F32)
    odb_r = k.rot(3, [128, TC], BF16)
    oc_r = k.rot(2, [128, TC], F32)
    yst_r = k.rot(2, [128, S], BF16)
    vv = vSW.ap().rearrange("(b p) c -> p b c", p=128)
    pend = Pend(3)
    for gI in range(2):
        k.ld(hT.all(), [KE], KE[0:64, :, :], hT.ap()[768 + gI * 64:768 + gI * 64 + 64, :].rearrange("d (b t) -> d b t", t=128))
        k.ld(hT.all(), [KW], KW[0:64, :], hT.ap()[896 + gI * 64:896 + gI * 64 + 64, :])
        for bi, c0 in ((0, gI * 64), (1, 128 + gI * 64)):
            for par in range(2):
                vt = vts[bi * 2 + par]
                k.ld(vSW.all(), [vt], vt[:, :, par * 64:par * 64 + 64], vv[:, :, c0:c0 + 64])
                k.memset("pool", vt, vt[:, :, (1 - par) * 64:(1 - par) * 64 + 64], 1.0)
        for hg in range(4):
            head = 4 * gI + hg
            par = head % 2
            rows = slice(par * 64, par * 64 + 64)
            rsel = g["rsel0"] if par == 0 else g["rsel1"]
            if par == 0:
                yst = yst_r()
            QN = QN_r()
            k.ld(hT.all(), [QN], QN[0:64, :], hT.ap()[head * 64:(head + 1) * 64, :])
            k.ld(negm.all(), [QN], QN[64:128, :], negm.ap()[gI])
            k.ld(g["EF"].all(), [Es], Es[:], g["EF"].ap()[head].rearrange("d p q -> p d q"))
            k.ld(g["EW"].all(), [Ew], Ew[:], g["EW"].ap()[head].rearrange("d p q -> p d q"))
            for j in range(NTC):
                tsl = slice(j * TC, (j + 1) * TC)
                gsb = []
                for b3 in range(3):
                    r = (b3 * 2 + gI) * 4 + hg
                    gp = k.pM()
                    k.mm(gp, gp[:], Sel, Sel[:, r, :], gt, gt[:, tsl], True, True)
                    gs_ = gs_r()
                    k.cp("act", gs_, gs_[rows, :], gp, gp[rows, :])
                    gsb.append(gs_)
                res_sw = {}
                for br in (1, 2):
                    O = k.pA()
                    vt = vts[(br - 1) * 2 + par]
                    ilist = list(range(4 * j + 4)) if br == 1 else list(range(max(0, 4 * j - 4), 4 * j + 4))
                    if br == 2 and len(ilist) == 8:
                        ilist = [ilist[1], ilist[0]] + ilist[2:]
                    for ii, i in enumerate(ilist):
                        dl = 4 * j - i
                        near = (br == 2) or dl < 13
                        c0, c1 = live_cols(dl, br == 2)
                        qsl = slice(j * TC + c0, j * TC + c1)
                        ps = k.pS()
                        if br == 1:
                            k.mm(ps, ps[:, c0:c1], KE, KE[:, i, :], QN, QN[:, qsl], True, not near)
                        else:
                            k.mm(ps, ps[:, c0:c1], KW, KW[:, i * 128:(i + 1) * 128], QN, QN[:, qsl], True, not near)
                        P = p_r()
                        if near:
                            Et = Es if br == 1 else Ew
                            k.mm(ps, ps[:, c0:c1], g["ident"], g["ident"][:], Et, Et[:, dl + 3, c0:c1], False, True)
                            k.act(P, P[:, c0:c1], ps, ps[:, c0:c1], AF.Exp, scale=0.125)
                        else:
                            k.act(P, P[:], ps, ps[:], AF.Exp, bias=g["b31"][:, head:head + 1], scale=0.125, extra=[g["b31"]])
                        first = ii == 0
                        last = ii == len(ilist) - 1

                        def fin(P=P, i=i, O=O, first=first, last=last, vt=vt, br=br, res_sw=res_sw, j=j,
                                head=head, rows=rows, yst=yst, tsl=tsl, rsel=rsel, gsb=gsb, c0=c0, c1=c1):
                            k.mm(O, O[:, c0:c1], vt, vt[:, i, :], P, P[:, c0:c1], first, last)
                            if not last:
                                return None

                            def evac1():
                                od = od_r()
                                k.cp("act", od, od[:], O, O[:])

                                def evac2():
                                    dn = k.pM()
                                    k.mm(dn, dn[:], rsel, rsel[:], od, od[:], True, True)
                                    rd = f_r()
                                    k.recip(rd, rd[rows, :], dn, dn[rows, :])
                                    fac = f_r()
                                    k.tt("pool", fac, fac[rows, :], rd, rd[rows, :], gsb[br], gsb[br][rows, :], ALU.mult)
                                    on = f_r()
                                    k.tt("dve", on, on[rows, :], od, od[rows, :], fac, fac[rows, :], ALU.mult)
                                    res_sw[br] = on
                                    if br == 1:
                                        return None
                                    oc = oc_r()
                                    k.ld(ocmp.all(), [oc], oc[rows, :], ocmp.ap()[head][rows, tsl])
                                    tcm = f_r()
                                    k.tt("pool", tcm, tcm[rows, :], oc, oc[rows, :], gsb[0], gsb[0][rows, :], ALU.mult)
                                    a2 = f_r()
                                    k.tt("pool", a2, a2[rows, :], tcm, tcm[rows, :], res_sw[1], res_sw[1][rows, :], ALU.add)
                                    k.tt("dve", yst, yst[rows, tsl], a2, a2[rows, :], on, on[rows, :], ALU.add)
                                    return None
                                return evac2
                            return evac1
                        pend.push(fin)
            pend.flush()
            if par == 1:
                k.st([yst], [yT.res(head // 2)], yT.ap()[(head // 2) * 128:(head // 2 + 1) * 128, :], yst[:])
    k.end()


def odd_layer(k, g, xin, xout, W, cin):
    hT, vSW, gateT, qdT, kdT, krr, vD = odd_proj(k, g, xin, W, cin)
    yT = k.dram([1024, S], BF16, "od_yT")
    attn_D(k, g, qdT, kdT, krr, vD, yT)
    kcT, vcd = nsa_compress(k, g, hT, W)
    ocmp, negm = nsa_cmp_select(k, g, hT, kcT, vcd, cin)
    nsa_slc_win(k, g, hT, vSW, gateT, ocmp, negm, yT, cin)
    x1 = k.dram([DM, S], F32, "x1")
    x1.bf = k.dram([DM, S], BF16)
    out_phase(k, g, yT, 8, W["w_out"], xin, W["ln_g0"], W["ln_b0"], x1)
    ffn_phase(k, g, x1, W["w_up"], W["conv_w"], W["conv_b"], W["w_down"], W["ln_g1"], W["ln_b1"], xout)


EV_W = {"w_in": [1024, 3840], "w_out": [768, 1024], "lam": [4, 64], "subln": [128]}
OD_W = {"w_in": [1024, 1976], "w_out": [1024, 1024], "cmp_pe": [2, 32, 64], "cmp_w1": [2, 2048, 256], "cmp_w2": [2, 256, 64],
        "q_norm": [384], "kv_norm": [256], "w_uq": [384, 768], "w_uk": [256, 512], "w_uv": [256, 512]}
FF_W = {"w_up": [1024, 5632], "conv_w": [3, 2816], "conv_b": [2816], "w_down": [2816, 1024], "ln_g0": [1024], "ln_b0": [1024],
        "ln_g1": [1024], "ln_b1": [1024]}
FUSED = True


def layer_shapes(l):
    sh = dict(EV_W if l % 2 == 0 else OD_W)
    sh.update(FF_W)
    return sh


def build_program(layers):
    nc = bass.Bass("TRN2", target_bir_lowering=False)
    k = K(nc)
    cin = {n: nc.dram_tensor(n, s, F32, kind="ExternalInput") for n, s in CONST_SHAPES.items()}
    rel_bias = nc.dram_tensor("rel_bias", [32, 16], F32, kind="ExternalInput")
    xin = DT(nc.dram_tensor("xin", [DM, S], F32, kind="ExternalInput"))
    xo = DT(nc.dram_tensor("xo", [DM, S], F32, kind="ExternalOutput"))
    Ws = {}
    for l in layers:
        Ws[l] = {n: nc.dram_tensor("L%d_%s" % (l, n), s, F32, kind="ExternalInput").ap() for n, s in layer_shapes(l).items()}
    g = setup_globals(k, rel_bias, cin)
    cur = xin
    for li, l in enumerate(layers):
        nxt = xo if li == len(layers) - 1 else k.dram([DM, S], F32)
        if nxt is not xo:
            nxt.bf = k.dram([DM, S], BF16)
        if l % 2 == 0:
            even_layer(k, g, cur, nxt, Ws[l], l)
        else:
            odd_layer(k, g, cur, nxt, Ws[l], cin)
        cur = nxt
    k.c.barrier()
    return nc


def layer_inputs(inp, l):
    i = l // 2
    m = {}
    if l % 2 == 0:
        m.update({"w_in": inp["ev_w_in"][i], "w_out": inp["ev_w_out"][i], "lam": inp["ev_lambda"][i], "subln": inp["ev_subln"][i]})
    else:
        m.update({"w_in": inp["od_w_in"][i], "w_out": inp["od_w_out"][i], "cmp_pe": inp["od_cmp_pe"][i], "cmp_w1": inp["od_cmp_w1"][i],
                  "cmp_w2": inp["od_cmp_w2"][i], "q_norm": inp["od_q_norm"][i], "kv_norm": inp["od_kv_norm"][i],
                  "w_uq": inp["od_w_uq"][i], "w_uk": inp["od_w_uk"][i], "w_uv": inp["od_w_uv"][i]})
    m.update({"w_up": inp["ffn_w_up"][l], "conv_w": inp["ffn_conv_w"][l], "conv_b": inp["ffn_conv_b"][l], "w_down": inp["ffn_w_down"][l],
              "ln_g0": inp["ln_g"][l, 0], "ln_b0": inp["ln_b"][l, 0], "ln_g1": inp["ln_g"][l, 1], "ln_b1": inp["ln_b"][l, 1]})
    return {"L%d_%s" % (l, n): np.ascontiguousarray(np.asarray(v, dtype=np.float32)) for n, v in m.items()}


def kernel(**inputs):
    inp = {n: np.asarray(v) for n, v in inputs.items()}
    x = inp["x"].astype(np.float32, copy=False)
    nb = x.shape[0]
    consts = host_consts()
    xT = [np.ascontiguousarray(x[b].T) for b in range(nb)]
    groups = [[0, 1, 2, 3]] if FUSED else [[0], [1], [2], [3]]
    for layers in groups:
        nc = build_program(layers)
        shared = dict(consts)
        shared["rel_bias"] = np.ascontiguousarray(inp["rel_bias"].astype(np.float32))
        for l in layers:
            shared.update(layer_inputs(inp, l))
        in_maps = []
        for b in range(nb):
            m = dict(shared)
            m["xin"] = xT[b]
            in_maps.append(m)
        res = run_bass_kernel_spmd(nc, in_maps, core_ids=list(range(nb)))
        xT = [np.asarray(res.results[b]["xo"]) for b in range(nb)]
    out = np.stack([xT[b].T for b in range(nb)], axis=0).astype(np.float32)
    return np.ascontiguousarray(out)
```
